# Optimizing a Trainium2 kernel written in Bass

```python
import math
import jax, jax.numpy as jnp
from jax import lax
import numpy as np

D_MODEL = 4096
BATCH = 2
SEQ = 4096
DEPTH = 2

HD = 128
GRID_W = 64
Q_BLOCK = 128
A_HEADS = 8
NA_ROWS = 8
NA_COLS = 16
B_Q_HEADS = 8
B_KV_HEADS = 2
ROPE_THETA = 10000.0
C_HEADS = 4
D_PATTERNS = ((128, 1), (512, 4), (2048, 16))
D_GROUPS = 3
D_HEADS_PER_GROUP = 4
N_BRANCHES = 4
RMS_EPS = 1e-6
NEG_INF = -1e30

A_W = A_HEADS * HD
B_QW = B_Q_HEADS * HD
B_KVW = B_KV_HEADS * HD
C_QKW = C_HEADS * 2 * HD
C_VW = C_HEADS * 2 * HD
D_QKVW = D_GROUPS * D_HEADS_PER_GROUP * HD
D_OW = D_HEADS_PER_GROUP * HD
SPLIT_WIDTHS = (A_W, A_W, A_W, A_W,
                B_QW, B_KVW, B_KVW, B_QW,
                C_QKW, C_QKW, C_VW, C_VW,
                D_QKVW, D_QKVW, D_QKVW, D_OW) + (D_MODEL,) * N_BRANCHES
N_IN = sum(SPLIT_WIDTHS)

kernel_name = "hybrid_parallel_gated_encoder"


def rms_norm(x, g):
    xf = x.astype(jnp.float32)
    y = xf * lax.rsqrt(jnp.mean(xf * xf, axis=-1, keepdims=True) + RMS_EPS)
    return (y * g.astype(jnp.float32)).astype(x.dtype)


def alibi_slopes(n):
    return jnp.asarray(2.0 ** (-8.0 * np.arange(1, n + 1) / n), dtype=jnp.float32)


def axial_rope(S):
    t = jnp.arange(S)
    row = (t // GRID_W).astype(jnp.float32)
    col = (t % GRID_W).astype(jnp.float32)
    n_pairs = HD // 4
    inv_freq = ROPE_THETA ** (-jnp.arange(n_pairs, dtype=jnp.float32) / n_pairs)
    ang = jnp.concatenate([row[:, None] * inv_freq, col[:, None] * inv_freq], axis=-1)
    return jnp.cos(ang), jnp.sin(ang)


def apply_rope(x, cos, sin):
    xf = x.astype(jnp.float32)
    x1, x2 = xf[..., 0::2], xf[..., 1::2]
    c, s = cos[None, :, None, :], sin[None, :, None, :]
    out = jnp.stack([x1 * c - x2 * s, x1 * s + x2 * c], axis=-1)
    return out.reshape(x.shape).astype(x.dtype)


def neighbourhood_attention(q, k, v, rel_bias, rows):
    B, S, H, _ = q.shape
    win_r = min(NA_ROWS, rows)
    qg = q.reshape(B, rows, GRID_W, H, HD)
    kg = k.reshape(B, rows, GRID_W, H, HD)
    vg = v.reshape(B, rows, GRID_W, H, HD)
    cols = jnp.arange(GRID_W)
    c0 = jnp.clip(cols - NA_COLS // 2, 0, GRID_W - NA_COLS)
    col_idx = c0[:, None] + jnp.arange(NA_COLS)[None, :]
    col_off = col_idx - cols[:, None] + (NA_COLS - 1)

    def row_block(r):
        r0 = jnp.clip(r - NA_ROWS // 2, 0, rows - win_r)
        q_r = lax.dynamic_index_in_dim(qg, r, axis=1, keepdims=False)
        k_r = lax.dynamic_slice_in_dim(kg, r0, win_r, axis=1)
        v_r = lax.dynamic_slice_in_dim(vg, r0, win_r, axis=1)
        k_nb = k_r[:, :, col_idx]
        v_nb = v_r[:, :, col_idx]
        s = jnp.einsum('bchd,bicjhd->bhcij', q_r, k_nb, preferred_element_type=jnp.float32)
        row_off = r0 + jnp.arange(win_r) - r + (NA_ROWS - 1)
        bias = rel_bias.astype(jnp.float32)[:, row_off][:, :, col_off]
        s = s + bias.transpose(0, 2, 1, 3)[None]
        p = jax.nn.softmax(s.reshape(B, H, GRID_W, win_r * NA_COLS), axis=-1)
        p = p.reshape(B, H, GRID_W, win_r, NA_COLS).astype(v.dtype)
        return jnp.einsum('bhcij,bicjhd->bchd', p, v_nb)

    o = lax.map(row_block, jnp.arange(rows))
    return o.transpose(1, 0, 2, 3, 4).reshape(B, S, H * HD)


def gqa_block_attention(q, k, v):
    B, S, Hq, _ = q.shape
    Hkv = k.shape[2]
    G = Hq // Hkv
    nblk = S // Q_BLOCK
    qb = q.reshape(B, nblk, Q_BLOCK, Hkv, G, HD).swapaxes(0, 1)

    def block(qi):
        s = jnp.einsum('bqhgd,bkhd->bhgqk', qi, k, preferred_element_type=jnp.float32)
        p = jax.nn.softmax(s, axis=-1).astype(v.dtype)
        return jnp.einsum('bhgqk,bkhd->bqhgd', p, v)

    o = lax.map(block, qb)
    return o.swapaxes(0, 1).reshape(B, S, Hq * HD)


def diff_block_attention(q, k, v, lam, slopes):
    B, S, H = q.shape[:3]
    nblk = S // Q_BLOCK
    kpos = jnp.arange(S, dtype=jnp.float32)
    qb = q.reshape(B, nblk, Q_BLOCK, H, 2, HD).swapaxes(0, 1)

    def block(args):
        qi, i = args
        qpos = (i * Q_BLOCK + jnp.arange(Q_BLOCK)).astype(jnp.float32)
        dist = jnp.abs(qpos[:, None] - kpos[None, :])
        s = jnp.einsum('bqhmd,bkhmd->bhmqk', qi, k, preferred_element_type=jnp.float32)
        s = s - slopes[None, :, None, None, None] * dist
        p = jax.nn.softmax(s, axis=-1)
        a = (p[:, :, 0] - lam * p[:, :, 1]).astype(v.dtype)
        return jnp.einsum('bhqk,bkhe->bqhe', a, v)

    o = lax.map(block, (qb, jnp.arange(nblk)))
    return o.swapaxes(0, 1).reshape(B, S, H, 2 * HD)


def dilated_attention(q, k, v, slopes):
    B, S = q.shape[:2]
    nblk = S // Q_BLOCK
    outs, lses = [], []
    for g, (window, dilation) in enumerate(D_PATTERNS):
        n_side = window // (2 * dilation)
        offs = dilation * jnp.arange(-n_side, n_side + 1)
        penalty = slopes[g][:, None, None] * jnp.abs(offs).astype(jnp.float32)
        kg, vg = k[:, :, g], v[:, :, g]
        qb = q[:, :, g].reshape(B, nblk, Q_BLOCK, D_HEADS_PER_GROUP, HD).swapaxes(0, 1)

        def block(args, kg=kg, vg=vg, offs=offs, penalty=penalty):
            qi, i = args
            t = i * Q_BLOCK + jnp.arange(Q_BLOCK)
            idx = t[:, None] + offs[None, :]
            valid = (idx >= 0) & (idx < S)
            idx = jnp.clip(idx, 0, S - 1)
            kn, vn = kg[:, idx], vg[:, idx]
            s = jnp.einsum('bqhd,bqkhd->bhqk', qi, kn, preferred_element_type=jnp.float32) - penalty[None]
            s = jnp.where(valid[None, None], s, NEG_INF)
            lse = jax.nn.logsumexp(s, axis=-1)
            p = jnp.exp(s - lse[..., None]).astype(vn.dtype)
            return jnp.einsum('bhqk,bqkhd->bqhd', p, vn), lse

        o, lse = lax.map(block, (qb, jnp.arange(nblk)))
        outs.append(o.swapaxes(0, 1).reshape(B, S, D_HEADS_PER_GROUP, HD))
        lses.append(lse.transpose(1, 0, 3, 2).reshape(B, S, D_HEADS_PER_GROUP))
    o = jnp.stack(outs, axis=0).astype(jnp.float32)
    w = jax.nn.softmax(jnp.stack(lses, axis=0), axis=0)
    out = jnp.sum(w[..., None] * o, axis=0)
    return out.astype(q.dtype).reshape(B, S, D_OW)


def hybrid_layer(x, layer_idx, norm_g, w_in, qk_gain, na_rel_bias, diff_lambda, diff_subln_g,
                 w_branch_a, w_branch_b, w_branch_c, w_branch_d, w_out):
    B, S, _ = x.shape
    rows = S // GRID_W
    scale = HD ** -0.5
    xn = rms_norm(x, norm_g)
    proj = jnp.einsum('bsd,de->bse', xn, w_in)
    split_points = [int(p) for p in np.cumsum(SPLIT_WIDTHS)[:-1]]
    (a_q, a_k, a_v, a_z, b_q, b_k, b_v, b_z, c_q, c_k, c_v, c_z,
     d_q, d_k, d_v, d_z, g_a, g_b, g_c, g_d) = jnp.split(proj, split_points, axis=-1)

    qa = rms_norm(a_q.reshape(B, S, A_HEADS, HD), qk_gain[0, 0]) * scale
    ka = rms_norm(a_k.reshape(B, S, A_HEADS, HD), qk_gain[0, 1])
    va = a_v.reshape(B, S, A_HEADS, HD)
    y_a = neighbourhood_attention(qa, ka, va, na_rel_bias, rows) * jax.nn.silu(a_z)

    cos, sin = axial_rope(S)
    qb = apply_rope(rms_norm(b_q.reshape(B, S, B_Q_HEADS, HD), qk_gain[1, 0]), cos, sin) * scale
    kb = apply_rope(rms_norm(b_k.reshape(B, S, B_KV_HEADS, HD), qk_gain[1, 1]), cos, sin)
    vb = b_v.reshape(B, S, B_KV_HEADS, HD)
    y_b = gqa_block_attention(qb, kb, vb) * jax.nn.silu(b_z)

    lam_init = 0.8 - 0.6 * math.exp(-0.3 * layer_idx)
    lam_p = diff_lambda.astype(jnp.float32)
    lam = jnp.exp(jnp.sum(lam_p[0] * lam_p[1])) - jnp.exp(jnp.sum(lam_p[2] * lam_p[3])) + lam_init
    qc = rms_norm(c_q.reshape(B, S, C_HEADS, 2, HD), qk_gain[2, 0]) * scale
    kc = rms_norm(c_k.reshape(B, S, C_HEADS, 2, HD), qk_gain[2, 1])
    vc = c_v.reshape(B, S, C_HEADS, 2 * HD)
    oc = diff_block_attention(qc, kc, vc, lam, alibi_slopes(C_HEADS))
    oc = rms_norm(oc, diff_subln_g) * (1.0 - lam_init)
    y_c = oc.reshape(B, S, C_VW) * jax.nn.silu(c_z)

    d_slopes = alibi_slopes(D_GROUPS * D_HEADS_PER_GROUP).reshape(D_GROUPS, D_HEADS_PER_GROUP)
    qd = rms_norm(d_q.reshape(B, S, D_GROUPS, D_HEADS_PER_GROUP, HD), qk_gain[3, 0]) * scale
    kd = rms_norm(d_k.reshape(B, S, D_GROUPS, D_HEADS_PER_GROUP, HD), qk_gain[3, 1])
    vd = d_v.reshape(B, S, D_GROUPS, D_HEADS_PER_GROUP, HD)
    y_d = dilated_attention(qd, kd, vd, d_slopes) * jax.nn.silu(d_z)

    merged = (jax.nn.sigmoid(g_a) * jnp.einsum('bse,ed->bsd', y_a, w_branch_a)
              + jax.nn.sigmoid(g_b) * jnp.einsum('bse,ed->bsd', y_b, w_branch_b)
              + jax.nn.sigmoid(g_c) * jnp.einsum('bse,ed->bsd', y_c, w_branch_c)
              + jax.nn.sigmoid(g_d) * jnp.einsum('bse,ed->bsd', y_d, w_branch_d))
    return x + jnp.einsum('bsd,de->bse', merged, w_out)


def setup_inputs(seed: int = 0) -> dict:
    key = jax.random.key(seed)
    ks = jax.random.split(key, 12)
    f32 = jnp.float32
    nrm = jax.random.normal
    return {
        'x': nrm(ks[0], (BATCH, SEQ, D_MODEL), f32),
        'norm_g': 1.0 + 0.01 * nrm(ks[1], (DEPTH, D_MODEL), f32),
        'w_in': nrm(ks[2], (DEPTH, D_MODEL, N_IN), f32) * D_MODEL ** -0.5,
        'qk_gain': 1.0 + 0.01 * nrm(ks[3], (DEPTH, N_BRANCHES, 2, HD), f32),
        'na_rel_bias': 0.1 * nrm(ks[4], (DEPTH, A_HEADS, 2 * NA_ROWS - 1, 2 * NA_COLS - 1), f32),
        'diff_lambda': 0.1 * nrm(ks[5], (DEPTH, 4, HD), f32),
        'diff_subln_g': 1.0 + 0.01 * nrm(ks[6], (DEPTH, 2 * HD), f32),
        'w_branch_a': nrm(ks[7], (DEPTH, A_W, D_MODEL), f32) * A_W ** -0.5,
        'w_branch_b': nrm(ks[8], (DEPTH, B_QW, D_MODEL), f32) * B_QW ** -0.5,
        'w_branch_c': nrm(ks[9], (DEPTH, C_VW, D_MODEL), f32) * C_VW ** -0.5,
        'w_branch_d': nrm(ks[10], (DEPTH, D_OW, D_MODEL), f32) * D_OW ** -0.5,
        'w_out': nrm(ks[11], (DEPTH, D_MODEL, D_MODEL), f32) * D_MODEL ** -0.5,
    }


def reference(x, norm_g, w_in, qk_gain, na_rel_bias, diff_lambda, diff_subln_g,
              w_branch_a, w_branch_b, w_branch_c, w_branch_d, w_out):
    for l in range(DEPTH):
        x = hybrid_layer(x, l, norm_g[l], w_in[l], qk_gain[l], na_rel_bias[l], diff_lambda[l],
                         diff_subln_g[l], w_branch_a[l], w_branch_b[l], w_branch_c[l],
                         w_branch_d[l], w_out[l])
    return x
```

```python
import math
import numpy as np
import ml_dtypes
import concourse.bass as bass
import concourse.mybir as mybir
from concourse.bass_utils import run_bass_kernel_spmd
from contextlib import ExitStack

F32 = mybir.dt.float32
BF16 = mybir.dt.bfloat16
AF = mybir.ActivationFunctionType
ALU = mybir.AluOpType
AX = mybir.AxisListType
NPBF = ml_dtypes.bfloat16

PE, ACT, DVE, POOL, SP = "tensor", "scalar", "vector", "gpsimd", "sync"
ENGINES = (PE, ACT, DVE, POOL, SP)

D_MODEL = 4096
SEQ = 4096
NQKVZ = 15872
N_IN = 32256
SLAB = 1024
NTB = 8
RMS_EPS = 1e-6
NEG = -1e30


class Res:
    __slots__ = ("name", "last_w", "readers", "dma_sem", "dma_cnt", "slot", "phase")

    def __init__(self, name):
        self.name = name
        self.last_w = None
        self.readers = []
        self.dma_sem = None
        self.dma_cnt = 0
        self.slot = -1
        self.phase = -1


class Op:
    __slots__ = ("eng", "fn", "waits", "is_dma", "needs_inc", "sem_res", "tok", "nd", "inc", "seq")

    def __init__(self, eng, fn):
        self.eng = eng
        self.fn = fn
        self.waits = []
        self.is_dma = False
        self.needs_inc = False
        self.sem_res = None
        self.tok = None
        self.nd = 1
        self.inc = 16
        self.seq = 0


class Prog:
    def __init__(self, nc):
        self.nc = nc
        self.q = {e: [] for e in ENGINES}
        self.stack = ExitStack()
        self.dma_res = []
        self.scopes = []
        self.phase = 0
        self.nslot = 0
        self.max_slot = 0
        self.reg_values = []
        self.regs = {}
        self._bar_t = self.stack.enter_context(nc.sbuf_tensor("bar_t", [128, 8], F32))
        self._bar_b = self.stack.enter_context(nc.sbuf_tensor("bar_b", [128, 8], BF16))
        self._bar_ps = self.stack.enter_context(nc.psum_tensor("bar_ps", [128, 8], F32))

    def sbuf(self, name, shape, dt):
        st = self.scopes[-1] if self.scopes else self.stack
        return st.enter_context(self.nc.sbuf_tensor(name, list(shape), dt))

    def psum(self, name, shape, dt=F32):
        st = self.scopes[-1] if self.scopes else self.stack
        return st.enter_context(self.nc.psum_tensor(name, list(shape), dt))

    def push_scope(self):
        self.scopes.append(ExitStack())

    def pop_scope(self):
        self.barrier()
        self.scopes.pop().close()
        if not self.scopes:
            self.phase += 1
            self.nslot = 0

    def _dep(self, op, prod):
        if prod is None or prod is op:
            return
        if prod.eng == PE and op.eng == PE and not prod.is_dma:
            return
        op.waits.append(prod)
        if not prod.is_dma:
            prod.needs_inc = True

    def op(self, eng, fn, reads=(), writes=()):
        o = Op(eng, fn)
        self.nseq = getattr(self, "nseq", 0) + 1
        o.seq = self.nseq
        reads = [r for r in reads if r is not None]
        writes = [r for r in writes if r is not None]
        for r in reads:
            self._dep(o, r.last_w)
        for r in writes:
            self._dep(o, r.last_w)
            for rd in r.readers:
                self._dep(o, rd)
        for r in reads:
            r.readers.append(o)
        for r in writes:
            r.last_w = o
            r.readers = []
        self.q[eng].append(o)
        return o

    def dma(self, eng, fn, sem_res, reads=(), writes=(), nd=1, inc=16):
        o = self.op(eng, fn, reads, writes)
        o.is_dma = True
        o.sem_res = sem_res
        o.nd = nd
        o.inc = inc
        if sem_res.dma_sem is None:
            sem_res.dma_sem = "pending"
            sem_res.slot = self.nslot
            sem_res.phase = self.phase
            self.nslot += 1
            self.max_slot = max(self.max_slot, self.nslot)
            self.dma_res.append(sem_res)
        return o

    def barrier(self):
        t = self._bar_t
        tb_ = self._bar_b
        ps = self._bar_ps
        arr = {}
        arr[DVE] = self.op(DVE, lambda e: e.memset(t[:, 0:1], 0.0))
        arr[ACT] = self.op(ACT, lambda e: e.activation(out=t[:, 1:2], in_=t[:, 4:5], func=AF.Copy))
        arr[POOL] = self.op(POOL, lambda e: e.memset(t[:, 2:3], 0.0))
        arr[PE] = self.op(PE, lambda e: e.matmul(ps[0:1, 0:1], tb_[:, 0:1], tb_[:, 0:1], start=True, stop=True))
        for o in arr.values():
            o.needs_inc = True
        dmas = []
        for e in ENGINES:
            last = {}
            for o in self.q[e]:
                if o.is_dma:
                    last[id(o.sem_res)] = o
            dmas.extend(last.values())
        for e in ENGINES:
            o = Op(e, lambda eng: eng.nop())
            o.waits = list(arr.values()) + dmas
            self.q[e].append(o)

    def emit(self):
        nc = self.nc
        st = self.stack
        esem = {e: st.enter_context(nc.semaphore("es_" + e)) for e in (PE, ACT, DVE, POOL)}
        slot_sems = [st.enter_context(nc.semaphore("ds%d" % i)) for i in range(self.max_slot)]
        local = {}
        all_dma = sorted((o for e in ENGINES for o in self.q[e] if o.is_dma), key=lambda o: o.seq)
        for o in all_dma:
            o.sem_res.dma_cnt += o.nd * o.inc
            local[id(o)] = o.sem_res.dma_cnt
        for e in ENGINES:
            c = 0
            for o in self.q[e]:
                if (not o.is_dma) and o.needs_inc:
                    c += 1
                    o.tok = (e, c)
        nph = self.phase + 1
        tot = [[0] * self.max_slot for _ in range(nph + 1)]
        for r in self.dma_res:
            tot[r.phase][r.slot] += r.dma_cnt
        base = [[0] * self.max_slot for _ in range(nph + 1)]
        for p in range(1, nph + 1):
            for sl in range(self.max_slot):
                base[p][sl] = base[p - 1][sl] + tot[p - 1][sl]
        for r in self.dma_res:
            r.dma_sem = slot_sems[r.slot]
        for e in ENGINES:
            for o in self.q[e]:
                if o.is_dma:
                    r = o.sem_res
                    o.tok = (r.slot, base[r.phase][r.slot] + local[id(o)])
        final = [(slot_sems[sl], base[nph][sl]) for sl in range(self.max_slot)]
        self.stats = dict(slots=self.max_slot, max_dma_val=max(base[nph]) if self.max_slot else 0,
                          eng={e: sum(1 for o in self.q[e] if (not o.is_dma) and o.needs_inc) for e in ENGINES})
        global LAST_STATS
        LAST_STATS = self.stats

        def run(e):
            def body(eng):
                seen = {}
                if e == POOL:
                    for v_ in self.reg_values:
                        reg = eng.alloc_register("c%d" % v_)
                        eng.reg_mov(reg, v_)
                        self.regs[v_] = reg
                for o in self.q[e]:
                    for p in o.waits:
                        key, val = p.tok
                        if seen.get(key, 0) >= val:
                            continue
                        seen[key] = val
                        sem = esem[key] if isinstance(key, str) else slot_sems[key]
                        eng.wait_ge(sem, val)
                    ins = o.fn(eng)
                    if o.is_dma:
                        if isinstance(ins, (list, tuple)):
                            assert len(ins) == o.nd
                            for i_ in ins:
                                i_.then_inc(o.sem_res.dma_sem, o.inc)
                        else:
                            assert o.nd == 1
                            ins.then_inc(o.sem_res.dma_sem, o.inc)
                    elif o.needs_inc:
                        ins.then_inc(esem[e], 1)
                if e == SP:
                    for sem, val in final:
                        if val > 0:
                            eng.wait_ge(sem, val)
            return body

        with nc.Block() as block:
            for e in ENGINES:
                getattr(block, e)(run(e))
        st.close()


def make_identity(P, name="ident"):
    idf = P.sbuf(name + "_f", [128, 128], F32)
    ident = P.sbuf(name, [128, 128], BF16)
    r_idf = Res(name + "_f")
    r_id = Res(name)
    P.op(POOL, lambda e: e.memset(idf[:], 0.0), writes=[r_idf])
    P.op(POOL, lambda e: e.affine_select(idf[:], idf[:], [[-1, 128]], ALU.not_equal, 1.0, base=0,
                                         channel_multiplier=1), reads=[r_idf], writes=[r_idf])
    P.op(DVE, lambda e: e.tensor_copy(ident[:], idf[:]), reads=[r_idf], writes=[r_id])
    return ident, r_id


def unit_table():
    U = []
    U += [("q", 0, i) for i in range(8)] + [("k", 0, i) for i in range(8)]
    U += [("v", 0, i) for i in range(8)] + [("z", 0, i) for i in range(8)]
    U += [("q", 1, 8 + i) for i in range(8)] + [("k", 1, 8 + i) for i in range(2)]
    U += [("v", 1, 8 + i) for i in range(2)] + [("z", 1, 8 + i) for i in range(8)]
    U += [("q", 2, 16 + i) for i in range(8)] + [("k", 2, 10 + i) for i in range(8)]
    U += [("v", 2, 10 + i) for i in range(8)] + [("z", 2, 16 + i) for i in range(8)]
    U += [("q", 3, 24 + i) for i in range(12)] + [("k", 3, 18 + i) for i in range(12)]
    U += [("v", 3, 18 + i) for i in range(12)] + [("z", 3, 24 + i) for i in range(4)]
    assert len(U) == 124
    return U


NQH, NKH, NVU, NZU = 36, 30, 30, 28


def build_l1():
    nc = bass.Bass("TRN2", target_bir_lowering=False)
    x = nc.dram_tensor("x", [SLAB, D_MODEL], F32, kind="ExternalInput").ap()
    ng = nc.dram_tensor("ng", [1, D_MODEL], F32, kind="ExternalInput").ap()
    w = nc.dram_tensor("w", [D_MODEL, NQKVZ], F32, kind="ExternalInput").ap()
    qkg = nc.dram_tensor("qkg", [1, 1024], F32, kind="ExternalInput").ap()
    cs = nc.dram_tensor("cs", [SLAB, 128], F32, kind="ExternalInput").ap()
    xnT_o = nc.dram_tensor("xnT_o", [32, 128, SLAB], BF16, kind="ExternalOutput").ap()
    qT_o = nc.dram_tensor("qT_o", [NQH, 128, SLAB], BF16, kind="ExternalOutput").ap()
    kT_o = nc.dram_tensor("kT_o", [NKH, 128, SLAB], BF16, kind="ExternalOutput").ap()
    v_o = nc.dram_tensor("v_o", [SLAB, NVU * 128], BF16, kind="ExternalOutput").ap()
    z_o = nc.dram_tensor("z_o", [SLAB, NZU * 128], F32, kind="ExternalOutput").ap()
    P = Prog(nc)
    emit_l1(P, x, ng, w, qkg, cs, xnT_o, qT_o, kT_o, v_o, z_o, 0)
    P.emit()
    return nc


def emit_norm_transpose(P, x, ng, xnT, r_xnT, ident, r_id, tag):
    P.push_scope()
    gt = P.sbuf("gt" + tag, [128, D_MODEL], F32)
    r_gt = Res("gt")
    xb = [P.sbuf("xb%d%s" % (i, tag), [128, D_MODEL], F32) for i in range(2)]
    r_xb = [Res("xb0"), Res("xb1")]
    xn = [P.sbuf("xn%d%s" % (i, tag), [128, D_MODEL], BF16) for i in range(2)]
    r_xn = [Res("xn0"), Res("xn1")]
    st = P.sbuf("st" + tag, [128, 8], F32)
    r_st = Res("st")
    eps = P.sbuf("eps" + tag, [128, 1], F32)
    r_eps = Res("eps")
    pt = [P.psum("ptA%d%s" % (i, tag), [128, 4, 128], BF16) for i in range(2)]
    r_pt = [Res("ptA0"), Res("ptA1")]
    P.op(DVE, lambda e: e.memset(eps[:], RMS_EPS), writes=[r_eps])
    P.dma(SP, lambda e: e.dma_start(out=gt[:].unsqueeze(1), in_=ng.partition_broadcast(128)), r_gt, writes=[r_gt])
    cnt = 0
    for tb in range(NTB):
        b = tb % 2
        P.dma(SP, lambda e, tb=tb, b=b: e.dma_start(out=xb[b][:], in_=x[tb * 128:(tb + 1) * 128, :]),
              r_xb[b], writes=[r_xb[b]])
        P.op(ACT, lambda e, b=b: e.activation(out=xn[b][:], in_=xb[b][:], func=AF.Square, accum_out=st[:, 0:1]),
             reads=[r_xb[b]], writes=[r_xn[b], r_st])
        P.op(ACT, lambda e: e.activation(out=st[:, 1:2], in_=st[:, 0:1], func=AF.Ln, scale=1.0 / D_MODEL, bias=eps[:]),
             reads=[r_st, r_eps], writes=[r_st])
        P.op(ACT, lambda e: e.activation(out=st[:, 2:3], in_=st[:, 1:2], func=AF.Exp, scale=-0.5),
             reads=[r_st], writes=[r_st])
        P.op(DVE, lambda e, b=b: e.scalar_tensor_tensor(out=xn[b][:], in0=xb[b][:], scalar=st[:, 2:3], in1=gt[:],
                                                        op0=ALU.mult, op1=ALU.mult),
             reads=[r_xb[b], r_st, r_gt], writes=[r_xn[b]])
        for g in range(8):
            pb = cnt % 2
            cnt += 1

            def tr(e, g=g, b=b, pb=pb):
                last = None
                for j in range(4):
                    k = g * 4 + j
                    last = e.transpose(pt[pb][:, j, :], xn[b][:, k * 128:(k + 1) * 128], ident[:])
                return last
            P.op(PE, tr, reads=[r_xn[b], r_id], writes=[r_pt[pb]])
            if pb == 0:
                P.op(DVE, lambda e, g=g, tb=tb, pb=pb: e.tensor_copy(xnT[:, g * 4:g * 4 + 4, tb * 128:(tb + 1) * 128], pt[pb][:]),
                     reads=[r_pt[pb]], writes=[r_xnT])
            else:
                P.op(ACT, lambda e, g=g, tb=tb, pb=pb: e.copy(xnT[:, g * 4:g * 4 + 4, tb * 128:(tb + 1) * 128], pt[pb][:]),
                     reads=[r_pt[pb]], writes=[r_xnT])
    P.pop_scope()


def emit_l1(P, x, ng, w, qkg, cs, xnT_o, qT_o, kT_o, v_o, z_o, layer, dbg_groups=None):
    tag = "_%d" % layer
    U = unit_table()
    scale = 128 ** -0.5
    P.push_scope()
    ident, r_id = make_identity(P, "ident" + tag)
    xnT = P.sbuf("xnT" + tag, [128, 32, SLAB], BF16)
    r_xnT = Res("xnT")
    emit_norm_transpose(P, x, ng, xnT, r_xnT, ident, r_id, tag)
    P.dma(SP, lambda e: [e.dma_start(out=xnT_o[8 * i:8 * i + 8].rearrange("k p t -> p k t"), in_=xnT[:, 8 * i:8 * i + 8, :])
                         for i in range(4)], r_xnT, reads=[r_xnT], nd=4)

    P.push_scope()
    wb = [P.sbuf("wb%d%s" % (i, tag), [128, 32, 512], BF16) for i in range(2)]
    r_wb = [Res("wb0"), Res("wb1")]
    gain = P.sbuf("gain" + tag, [128, 1024], F32)
    r_gain = Res("gain")
    cst = P.sbuf("cst" + tag, [128, NTB, 128], F32)
    r_cst = Res("cst")
    eps = P.sbuf("epsB" + tag, [128, 1], F32)
    r_eps = Res("epsB")
    ps = [P.psum("psB%d%s" % (i, tag), [128, 512], F32) for i in range(2)]
    r_ps = [Res("psB0"), Res("psB1")]
    pt = [P.psum("ptB%d%s" % (i, tag), [128, 4, 128], BF16) for i in range(2)]
    r_pt = [Res("ptB0"), Res("ptB1")]
    sq = [P.sbuf("sq%d%s" % (i, tag), [128, 512], F32) for i in range(2)]
    r_sq = [Res("sq0"), Res("sq1")]
    qn = [P.sbuf("qn%d%s" % (i, tag), [128, 512], F32) for i in range(2)]
    r_qn = [Res("qn0"), Res("qn1")]
    rt = [P.sbuf("rt%d%s" % (i, tag), [128, 4, 64], F32) for i in range(4)]
    r_rt = Res("rt")
    ssq = [P.sbuf("ssq%d%s" % (i, tag), [128, 16], F32) for i in range(2)]
    r_ssq = [Res("ssq0"), Res("ssq1")]
    qb16 = [P.sbuf("qb16%d%s" % (i, tag), [128, 512], BF16) for i in range(2)]
    r_qb16 = [Res("qb160"), Res("qb161")]
    stg = [P.sbuf("stg%d%s" % (i, tag), [128, 4, SLAB], BF16) for i in range(2)]
    r_stg = [Res("stg0"), Res("stg1")]
    vst = [P.sbuf("vst%d%s" % (i, tag), [128, 512], BF16) for i in range(2)]
    r_vst = [Res("vst0"), Res("vst1")]
    zst = [P.sbuf("zst%d%s" % (i, tag), [128, 512], F32) for i in range(2)]
    r_zst = [Res("zst0"), Res("zst1")]

    P.op(DVE, lambda e: e.memset(eps[:], RMS_EPS), writes=[r_eps])
    P.dma(SP, lambda e: e.dma_start(out=gain[:].unsqueeze(1), in_=qkg.partition_broadcast(128)), r_gain, writes=[r_gain])
    P.dma(SP, lambda e: e.dma_start(out=cst[:], in_=cs.rearrange("(t p) c -> p t c", p=128)), r_cst, writes=[r_cst])
    for m in range(4):
        P.op(ACT, lambda e, m=m: e.mul(gain[:, m * 256:m * 256 + 128], gain[:, m * 256:m * 256 + 128], scale),
             reads=[r_gain], writes=[r_gain])

    wv = w.rearrange("(k p) n -> p k n", p=128)
    NG = NQKVZ // 512 if dbg_groups is None else dbg_groups

    def load_w(cg):
        b = cg % 2
        P.dma(POOL, lambda e, cg=cg, b=b: [e.dma_start(out=wb[b][:, 8 * i:8 * i + 8, :],
                                                       in_=wv[:, 8 * i:8 * i + 8, cg * 512:(cg + 1) * 512]) for i in range(4)],
              r_wb[b], writes=[r_wb[b]], nd=4)

    load_w(0)
    tcount = 0
    pending = None
    trc = 0
    for cg in range(NG):
        if cg + 1 < NG:
            load_w(cg + 1)
        b = cg % 2
        units = U[cg * 4:cg * 4 + 4]
        segs = []
        for i, (kind, mixer, idx) in enumerate(units):
            if segs and segs[-1][2] == kind and segs[-1][3] == mixer:
                segs[-1][1] += 1
            else:
                segs.append([i, 1, kind, mixer, idx])
        sb = cg % 2
        has_qk = any(s[2] in ("q", "k") for s in segs)
        for tb in range(NTB):
            pb = tcount % 2
            tcount += 1

            def mm(e, tb=tb, b=b, pb=pb):
                last = None
                for k in range(32):
                    last = e.matmul(ps[pb][:], xnT[:, k, tb * 128:(tb + 1) * 128], wb[b][:, k, :],
                                    start=(k == 0), stop=(k == 31))
                return last
            P.op(PE, mm, reads=[r_xnT, r_wb[b]], writes=[r_ps[pb]])
            if pending is not None:
                pending()
                pending = None
            stage2 = []
            for (o0, n, kind, mixer, idx) in segs:
                sl = slice(o0 * 128, (o0 + n) * 128)
                if kind == "v":
                    P.op(DVE, lambda e, pb=pb, sl=sl: e.tensor_copy(vst[pb][:, sl], ps[pb][:, sl]),
                         reads=[r_ps[pb]], writes=[r_vst[pb]])
                    P.dma(SP, lambda e, pb=pb, sl=sl, tb=tb, idx=idx, n=n: e.dma_start(
                        out=v_o[tb * 128:(tb + 1) * 128, idx * 128:(idx + n) * 128], in_=vst[pb][:, sl]),
                        r_vst[pb], reads=[r_vst[pb]])
                elif kind == "z":
                    P.op(ACT, lambda e, pb=pb, sl=sl: e.activation(out=zst[pb][:, sl], in_=ps[pb][:, sl], func=AF.Silu),
                         reads=[r_ps[pb]], writes=[r_zst[pb]])
                    P.dma(SP, lambda e, pb=pb, sl=sl, tb=tb, idx=idx, n=n: e.dma_start(
                        out=z_o[tb * 128:(tb + 1) * 128, idx * 128:(idx + n) * 128], in_=zst[pb][:, sl]),
                        r_zst[pb], reads=[r_zst[pb]])
                else:
                    gi = mixer * 256 + (0 if kind == "q" else 128)
                    P.op(ACT, lambda e, pb=pb, sl=sl: e.activation(out=sq[pb][:, sl], in_=ps[pb][:, sl], func=AF.Square),
                         reads=[r_ps[pb]], writes=[r_sq[pb]])
                    P.op(DVE, lambda e, pb=pb, sl=sl, o0=o0, n=n: e.reduce_sum(
                        out=ssq[pb][:, o0:o0 + n], in_=sq[pb][:, sl].rearrange("p (n d) -> p n d", d=128), axis=AX.X),
                        reads=[r_sq[pb]], writes=[r_ssq[pb]])
                    P.op(ACT, lambda e, pb=pb, o0=o0, n=n: e.activation(out=ssq[pb][:, 4 + o0:4 + o0 + n], in_=ssq[pb][:, o0:o0 + n],
                                                                      func=AF.Ln, scale=1.0 / 128, bias=eps[:]),
                         reads=[r_ssq[pb], r_eps], writes=[r_ssq[pb]])
                    P.op(ACT, lambda e, pb=pb, o0=o0, n=n: e.activation(out=ssq[pb][:, 8 + o0:8 + o0 + n], in_=ssq[pb][:, 4 + o0:4 + o0 + n],
                                                                      func=AF.Exp, scale=-0.5),
                         reads=[r_ssq[pb]], writes=[r_ssq[pb]])
                    P.op(DVE, lambda e, pb=pb, sl=sl, o0=o0, n=n: e.tensor_tensor(
                        out=qn[pb][:, sl].rearrange("p (n d) -> p n d", d=128),
                        in0=ps[pb][:, sl].rearrange("p (n d) -> p n d", d=128),
                        in1=ssq[pb][:, 8 + o0:8 + o0 + n].unsqueeze(2).to_broadcast([128, n, 128]), op=ALU.mult),
                        reads=[r_ps[pb], r_ssq[pb]], writes=[r_qn[pb]])
                    rope = (mixer == 1)
                    gv = lambda n=n, gi=gi: gain[:, gi:gi + 128].unsqueeze(1).to_broadcast([128, n, 128])
                    if not rope:
                        P.op(DVE, lambda e, pb=pb, sl=sl, n=n, gv=gv: e.tensor_tensor(
                            out=qb16[pb][:, sl].rearrange("p (n d) -> p n d", d=128),
                            in0=qn[pb][:, sl].rearrange("p (n d) -> p n d", d=128), in1=gv(), op=ALU.mult),
                            reads=[r_qn[pb], r_gain], writes=[r_qb16[pb]])
                    else:
                        P.op(DVE, lambda e, pb=pb, sl=sl, n=n, gv=gv: e.tensor_tensor(
                            out=qn[pb][:, sl].rearrange("p (n d) -> p n d", d=128),
                            in0=qn[pb][:, sl].rearrange("p (n d) -> p n d", d=128), in1=gv(), op=ALU.mult),
                            reads=[r_qn[pb], r_gain], writes=[r_qn[pb]])
                        v4 = lambda pb=pb, sl=sl: qn[pb][:, sl].rearrange("p (n i two) -> p n i two", two=2, i=64)
                        o4 = lambda pb=pb, sl=sl: qb16[pb][:, sl].rearrange("p (n i two) -> p n i two", two=2, i=64)
                        cosv = lambda tb=tb, n=n: cst[:, tb, 0:64].unsqueeze(1).to_broadcast([128, n, 64])
                        sinv = lambda tb=tb, n=n: cst[:, tb, 64:128].unsqueeze(1).to_broadcast([128, n, 64])

                        def ropef(e, v4=v4, o4=o4, cosv=cosv, sinv=sinv, n=n):
                            x1 = v4()[:, :, :, 0]
                            x2 = v4()[:, :, :, 1]
                            e.tensor_tensor(out=rt[0][:, 0:n, :], in0=x1, in1=cosv(), op=ALU.mult)
                            e.tensor_tensor(out=rt[1][:, 0:n, :], in0=x2, in1=sinv(), op=ALU.mult)
                            e.tensor_tensor(out=rt[2][:, 0:n, :], in0=x1, in1=sinv(), op=ALU.mult)
                            return e.tensor_tensor(out=rt[3][:, 0:n, :], in0=x2, in1=cosv(), op=ALU.mult)
                        P.op(DVE, ropef, reads=[r_qn[pb], r_cst], writes=[r_rt])

                        def ropeg(e, o4=o4, n=n):
                            e.tensor_tensor(out=o4()[:, :, :, 0], in0=rt[0][:, 0:n, :], in1=rt[1][:, 0:n, :], op=ALU.subtract)
                            return e.tensor_tensor(out=o4()[:, :, :, 1], in0=rt[2][:, 0:n, :], in1=rt[3][:, 0:n, :], op=ALU.add)
                        P.op(DVE, ropeg, reads=[r_rt], writes=[r_qb16[pb], r_rt])
                    stage2.append((o0, n))
            if stage2:
                def s2(pb=pb, tb=tb, sb=sb, stage2=stage2):
                    nonlocal trc
                    for (o0, n) in stage2:
                        tp = trc % 2
                        trc += 1

                        def tr(e, pb=pb, o0=o0, n=n, tp=tp):
                            last = None
                            for j in range(n):
                                last = e.transpose(pt[tp][:, j, :], qb16[pb][:, (o0 + j) * 128:(o0 + j + 1) * 128], ident[:])
                            return last
                        P.op(PE, tr, reads=[r_qb16[pb], r_id], writes=[r_pt[tp]])
                        if tp == 0:
                            P.op(DVE, lambda e, tp=tp, o0=o0, n=n, tb=tb, sb=sb: e.tensor_copy(
                                stg[sb][:, o0:o0 + n, tb * 128:(tb + 1) * 128], pt[tp][:, 0:n, :]),
                                reads=[r_pt[tp]], writes=[r_stg[sb]])
                        else:
                            P.op(ACT, lambda e, tp=tp, o0=o0, n=n, tb=tb, sb=sb: e.copy(
                                stg[sb][:, o0:o0 + n, tb * 128:(tb + 1) * 128], pt[tp][:, 0:n, :]),
                                reads=[r_pt[tp]], writes=[r_stg[sb]])
                pending = s2
            if tb == NTB - 1 and has_qk:
                if pending is not None:
                    pending()
                    pending = None

                def store(e, sb=sb, segs=segs):
                    outs = []
                    for (o0, n, kind, mixer, idx) in segs:
                        if kind == "q":
                            outs.append(e.dma_start(out=qT_o[idx:idx + n].rearrange("h p t -> p h t"), in_=stg[sb][:, o0:o0 + n, :]))
                        elif kind == "k":
                            outs.append(e.dma_start(out=kT_o[idx:idx + n].rearrange("h p t -> p h t"), in_=stg[sb][:, o0:o0 + n, :]))
                    return outs
                nqk = sum(1 for s in segs if s[2] in ("q", "k"))
                P.dma(SP, store, r_stg[sb], reads=[r_stg[sb]], nd=nqk)
    if pending is not None:
        pending()
    P.pop_scope()
    P.pop_scope()


D_PAT = ((128, 1), (512, 4), (2048, 16))
DBG = {}
OD_W = 132


def d_tiles():
    items = []
    for g, (win, dil) in enumerate(D_PAT):
        ncls = dil
        cl3 = 3072 // dil
        cl1 = 1024 // dil
        qs = min(128, cl1)
        for r in range(ncls):
            for ub in range(cl1 // qs):
                q0 = r * cl1 + ub * qs
                kstart = r * cl3 + cl1 + ub * qs - 64
                nkeys = qs + 128
                blocks = []
                o = 0
                while o < nkeys:
                    nk = min(128, nkeys - o)
                    blocks.append((kstart + o, nk))
                    o += nk
                items.append((g, r, ub, qs, q0, blocks))
    return items


def build_l2(layer=0):
    nc = bass.Bass("TRN2", target_bir_lowering=False)
    dt = nc.dram_tensor
    I = lambda n, s, d=BF16: dt(n, s, d, kind="ExternalInput").ap()
    qT = I("qT", [NQH, 128, SLAB])
    kTa = I("kTa", [8, 128, 3072]); vEa = I("vEa", [8, 128, 24, 129])
    kTb = I("kTb", [2, 128, 4096]); vEb = I("vEb", [2, 128, 32, 129])
    kTc = I("kTc", [8, 128, 4096]); vEc = I("vEc", [4, 128, 32, 257])
    kTd = I("kTd", [12, 128, 3072]); vEd = I("vEd", [12, 3072, 129])
    sz = I("sz", [SLAB, NZU * 128], F32)
    abias = I("abias", [8, 128, 5, 7, 128], F32)
    cstrip = I("cstrip", [4, 128, 4992], F32)
    dbias = I("dbias", [128, 12, 2, 128], F32)
    dval = I("dval", [128, 64], F32)
    qkg = I("qkg", [1, 1024], F32)
    relb = I("relb", [1, 3720], F32)
    dlam = I("dlam", [1, 512], F32)
    subg = I("subg", [1, 256], F32)
    yT_o = dt("yT_o", [24, 128, SLAB], BF16, kind="ExternalOutput").ap()
    od_o = dt("od_o", [3, SLAB, 4, OD_W], F32, kind="ExternalOutput").ap()
    P = Prog(nc)
    emit_l2(P, layer, qT, kTa, vEa, kTb, vEb, kTc, vEc, kTd, vEd, sz, abias, cstrip, dbias, dval, qkg, relb, dlam, subg,
            yT_o, od_o)
    P.emit()
    return nc


def emit_l2(P, layer, qT, kTa, vEa, kTb, vEb, kTc, vEc, kTd, vEd, sz, abias, cstrip, dbias, dval, qkg, relb, dlam, subg,
            yT_o, od_o, fz=None, mixers="ABCD"):
    tag = "_a%d" % layer
    lam_init = 0.8 - 0.6 * math.exp(-0.3 * layer)
    P.push_scope()
    ident, r_id = make_identity(P, "identB" + tag)
    cons = P.sbuf("cons" + tag, [128, 64], F32)
    r_cons = Res("cons")
    gain = P.sbuf("gainc" + tag, [128, 1024], F32)
    r_gain = Res("gainc")
    bia = [P.sbuf("bia%d%s" % (i, tag), [128, 4992], F32) for i in range(2)]
    r_bia = [Res("bia0"), Res("bia1")]
    rb = bia[0][:, 0:3720]
    r_rb = r_bia[0]
    lamt = P.sbuf("lamt" + tag, [128, 512], F32)
    r_lamt = Res("lamt")
    sgt = P.sbuf("sgt" + tag, [128, 256], F32)
    r_sgt = Res("sgt")
    dvt = P.sbuf("dvt" + tag, [128, 64], F32)
    r_dvt = Res("dvt")
    dbt = P.sbuf("dbt" + tag, [128, 12, 2, 128], F32)
    r_dbt = Res("dbt")
    P.dma(SP, lambda e: e.dma_start(out=gain[:].unsqueeze(1), in_=qkg.partition_broadcast(128)), r_gain, writes=[r_gain])
    P.dma(SP, lambda e: e.dma_start(out=rb.unsqueeze(1), in_=relb.partition_broadcast(128)), r_rb, writes=[r_rb])
    P.dma(SP, lambda e: e.dma_start(out=lamt[:].unsqueeze(1), in_=dlam.partition_broadcast(128)), r_lamt, writes=[r_lamt])
    P.dma(SP, lambda e: e.dma_start(out=sgt[:].unsqueeze(1), in_=subg.partition_broadcast(128)), r_sgt, writes=[r_sgt])
    P.dma(SP, lambda e: e.dma_start(out=dvt[:], in_=dval), r_dvt, writes=[r_dvt])
    P.dma(SP, lambda e: e.dma_start(out=dbt[:], in_=dbias), r_dbt, writes=[r_dbt])
    P.op(DVE, lambda e: e.reduce_max(out=cons[:, 0:8], in_=gain[:].rearrange("p (n d) -> p n d", d=128), axis=AX.X,
                                     apply_absolute_value=True), reads=[r_gain], writes=[r_cons])
    for m in range(4):
        P.op(DVE, lambda e, m=m: e.scalar_tensor_tensor(out=cons[:, 8 + m:9 + m], in0=cons[:, 2 * m:2 * m + 1], scalar=-(128 ** 0.5),
                                                        in1=cons[:, 2 * m + 1:2 * m + 2], op0=ALU.mult, op1=ALU.mult),
             reads=[r_cons], writes=[r_cons])
    P.op(DVE, lambda e: e.reduce_max(out=cons[:, 12:13], in_=rb, axis=AX.X), reads=[r_rb], writes=[r_cons])
    P.op(DVE, lambda e: e.tensor_scalar(cons[:, 12:13], cons[:, 12:13], 0.0, None, ALU.max), reads=[r_cons], writes=[r_cons])
    P.op(DVE, lambda e: e.tensor_sub(cons[:, 8:9], cons[:, 8:9], cons[:, 12:13]), reads=[r_cons], writes=[r_cons])
    P.op(DVE, lambda e: e.memset(cons[:, 13:14], RMS_EPS), writes=[r_cons])
    P.op(DVE, lambda e: e.tensor_tensor(out=lamt[:, 0:128], in0=lamt[:, 0:128], in1=lamt[:, 128:256], op=ALU.mult),
         reads=[r_lamt], writes=[r_lamt])
    P.op(DVE, lambda e: e.tensor_tensor(out=lamt[:, 256:384], in0=lamt[:, 256:384], in1=lamt[:, 384:512], op=ALU.mult),
         reads=[r_lamt], writes=[r_lamt])
    P.op(DVE, lambda e: e.reduce_sum(out=cons[:, 14:15], in_=lamt[:, 0:128], axis=AX.X), reads=[r_lamt], writes=[r_cons])
    P.op(DVE, lambda e: e.reduce_sum(out=cons[:, 15:16], in_=lamt[:, 256:384], axis=AX.X), reads=[r_lamt], writes=[r_cons])
    P.op(ACT, lambda e: e.activation(out=cons[:, 14:16], in_=cons[:, 14:16], func=AF.Exp), reads=[r_cons], writes=[r_cons])
    P.op(DVE, lambda e: e.tensor_sub(cons[:, 16:17], cons[:, 15:16], cons[:, 14:15]), reads=[r_cons], writes=[r_cons])
    P.op(DVE, lambda e: e.tensor_scalar(cons[:, 16:17], cons[:, 16:17], -lam_init, None, ALU.add), reads=[r_cons], writes=[r_cons])
    P.op(DVE, lambda e: e.tensor_scalar(dvt[:], dvt[:], cons[:, 11:12], None, ALU.add), reads=[r_cons, r_dvt], writes=[r_dvt])
    P.op(ACT, lambda e: e.mul(sgt[:], sgt[:], 1.0 - lam_init), reads=[r_sgt], writes=[r_sgt])

    sT = [P.psum("sT%d%s" % (i, tag), [128, 512], F32) for i in range(2)]
    r_sT = [Res("sT0"), Res("sT1")]
    Ot = [P.psum("O%d%s" % (i, tag), [128, 512], F32) for i in range(2)]
    r_O = [Res("O0"), Res("O1")]
    ptT = P.psum("ptT" + tag, [128, 2, 128], BF16)
    r_ptT = Res("ptT")
    tmp = [P.sbuf("tmp%d%s" % (i, tag), [128, 512], F32) for i in range(2)]
    r_tmp = [Res("tmp0"), Res("tmp1")]
    pT = [P.sbuf("pT%d%s" % (i, tag), [128, 512], BF16) for i in range(2)]
    r_pT = [Res("pT0"), Res("pT1")]
    kt = [P.sbuf("kt%d%s" % (i, tag), [128, 4096], BF16) for i in range(2)]
    r_kt = [Res("kt0"), Res("kt1")]
    vt = [P.sbuf("vt%d%s" % (i, tag), [128, 32 * 257], BF16) for i in range(2)]
    r_vt = [Res("vt0"), Res("vt1")]
    qt_ = [P.sbuf("qt%d%s" % (i, tag), [128, 2, SLAB], BF16) for i in range(2)]
    r_qt = [Res("qt0"), Res("qt1")]
    szt = [P.sbuf("szt%d%s" % (i, tag), [128, NTB, 256], F32) for i in range(2)]
    r_szt = [Res("szt0"), Res("szt1")]
    ysb = P.sbuf("ysb" + tag, [128, 256], BF16)
    r_ysb = Res("ysb")
    yst = [P.sbuf("yst%d%s" % (i, tag), [128, 2, SLAB], BF16) for i in range(2)]
    r_yst = [Res("yst0"), Res("yst1")]
    fin = P.sbuf("fin" + tag, [128, 16], F32)
    r_fin = Res("fin")
    f1 = P.sbuf("f1" + tag, [128, 256], F32)
    r_f1 = Res("f1")
    f2 = P.sbuf("f2" + tag, [128, 256], F32)
    r_f2 = Res("f2")
    gz = P.sbuf("gz" + tag, [128, 256], F32)
    r_gz = Res("gz")
    odt = [P.sbuf("odt%d%s" % (i, tag), [128, OD_W], F32) for i in range(2)]
    r_odt = [Res("odt0"), Res("odt1")]
    dv = [P.sbuf("dv%d%s" % (i, tag), [128, 2, 129], BF16) for i in range(2)]
    r_dv = [Res("dv0"), Res("dv1")]

    I32 = mybir.dt.int32
    if fz is not None:
        ixK = P.sbuf("ixK" + tag, [128, 90], I32)
        ixA = P.sbuf("ixA" + tag, [128, 192], I32)
        ixD = P.sbuf("ixD" + tag, [128, 768], I32)
        r_ix = Res("ix")
        P.dma(SP, lambda e: [e.dma_start(out=ixK[:], in_=fz["idxK"]), e.dma_start(out=ixA[:], in_=fz["idxVA"]),
                             e.dma_start(out=ixD[:], in_=fz["idxVD"])], r_ix, writes=[r_ix], nd=3)
        vA = P.sbuf("vA" + tag, [128, 14, 8, 129], BF16)
        r_vA = Res("vA")
        ktcm = P.sbuf("ktcm" + tag, [128, 3072], BF16)
        r_ktcm = Res("ktcm")
        qcm = P.sbuf("qcm" + tag, [128, SLAB], BF16)
        r_qcm = Res("qcm")
        for i in range(2):
            P.op(POOL, lambda e, i=i: e.memset(kt[i][:], 0.0), writes=[r_kt[i]])
            P.op(POOL, lambda e, i=i: e.memset(dv[i][:], 1.0), writes=[r_dv[i]])
            P.op(POOL, lambda e, i=i: e.memset(vt[i][:], 1.0), writes=[r_vt[i]])
        P.op(POOL, lambda e: e.memset(vA[:], 1.0), writes=[r_vA])
        G_k4, G_kr, G_v, r_G = fz["G_k4"], fz["G_kr"], fz["G_v"], fz["r_G"]

        def k_window(hb, kidx):
            P.dma(POOL, lambda e: [e.reg_mov(P.regs[15359], 15359)] and [e.indirect_dma_start(out=kt[hb][:, d_ * 1024:(d_ + 1) * 1024], out_offset=None, in_=G_kr,
                                                        in_offset=bass.IndirectOffsetOnAxis(ap=ixK[:, kidx * 3 + d_:kidx * 3 + d_ + 1], axis=0),
                                                        bounds_check=P.regs[15359], oob_is_err=False) for d_ in range(3)],
                  r_kt[hb], reads=[r_ix, r_G], writes=[r_kt[hb]], nd=3)

        def k_global(dst, r_dst, kidx):
            P.dma(SP, lambda e: e.dma_start(out=dst[:, 0:4096].rearrange("d (r t) -> d r t", r=4),
                                            in_=G_k4[kidx // 3, :, kidx % 3].rearrange("r d t -> d r t")), r_dst, reads=[r_G], writes=[r_dst])

        def v_global(hb, c0, wv):
            P.dma(SP, lambda e: [e.dma_start(
                out=vt[hb][:, 0:32 * (wv + 1)].rearrange("p (r tb w) -> p r tb w", r=4, tb=8, w=wv + 1)[:, r_, :, 0:wv],
                in_=G_v.rearrange("(tb r p) c -> tb r p c", tb=8, r=4, p=128)[:, r_, :, c0:c0 + wv].rearrange("tb p c -> p tb c"))
                for r_ in range(4)], r_vt[hb], reads=[r_G], writes=[r_vt[hb]], nd=4)

    st = {"s": 0, "o": 0}
    deferred = []

    def flush():
        while deferred:
            deferred.pop(0)()

    def attend(q_ap, nq, blocks, W, negc_col, oi, bias_fn=None, col_fn=None, r_q=None, r_k=None, r_v=None, r_b=None):
        nb = len(blocks)
        groups = [(gi, blocks[gi:gi + 4]) for gi in range(0, nb, 4)]
        sbs = []
        for _ in groups:
            sbs.append(st["s"] % 2)
            st["s"] += 1

        def issue_qk(idx):
            gi, grp = groups[idx]
            sb = sbs[idx]

            def qk(e, grp=grp, sb=sb):
                last = None
                for j, (k_ap, v_ap, nk) in enumerate(grp):
                    last = e.matmul(sT[sb][0:nk, j * nq:(j + 1) * nq], k_ap, q_ap, start=True, stop=True)
                return last
            P.op(PE, qk, reads=[r_q, r_k], writes=[r_sT[sb]])
            flush()

        issue_qk(0)
        for idx, (gi, grp) in enumerate(groups):
            sb = sbs[idx]
            if idx + 1 < len(groups):
                issue_qk(idx + 1)
            uniform = all(nk == 128 for (_, _, nk) in grp) and col_fn is None
            src = sT[sb]
            r_src = r_sT[sb]
            if bias_fn is not None:
                def addb(e, grp=grp, sb=sb, gi=gi):
                    last = None
                    if all(nk == 128 for (_, _, nk) in grp) and bias_fn(gi, len(grp)) is not None:
                        return e.tensor_tensor(out=tmp[sb][:, 0:len(grp) * nq], in0=sT[sb][:, 0:len(grp) * nq],
                                               in1=bias_fn(gi, len(grp)), op=ALU.add)
                    for j, (k_ap, v_ap, nk) in enumerate(grp):
                        last = e.tensor_tensor(out=tmp[sb][0:nk, j * nq:(j + 1) * nq], in0=sT[sb][0:nk, j * nq:(j + 1) * nq],
                                               in1=bias_fn(gi + j, 1)[0:nk], op=ALU.add)
                    return last
                P.op(DVE, addb, reads=[r_sT[sb], r_b], writes=[r_tmp[sb]])
                src = tmp[sb]
                r_src = r_tmp[sb]

            def ex(e, grp=grp, sb=sb, gi=gi, src=src, uniform=uniform):
                if uniform:
                    return e.activation(out=pT[sb][:, 0:len(grp) * nq], in_=src[:, 0:len(grp) * nq], func=AF.Exp, bias=negc_col)
                last = None
                for j, (k_ap, v_ap, nk) in enumerate(grp):
                    col = negc_col if col_fn is None else col_fn(gi + j)
                    last = e.activation(out=pT[sb][0:nk, j * nq:(j + 1) * nq], in_=src[0:nk, j * nq:(j + 1) * nq], func=AF.Exp,
                                        bias=col[0:nk])
                return last
            P.op(ACT, ex, reads=[r_src, r_cons, r_dvt], writes=[r_pT[sb]])

            def pv(e, grp=grp, sb=sb, first=(idx == 0), last_g=(idx == len(groups) - 1)):
                last = None
                for j, (k_ap, v_ap, nk) in enumerate(grp):
                    last = e.matmul(Ot[oi][0:nq, 0:W], pT[sb][0:nk, j * nq:(j + 1) * nq], v_ap,
                                    start=(first and j == 0), stop=(last_g and j == len(grp) - 1))
                return last
            deferred.append(lambda pv=pv, sb=sb: P.op(PE, pv, reads=[r_pT[sb], r_v], writes=[r_O[oi]]))

    hc = {"n": 0}

    def head_loads(kT_src, kcols, v_src, vcols, q_srcs, sz_cols, bias_src=None, bias_cols=0):
        hb = hc["n"] % 2
        hc["n"] += 1
        if kT_src is not None:
            P.dma(SP, lambda e: e.dma_start(out=kt[hb][:, 0:kcols], in_=kT_src), r_kt[hb], writes=[r_kt[hb]])
        if v_src is not None:
            P.dma(SP, lambda e: e.dma_start(out=vt[hb][:, 0:vcols], in_=v_src), r_vt[hb], writes=[r_vt[hb]])
        P.dma(SP, lambda e: [e.dma_start(out=qt_[hb][:, i, :], in_=qs_) for i, qs_ in enumerate(q_srcs)], r_qt[hb],
              writes=[r_qt[hb]], nd=len(q_srcs))
        if sz_cols is not None:
            c0, cn = sz_cols
            P.dma(SP, lambda e: e.dma_start(out=szt[hb][:, :, 0:cn], in_=sz[:, c0:c0 + cn].rearrange("(t p) c -> p t c", p=128)),
                  r_szt[hb], writes=[r_szt[hb]])
        if bias_src is not None:
            P.dma(SP, lambda e: e.dma_start(out=bia[hb][:, 0:bias_cols], in_=bias_src), r_bia[hb], writes=[r_bia[hb]])
        return hb

    def finalize_simple(oi, hb, tb, u_local, ystage, slot):
        r_ys = r_yst[yi["n"] % 2]
        deferred.append(lambda: finalize_simple_now(oi, hb, tb, u_local, ystage, slot, r_ys))

    def finalize_simple_now(oi, hb, tb, u_local, ystage, slot, r_ys):
        P.op(DVE, lambda e: e.reciprocal(fin[:, 0:1], Ot[oi][:, 128:129]), reads=[r_O[oi]], writes=[r_fin])
        P.op(DVE, lambda e: e.scalar_tensor_tensor(out=ysb[:, 0:128], in0=Ot[oi][:, 0:128], scalar=fin[:, 0:1],
                                                   in1=szt[hb][:, tb, u_local * 128:(u_local + 1) * 128], op0=ALU.mult, op1=ALU.mult),
             reads=[r_O[oi], r_fin, r_szt[hb]], writes=[r_ysb])
        P.op(PE, lambda e: e.transpose(ptT[:, 0, :], ysb[:, 0:128], ident[:]), reads=[r_ysb, r_id], writes=[r_ptT])
        P.op(ACT, lambda e: e.copy(ystage[:, slot, tb * 128:(tb + 1) * 128], ptT[:, 0, :]), reads=[r_ptT], writes=[r_ys])

    yi = {"n": 0}

    slot_of = [0, 1, 2, 2, 2, 2, 3, 4]
    if fz is not None and "A" in mixers:
        for i in range(14):
            P.dma(POOL, lambda e, i=i: [e.reg_mov(P.regs[SEQ * NVU - 1], SEQ * NVU - 1)] and [
                e.indirect_dma_start(out=vA[:, i, h_, 0:128], out_offset=None, in_=G_v.rearrange("t (c d) -> (t c) d", d=128),
                                     in_offset=bass.IndirectOffsetOnAxis(ap=ixA[:, (5 + i) * 8 + h_:(5 + i) * 8 + h_ + 1], axis=0),
                                     bounds_check=P.regs[SEQ * NVU - 1], oob_is_err=False) for h_ in range(8)],
                  r_vA, reads=[r_ix, r_G], writes=[r_vA], nd=8)
    for h in (range(8) if "A" in mixers else []):
        if fz is None:
            hb = head_loads(kTa[h], 3072, vEa[h].rearrange("p k w -> p (k w)"), 24 * 129, [qT[h]], (h * 128, 128),
                            abias[h].rearrange("p s k q -> p (s k q)"), 5 * 7 * 128)
        else:
            hb = head_loads(None, 0, None, 0, [qT[h]], (h * 128, 128), abias[h].rearrange("p s k q -> p (s k q)"), 5 * 7 * 128)
            k_window(hb, h)
        yb = yi["n"] % 2
        for qb in range(NTB):
            oi = st["o"] % 2
            st["o"] += 1
            blocks = []
            for i in range(7):
                kb = 8 + qb - 3 + i
                if fz is None:
                    blocks.append((kt[hb][:, kb * 128:(kb + 1) * 128], vt[hb][:, kb * 129:(kb + 1) * 129], 128))
                else:
                    blocks.append((kt[hb][:, kb * 128:(kb + 1) * 128], vA[:, kb - 5, h, :], 128))
            s = slot_of[qb]
            bf = lambda gi, n, s=s, hb=hb: bia[hb][:, (s * 7 + gi) * 128:(s * 7 + gi + n) * 128]
            if DBG.get("A") == "loads":
                continue
            attend(qt_[hb][:, 0, qb * 128:(qb + 1) * 128], 128, blocks, 129, cons[:, 8:9], oi, bias_fn=bf,
                   r_q=r_qt[hb], r_k=r_kt[hb], r_v=(r_vt[hb] if fz is None else r_vA), r_b=r_bia[hb])
            if DBG.get("A") == "nofin":
                continue
            finalize_simple(oi, hb, qb, 0, yst[yb], 0)
        flush()
        P.dma(SP, lambda e, h=h, yb=yb: e.dma_start(out=yT_o[h], in_=yst[yb][:, 0, :]), r_yst[yb], reads=[r_yst[yb]])
        yi["n"] += 1

    for h in (range(8) if "B" in mixers else []):
        kv = h // 4
        if fz is None:
            hb = head_loads(kTb[kv], 4096, vEb[kv].rearrange("p k w -> p (k w)"), 32 * 129, [qT[8 + h]], ((8 + h) * 128, 128))
        else:
            hb = head_loads(None, 0, None, 0, [qT[8 + h]], ((8 + h) * 128, 128))
            k_global(kt[hb], r_kt[hb], 8 + kv)
            v_global(hb, 1024 + kv * 128, 128)
        yb = yi["n"] % 2
        for qb in range(NTB):
            oi = st["o"] % 2
            st["o"] += 1
            blocks = [(kt[hb][:, kb * 128:(kb + 1) * 128], vt[hb][:, kb * 129:(kb + 1) * 129], 128) for kb in range(32)]
            attend(qt_[hb][:, 0, qb * 128:(qb + 1) * 128], 128, blocks, 129, cons[:, 9:10], oi,
                   r_q=r_qt[hb], r_k=r_kt[hb], r_v=r_vt[hb])
            finalize_simple(oi, hb, qb, 0, yst[yb], 0)
        flush()
        P.dma(SP, lambda e, h=h, yb=yb: e.dma_start(out=yT_o[8 + h], in_=yst[yb][:, 0, :]), r_yst[yb], reads=[r_yst[yb]])
        yi["n"] += 1

    kt2 = P.sbuf("kt2" + tag, [128, 4096], BF16)
    r_kt2 = Res("kt2")
    if fz is not None:
        for i in range(2):
            P.op(POOL, lambda e, i=i: e.memset(vt[i][:], 1.0), writes=[r_vt[i]])
    for h in (range(4) if "C" in mixers else []):
        if fz is None:
            hb = head_loads(kTc[2 * h], 4096, vEc[h].rearrange("p k w -> p (k w)"), 32 * 257, [qT[16 + 2 * h], qT[16 + 2 * h + 1]],
                            ((16 + 2 * h) * 128, 256), cstrip[h], 4992)
            P.dma(SP, lambda e, h=h: e.dma_start(out=kt2[:], in_=kTc[2 * h + 1]), r_kt2, writes=[r_kt2])
        else:
            hb = head_loads(None, 0, None, 0, [qT[16 + 2 * h], qT[16 + 2 * h + 1]], ((16 + 2 * h) * 128, 256), cstrip[h], 4992)
            k_global(kt[hb], r_kt[hb], 10 + 2 * h)
            k_global(kt2, r_kt2, 10 + 2 * h + 1)
            v_global(hb, 1280 + h * 256, 256)
        yb = yi["n"] % 2
        for qb in range(NTB):
            for m in range(2):
                ksrc = kt[hb] if m == 0 else kt2
                blocks = [(ksrc[:, kb * 128:(kb + 1) * 128], vt[hb][:, kb * 257:(kb + 1) * 257], 128) for kb in range(32)]
                bf = lambda gi, n, qb=qb, hb=hb: (bia[hb][:, (qb - gi) * 128 + 3968:(qb - gi) * 128 + 3968 + 128] if n == 1 else None)
                attend(qt_[hb][:, m, qb * 128:(qb + 1) * 128], 128, blocks, 257, cons[:, 10:11], m, bias_fn=bf,
                       r_q=r_qt[hb], r_k=(r_kt[hb] if m == 0 else r_kt2), r_v=r_vt[hb], r_b=r_bia[hb])
            flush()
            P.op(DVE, lambda e: e.reciprocal(fin[:, 0:1], Ot[0][:, 256:257]), reads=[r_O[0]], writes=[r_fin])
            P.op(DVE, lambda e: e.reciprocal(fin[:, 1:2], Ot[1][:, 256:257]), reads=[r_O[1]], writes=[r_fin])
            P.op(DVE, lambda e: e.tensor_tensor(out=fin[:, 2:3], in0=fin[:, 1:2], in1=cons[:, 16:17], op=ALU.mult),
                 reads=[r_fin, r_cons], writes=[r_fin])
            P.op(DVE, lambda e: e.tensor_scalar(f1[:], Ot[0][:, 0:256], fin[:, 0:1], None, ALU.mult), reads=[r_O[0], r_fin], writes=[r_f1])
            P.op(DVE, lambda e: e.scalar_tensor_tensor(out=f2[:], in0=Ot[1][:, 0:256], scalar=fin[:, 2:3], in1=f1[:],
                                                       op0=ALU.mult, op1=ALU.add), reads=[r_O[1], r_fin, r_f1], writes=[r_f2])
            P.op(ACT, lambda e: e.activation(out=f1[:], in_=f2[:], func=AF.Square, accum_out=fin[:, 3:4]),
                 reads=[r_f2], writes=[r_f1, r_fin])
            P.op(ACT, lambda e: e.activation(out=fin[:, 4:5], in_=fin[:, 3:4], func=AF.Ln, scale=1.0 / 256, bias=cons[:, 13:14]),
                 reads=[r_fin, r_cons], writes=[r_fin])
            P.op(ACT, lambda e: e.activation(out=fin[:, 5:6], in_=fin[:, 4:5], func=AF.Exp, scale=-0.5), reads=[r_fin], writes=[r_fin])
            P.op(DVE, lambda e, hb=hb, qb=qb: e.tensor_tensor(out=gz[:], in0=szt[hb][:, qb, 0:256], in1=sgt[:], op=ALU.mult),
                 reads=[r_szt[hb], r_sgt], writes=[r_gz])
            P.op(DVE, lambda e: e.scalar_tensor_tensor(out=ysb[:, 0:256], in0=f2[:], scalar=fin[:, 5:6], in1=gz[:],
                                                       op0=ALU.mult, op1=ALU.mult), reads=[r_f2, r_fin, r_gz], writes=[r_ysb])

            def tr2(e):
                e.transpose(ptT[:, 0, :], ysb[:, 0:128], ident[:])
                return e.transpose(ptT[:, 1, :], ysb[:, 128:256], ident[:])
            P.op(PE, tr2, reads=[r_ysb, r_id], writes=[r_ptT])
            P.op(ACT, lambda e, qb=qb, yb=yb: e.copy(yst[yb][:, :, qb * 128:(qb + 1) * 128], ptT[:, :, :]), reads=[r_ptT], writes=[r_yst[yb]])
        P.dma(SP, lambda e, h=h, yb=yb: e.dma_start(out=yT_o[16 + 2 * h:16 + 2 * h + 2].rearrange("u p t -> p u t"), in_=yst[yb][:, :, :]),
              r_yst[yb], reads=[r_yst[yb]])
        yi["n"] += 1

    items = d_tiles()
    col_idx = 0
    cols_of = {}
    for it_i, (g, r, ub, qs, q0, blocks) in enumerate(items):
        cols_of[it_i] = list(range(col_idx, col_idx + len(blocks)))
        col_idx += len(blocks)
    assert col_idx <= 64
    oc = 0
    for g in (range(3) if "D" in mixers else []):
        for hh in range(4):
            hd = g * 4 + hh
            if fz is None:
                hb = head_loads(kTd[hd], 3072, None, 0, [qT[24 + hd]], None)
                ksrc, r_ksrc, qsrc, r_qsrc = kt[hb], r_kt[hb], qt_[hb][:, 0, :], r_qt[hb]
            else:
                hb = head_loads(None, 0, None, 0, [qT[24 + hd]], None)
                k_window(hb, 18 + hd)
                dil = D_PAT[g][1]
                if dil == 1:
                    ksrc, r_ksrc, qsrc, r_qsrc = kt[hb], r_kt[hb], qt_[hb][:, 0, :], r_qt[hb]
                else:
                    P.op(POOL, lambda e, hb=hb, dil=dil: e.tensor_copy(ktcm[:, 0:3072].rearrange("p (r u) -> p r u", r=dil),
                                                                       kt[hb][:, 0:3072].rearrange("p (u r) -> p r u", r=dil)),
                         reads=[r_kt[hb]], writes=[r_ktcm])
                    P.op(POOL, lambda e, hb=hb, dil=dil: e.tensor_copy(qcm[:, :].rearrange("p (r u) -> p r u", r=dil),
                                                                       qt_[hb][:, 0, :].rearrange("p (u r) -> p r u", r=dil)),
                         reads=[r_qt[hb]], writes=[r_qcm])
                    ksrc, r_ksrc, qsrc, r_qsrc = ktcm, r_ktcm, qcm[:, :], r_qcm
            for it_i, (g2, r, ub, qs, q0, blocks) in enumerate(items):
                if g2 != g:
                    continue
                oi = st["o"] % 2
                st["o"] += 1
                vbuf = dv[oc % 2]
                r_vbuf = r_dv[oc % 2]
                oc += 1
                if fz is None:
                    P.dma(SP, lambda e, hd=hd, blocks=blocks, vbuf=vbuf: [
                        e.dma_start(out=vbuf[0:nk, j, :], in_=vEd[hd][k0:k0 + nk, :]) for j, (k0, nk) in enumerate(blocks)],
                        r_vbuf, writes=[r_vbuf], nd=len(blocks))
                else:
                    P.dma(POOL, lambda e, hd=hd, blocks=blocks, vbuf=vbuf, it_i=it_i: [e.reg_mov(P.regs[SEQ * NVU - 1], SEQ * NVU - 1)] and [
                        e.indirect_dma_start(out=vbuf[0:nk, j, 0:128], out_offset=None, in_=G_v.rearrange("t (c d) -> (t c) d", d=128),
                                             in_offset=bass.IndirectOffsetOnAxis(ap=ixD[0:nk, hd * 64 + cols_of[it_i][j]:hd * 64 + cols_of[it_i][j] + 1], axis=0),
                                             bounds_check=P.regs[SEQ * NVU - 1], oob_is_err=False) for j, (k0, nk) in enumerate(blocks)],
                        r_vbuf, reads=[r_ix, r_G], writes=[r_vbuf], nd=len(blocks))
                blocks3 = [(ksrc[:, k0:k0 + nk], vbuf[0:nk, j, :], nk) for j, (k0, nk) in enumerate(blocks)]
                bf = lambda gi, n, hd=hd, qs=qs: (dbt[:, hd, gi, 0:qs] if n == 1 else None)
                cf = lambda bi, it_i=it_i: dvt[:, cols_of[it_i][bi]:cols_of[it_i][bi] + 1]
                attend(qsrc[:, q0:q0 + qs], qs, blocks3, 129, None, oi, bias_fn=bf, col_fn=cf,
                       r_q=r_qsrc, r_k=r_ksrc, r_v=r_vbuf, r_b=r_dbt)
                def d_fin(oi=oi, g=g, hh=hh, q0=q0, qs=qs, r=r, ub=ub):
                    ob = oi
                    P.op(DVE, lambda e, ob=ob, qs=qs: e.tensor_copy(odt[ob][0:qs, 0:129], Ot[ob][0:qs, 0:129]), reads=[r_O[ob]], writes=[r_odt[ob]])
                    if fz is None:
                        P.dma(SP, lambda e, ob=ob, g=g, hh=hh, q0=q0, qs=qs: e.dma_start(out=od_o[g, q0:q0 + qs, hh, 0:129], in_=odt[ob][0:qs, 0:129]),
                              r_odt[ob], reads=[r_odt[ob]])
                    else:
                        dil = D_PAT[g][1]
                        P.dma(SP, lambda e, ob=ob, g=g, hh=hh, r=r, ub=ub, qs=qs, dil=dil: e.dma_start(
                            out=od_o.rearrange("(u r) g h w -> r u g h w", r=dil)[r, ub * qs:(ub + 1) * qs, g, hh, 0:129], in_=odt[ob][0:qs, 0:129]),
                            r_odt[ob], reads=[r_odt[ob]])

                deferred.append(d_fin)
    flush()
    P.pop_scope()


BR_U0 = (0, 8, 16, 24)
BR_NK = (8, 8, 8, 4)


def build_l3():
    nc = bass.Bass("TRN2", target_bir_lowering=False)
    dt = nc.dram_tensor
    I = lambda n, s, d=BF16: dt(n, s, d, kind="ExternalInput").ap()
    xnT_i = I("xnT", [32, 128, SLAB])
    yT_i = I("yT", [24, 128, SLAB])
    od = I("od", [SLAB, 3, 4, OD_W], F32)
    szd = I("szd", [SLAB, 512], F32)
    wg = I("wg", [D_MODEL, 4 * D_MODEL], F32)
    wbr = I("wbr", [3584, D_MODEL], F32)
    mT_o = dt("mT_o", [32, 128, SLAB], BF16, kind="ExternalOutput").ap()
    P = Prog(nc)
    emit_l3(P, 0, xnT_i, yT_i, od, szd, wg, wbr, mT_o)
    P.emit()
    return nc


def emit_l3(P, layer, xnT_i, yT_i, od, szd, wg, wbr, mT_o):
    tag = "_m%d" % layer
    P.push_scope()
    ident, r_id = make_identity(P, "identM" + tag)
    xnT = P.sbuf("xnTm" + tag, [128, 32, SLAB], BF16)
    r_xnT = Res("xnTm")
    yT = P.sbuf("yTm" + tag, [128, 28, SLAB], BF16)
    r_yT = Res("yTm")
    P.dma(SP, lambda e: [e.dma_start(out=xnT[:, 8 * i:8 * i + 8, :], in_=xnT_i[8 * i:8 * i + 8].rearrange("k p t -> p k t"))
                         for i in range(4)], r_xnT, writes=[r_xnT], nd=4)
    r_yTl = Res("yTl")
    P.dma(SP, lambda e: [e.dma_start(out=yT[:, 8 * i:8 * i + 8, :], in_=yT_i[8 * i:8 * i + 8].rearrange("k p t -> p k t"))
                         for i in range(3)], r_yTl, writes=[r_yT], nd=3)
    odt = [P.sbuf("odm%d%s" % (i, tag), [128, 3, 4, OD_W], F32) for i in range(2)]
    r_odt = [Res("odm0"), Res("odm1")]
    szt = [P.sbuf("szm%d%s" % (i, tag), [128, 512], F32) for i in range(2)]
    r_szt = [Res("szm0"), Res("szm1")]
    acc = P.sbuf("accd" + tag, [128, 4, OD_W], F32)
    r_acc = Res("accd")
    rec = P.sbuf("recd" + tag, [128, 4], F32)
    r_rec = Res("recd")
    yd = P.sbuf("yd" + tag, [128, 512], BF16)
    r_yd = Res("yd")
    ptD = P.psum("ptD" + tag, [128, 4, 128], BF16)
    r_ptD = Res("ptD")
    for tb in range(NTB):
        b = tb % 2
        P.dma(SP, lambda e, tb=tb, b=b: e.dma_start(out=odt[b][:], in_=od[tb * 128:(tb + 1) * 128]), r_odt[b], writes=[r_odt[b]])
        P.dma(SP, lambda e, tb=tb, b=b: e.dma_start(out=szt[b][:], in_=szd[tb * 128:(tb + 1) * 128, :]), r_szt[b], writes=[r_szt[b]])
        P.op(DVE, lambda e, b=b: e.tensor_tensor(out=acc[:], in0=odt[b][:, 0], in1=odt[b][:, 1], op=ALU.add), reads=[r_odt[b]], writes=[r_acc])
        P.op(DVE, lambda e, b=b: e.tensor_tensor(out=acc[:], in0=acc[:], in1=odt[b][:, 2], op=ALU.add), reads=[r_odt[b], r_acc], writes=[r_acc])
        P.op(DVE, lambda e: e.reciprocal(rec[:], acc[:, :, 128]), reads=[r_acc], writes=[r_rec])
        for hh in range(4):
            P.op(DVE, lambda e, hh=hh, b=b: e.scalar_tensor_tensor(out=yd[:, hh * 128:(hh + 1) * 128], in0=acc[:, hh, 0:128], scalar=rec[:, hh:hh + 1],
                                                                   in1=szt[b][:, hh * 128:(hh + 1) * 128], op0=ALU.mult, op1=ALU.mult),
                 reads=[r_acc, r_rec, r_szt[b]], writes=[r_yd])

        def tr(e):
            last = None
            for hh in range(4):
                last = e.transpose(ptD[:, hh, :], yd[:, hh * 128:(hh + 1) * 128], ident[:])
            return last
        P.op(PE, tr, reads=[r_yd, r_id], writes=[r_ptD])
        P.op(ACT, lambda e, tb=tb: e.copy(yT[:, 24:28, tb * 128:(tb + 1) * 128], ptD[:]), reads=[r_ptD], writes=[r_yT])

    wgb = [P.sbuf("wgb%d%s" % (i, tag), [128, 32, 128], BF16) for i in range(3)]
    r_wgb = [Res("wgb%d" % i) for i in range(3)]
    wbb = [P.sbuf("wbb%d%s" % (i, tag), [128, 8, 128], BF16) for i in range(3)]
    r_wbb = [Res("wbb%d" % i) for i in range(3)]
    gp = [P.psum("gp%d%s" % (i, tag), [128, 512], F32) for i in range(2)]
    r_gp = [Res("gp0"), Res("gp1")]
    pp = [P.psum("pp%d%s" % (i, tag), [128, 512], F32) for i in range(2)]
    r_pp = [Res("pp0"), Res("pp1")]
    sg = [P.sbuf("sg%d%s" % (i, tag), [128, 512], F32) for i in range(2)]
    r_sg = [Res("sg0"), Res("sg1")]
    macc = [P.sbuf("macc%d%s" % (i, tag), [128, 512], F32) for i in range(2)]
    r_macc = [Res("macc0"), Res("macc1")]
    mst = [P.sbuf("mst%d%s" % (i, tag), [128, SLAB], BF16) for i in range(2)]
    r_mst = [Res("mst0"), Res("mst1")]
    wgv = wg.rearrange("(k p) n -> p k n", p=128)
    if isinstance(wbr, (list, tuple)):
        wbl = [w_.rearrange("(k p) n -> p k n", p=128) for w_ in wbr]
        wb_src = lambda br, cc: wbl[br][:, 0:BR_NK[br], cc * 128:(cc + 1) * 128]
    else:
        wbv = wbr.rearrange("(k p) n -> p k n", p=128)
        wb_src = lambda br, cc: wbv[:, BR_U0[br]:BR_U0[br] + BR_NK[br], cc * 128:(cc + 1) * 128]
    jobs = [(cc, br) for cc in range(32) for br in range(4)]

    def load(ji):
        cc, br = jobs[ji]
        b = ji % 3
        P.dma(POOL, lambda e, cc=cc, br=br, b=b: [e.dma_start(out=wgb[b][:, 16 * i:16 * i + 16, :],
                                                              in_=wgv[:, 16 * i:16 * i + 16, br * D_MODEL + cc * 128:br * D_MODEL + (cc + 1) * 128])
                                                  for i in range(2)], r_wgb[b], writes=[r_wgb[b]], nd=2)
        P.dma(POOL, lambda e, cc=cc, br=br, b=b: e.dma_start(out=wbb[b][:, 0:BR_NK[br], :],
                                                             in_=wb_src(br, cc)),
              r_wbb[b], writes=[r_wbb[b]])
    load(0)
    load(1)
    cnt = 0
    for ji, (cc, br) in enumerate(jobs):
        if ji + 2 < len(jobs):
            load(ji + 2)
        b = ji % 3
        ms = cc % 2
        for tg in range(2):
            pb = cnt % 2
            cnt += 1

            def gm(e, b=b, tg=tg, pb=pb):
                last = None
                for k in range(32):
                    last = e.matmul(gp[pb][:], wgb[b][:, k, :], xnT[:, k, tg * 512:(tg + 1) * 512], start=(k == 0), stop=(k == 31))
                return last
            P.op(PE, gm, reads=[r_wgb[b], r_xnT], writes=[r_gp[pb]])

            def pm(e, b=b, tg=tg, pb=pb, br=br):
                last = None
                nk = BR_NK[br]
                for k in range(nk):
                    last = e.matmul(pp[pb][:], wbb[b][:, k, :], yT[:, BR_U0[br] + k, tg * 512:(tg + 1) * 512], start=(k == 0), stop=(k == nk - 1))
                return last
            P.op(PE, pm, reads=[r_wbb[b], r_yT], writes=[r_pp[pb]])
            P.op(ACT, lambda e, pb=pb: e.activation(out=sg[pb][:], in_=gp[pb][:], func=AF.Sigmoid), reads=[r_gp[pb]], writes=[r_sg[pb]])
            if br == 0:
                P.op(DVE, lambda e, pb=pb, tg=tg: e.tensor_tensor(out=macc[tg][:], in0=sg[pb][:], in1=pp[pb][:], op=ALU.mult),
                     reads=[r_sg[pb], r_pp[pb]], writes=[r_macc[tg]])
            else:
                P.op(DVE, lambda e, pb=pb, tg=tg: e.tensor_tensor(out=sg[pb][:], in0=sg[pb][:], in1=pp[pb][:], op=ALU.mult),
                     reads=[r_sg[pb], r_pp[pb]], writes=[r_sg[pb]])
                if br < 3:
                    P.op(DVE, lambda e, pb=pb, tg=tg: e.tensor_tensor(out=macc[tg][:], in0=macc[tg][:], in1=sg[pb][:], op=ALU.add),
                         reads=[r_sg[pb], r_macc[tg]], writes=[r_macc[tg]])
                else:
                    P.op(DVE, lambda e, pb=pb, tg=tg, ms=ms: e.tensor_tensor(out=mst[ms][:, tg * 512:(tg + 1) * 512], in0=macc[tg][:], in1=sg[pb][:], op=ALU.add),
                         reads=[r_sg[pb], r_macc[tg]], writes=[r_mst[ms]])
        if br == 3:
            P.dma(SP, lambda e, cc=cc, ms=ms: e.dma_start(out=mT_o[cc], in_=mst[ms][:]), r_mst[ms], reads=[r_mst[ms]])
    P.pop_scope()


def build_l4():
    nc = bass.Bass("TRN2", target_bir_lowering=False)
    dt = nc.dram_tensor
    mT_i = dt("mT", [32, 128, SLAB], BF16, kind="ExternalInput").ap()
    wo = dt("wo", [D_MODEL, D_MODEL], F32, kind="ExternalInput").ap()
    x = dt("x", [SLAB, D_MODEL], F32, kind="ExternalInput").ap()
    out = dt("out", [SLAB, D_MODEL], F32, kind="ExternalOutput").ap()
    P = Prog(nc)
    emit_l4(P, 0, mT_i, wo, x, out)
    P.emit()
    return nc


def emit_l4(P, layer, mT_i, wo, x, out):
    tag = "_o%d" % layer
    P.push_scope()
    mT = P.sbuf("mT" + tag, [128, 32, SLAB], BF16)
    r_mT = Res("mT")
    P.dma(SP, lambda e: [e.dma_start(out=mT[:, 8 * i:8 * i + 8, :], in_=mT_i[8 * i:8 * i + 8].rearrange("k p t -> p k t"))
                         for i in range(4)], r_mT, writes=[r_mT], nd=4)
    wb = [P.sbuf("wob%d%s" % (i, tag), [128, 32, 512], BF16) for i in range(2)]
    r_wb = [Res("wob0"), Res("wob1")]
    xb = [P.sbuf("xob%d%s" % (i, tag), [128, 512], F32) for i in range(2)]
    r_xb = [Res("xob0"), Res("xob1")]
    ob = [P.sbuf("oob%d%s" % (i, tag), [128, 512], F32) for i in range(2)]
    r_ob = [Res("oob0"), Res("oob1")]
    ps = [P.psum("pso%d%s" % (i, tag), [128, 512], F32) for i in range(2)]
    r_ps = [Res("pso0"), Res("pso1")]
    wv = wo.rearrange("(k p) n -> p k n", p=128)

    def load_w(cg):
        b = cg % 2
        P.dma(POOL, lambda e, cg=cg, b=b: [e.dma_start(out=wb[b][:, 8 * i:8 * i + 8, :], in_=wv[:, 8 * i:8 * i + 8, cg * 512:(cg + 1) * 512])
                                           for i in range(4)], r_wb[b], writes=[r_wb[b]], nd=4)
    load_w(0)
    cnt = 0
    for cg in range(8):
        if cg + 1 < 8:
            load_w(cg + 1)
        b = cg % 2
        for tb in range(NTB):
            pb = cnt % 2
            cnt += 1
            P.dma(SP, lambda e, tb=tb, cg=cg, pb=pb: e.dma_start(out=xb[pb][:], in_=x[tb * 128:(tb + 1) * 128, cg * 512:(cg + 1) * 512]),
                  r_xb[pb], writes=[r_xb[pb]])

            def mm(e, tb=tb, b=b, pb=pb):
                last = None
                for k in range(32):
                    last = e.matmul(ps[pb][:], mT[:, k, tb * 128:(tb + 1) * 128], wb[b][:, k, :], start=(k == 0), stop=(k == 31))
                return last
            P.op(PE, mm, reads=[r_mT, r_wb[b]], writes=[r_ps[pb]])
            P.op(DVE, lambda e, pb=pb: e.tensor_tensor(out=ob[pb][:], in0=ps[pb][:], in1=xb[pb][:], op=ALU.add),
                 reads=[r_ps[pb], r_xb[pb]], writes=[r_ob[pb]])
            P.dma(SP, lambda e, tb=tb, cg=cg, pb=pb: e.dma_start(out=out[tb * 128:(tb + 1) * 128, cg * 512:(cg + 1) * 512], in_=ob[pb][:]),
                  r_ob[pb], reads=[r_ob[pb]])
    P.pop_scope()


def rope_table():
    t = np.arange(SEQ)
    row = (t // 64).astype(np.float32)
    col = (t % 64).astype(np.float32)
    inv = (np.float32(10000.0) ** (-np.arange(32, dtype=np.float32) / np.float32(32))).astype(np.float32)
    ang = np.concatenate([row[:, None] * inv, col[:, None] * inv], -1).astype(np.float32)
    return np.concatenate([np.cos(ang), np.sin(ang)], -1).astype(np.float32)


def build_abias(rel_bias, qt):
    out = np.full((8, 128, 5, 7, 128), NEG, np.float32)
    rep_qb = [0, 1, 2, 6, 7]
    p = np.arange(128)
    for s, qb in enumerate(rep_qb):
        Q = qt * 8 + qb
        tq = Q * 128 + np.arange(128)
        r = tq // 64
        c = tq % 64
        r0 = np.clip(r - 4, 0, 56)
        c0 = np.clip(c - 8, 0, 48)
        for i in range(7):
            tk = (Q - 3 + i) * 128 + p
            if tk[0] < 0 or tk[0] >= SEQ:
                continue
            kr = tk // 64
            kc = tk % 64
            valid = ((kr[:, None] >= r0[None, :]) & (kr[:, None] < r0[None, :] + 8)
                     & (kc[:, None] >= c0[None, :]) & (kc[:, None] < c0[None, :] + 16))
            ro = np.clip(kr[:, None] - r[None, :] + 7, 0, 14)
            co = np.clip(kc[:, None] - c[None, :] + 15, 0, 30)
            vals = rel_bias[:, ro, co]
            out[:, :, s, i, :] = np.where(valid[None], vals, np.float32(NEG))
    return out


def build_cstrip(qt):
    w = np.arange(4992, dtype=np.float32)[None, :]
    ki = np.arange(128, dtype=np.float32)[:, None]
    dist = np.abs(w - 3968 + qt * 1024 - ki)
    slopes = np.asarray([2.0 ** (-8.0 * (h + 1) / 4) for h in range(4)], np.float32)
    return (-slopes[:, None, None] * dist[None]).astype(np.float32)


def build_dbias():
    out = np.full((128, 12, 2, 128), NEG, np.float32)
    p = np.arange(128)[:, None]
    qi = np.arange(128)[None, :]
    for g, (win, dil) in enumerate(D_PAT):
        for hh in range(4):
            hd = g * 4 + hh
            slope = np.float32(2.0 ** (-8.0 * (hd + 1) / 12))
            for j in range(2):
                delta = j * 128 + p - 64 - qi
                pen = -(slope * np.abs(delta * dil).astype(np.float32))
                out[:, hd, j, :] = np.where(np.abs(delta) <= 64, pen, np.float32(NEG))
    return out


def build_dval(qt):
    out = np.zeros((128, 64), np.float32)
    col = 0
    for (g, r, ub, qs, q0, blocks) in d_tiles():
        dil = D_PAT[g][1]
        cl3 = 3072 // dil
        for (k0, nk) in blocks:
            k = k0 + np.arange(128)
            u3 = k % cl3
            rr = k // cl3
            tok = (qt - 1) * 1024 + dil * u3 + rr
            ok = (tok >= 0) & (tok < SEQ) & (np.arange(128) < nk)
            out[:, col] = np.where(ok, 0.0, NEG)
            col += 1
    return out


def class_major_cols(a, dil):
    T = a.shape[-1]
    return np.ascontiguousarray(a.reshape(a.shape[:-1] + (T // dil, dil)).swapaxes(-1, -2).reshape(a.shape))


def class_major_rows(a, dil):
    T = a.shape[0]
    return np.ascontiguousarray(a.reshape((T // dil, dil) + a.shape[1:]).swapaxes(0, 1).reshape(a.shape))


def window3(a, qt, axis):
    shp = list(a.shape)
    shp[axis] = 3072
    out = np.zeros(shp, a.dtype)
    lo = (qt - 1) * 1024
    s0 = max(lo, 0)
    s1 = min(lo + 3072, SEQ)
    src = [slice(None)] * a.ndim
    dst = [slice(None)] * a.ndim
    src[axis] = slice(s0, s1)
    dst[axis] = slice(s0 - lo, s1 - lo)
    out[tuple(dst)] = a[tuple(src)]
    return out


def with_ones(v):
    T, H, W = v.shape
    out = np.ones((H, T, W + 1), v.dtype)
    out[:, :, :W] = v.transpose(1, 0, 2)
    return out


def p_layout(a):
    H, T, W = a.shape
    return np.ascontiguousarray(a.reshape(H, T // 128, 128, W).transpose(0, 2, 1, 3))


def prep_l2(r1, params, l):
    ins = []
    dbias = build_dbias()
    for b in range(2):
        kT = np.concatenate([np.asarray(r1[b * 4 + q]["kT_o"]) for q in range(4)], axis=2)
        v = np.concatenate([np.asarray(r1[b * 4 + q]["v_o"]) for q in range(4)], axis=0)
        vEb = p_layout(with_ones(v[:, 1024:1280].reshape(SEQ, 2, 128)))
        vEc = p_layout(with_ones(v[:, 1280:2304].reshape(SEQ, 4, 256)))
        for qt in range(4):
            c = b * 4 + qt
            kTa = window3(kT[0:8], qt, 2)
            vEa = p_layout(with_ones(window3(v[:, 0:1024], qt, 0).reshape(3072, 8, 128)))
            kTd_w = window3(kT[18:30], qt, 2)
            vd_w = with_ones(window3(v[:, 2304:3840], qt, 0).reshape(3072, 12, 128))
            kTd = np.empty_like(kTd_w)
            vEd = np.empty_like(vd_w)
            qT = np.array(np.asarray(r1[c]["qT_o"]))
            for g, (win, dil) in enumerate(D_PAT):
                for hh in range(4):
                    hd = g * 4 + hh
                    kTd[hd] = class_major_cols(kTd_w[hd], dil)
                    vEd[hd] = class_major_rows(vd_w[hd], dil)
                    qT[24 + hd] = class_major_cols(qT[24 + hd], dil)
            ins.append({
                "qT": qT, "kTa": kTa, "vEa": vEa, "kTb": np.ascontiguousarray(kT[8:10]), "vEb": vEb,
                "kTc": np.ascontiguousarray(kT[10:18]), "vEc": vEc, "kTd": kTd, "vEd": vEd,
                "sz": np.asarray(r1[c]["z_o"]),
                "abias": build_abias(params["na_rel_bias"][l], qt), "cstrip": build_cstrip(qt), "dbias": dbias,
                "dval": build_dval(qt), "qkg": params["qk_gain"][l].reshape(1, 1024).copy(),
                "relb": params["na_rel_bias"][l].reshape(1, 3720).copy(),
                "dlam": params["diff_lambda"][l].reshape(1, 512).copy(),
                "subg": params["diff_subln_g"][l].reshape(1, 256).copy(),
            })
    return ins


def od_natural(od_o):
    out = np.empty((SLAB, 3, 4, OD_W), np.float32)
    for g, (win, dil) in enumerate(D_PAT):
        a = np.asarray(od_o[g])
        out[:, g] = a.reshape(dil, SLAB // dil, 4, OD_W).swapaxes(0, 1).reshape(SLAB, 4, OD_W)
    return out


_PROGS = {}


def _prog(name, fn):
    if name not in _PROGS:
        _PROGS[name] = fn()
    return _PROGS[name]


def run_layer(xs, params, l, cs):
    cores = list(range(8))
    w_in = params["w_in"][l]
    wq = np.ascontiguousarray(w_in[:, :NQKVZ])
    ng = params["norm_g"][l][None, :].copy()
    qkg = params["qk_gain"][l].reshape(1, 1024).copy()
    in1 = [{"x": xs[c], "ng": ng, "w": wq, "qkg": qkg, "cs": np.ascontiguousarray(cs[(c % 4) * 1024:(c % 4 + 1) * 1024])} for c in cores]
    r1 = run_bass_kernel_spmd(_prog("l1", build_l1), in1, core_ids=cores).results
    del wq, in1
    in2 = prep_l2(r1, params, l)
    r2 = run_bass_kernel_spmd(_prog("l2_%d" % l, lambda: build_l2(l)), in2, core_ids=cores).results
    del in2
    wg = np.ascontiguousarray(w_in[:, NQKVZ:])
    wbr = np.concatenate([params["w_branch_a"][l], params["w_branch_b"][l], params["w_branch_c"][l], params["w_branch_d"][l]], axis=0)
    in3 = [{"xnT": np.asarray(r1[c]["xnT_o"]), "yT": np.asarray(r2[c]["yT_o"]), "od": od_natural(r2[c]["od_o"]),
            "szd": np.ascontiguousarray(np.asarray(r1[c]["z_o"])[:, 3072:3584]), "wg": wg, "wbr": wbr} for c in cores]
    r3 = run_bass_kernel_spmd(_prog("l3", build_l3), in3, core_ids=cores).results
    del wg, in3, r1, r2
    wo = np.ascontiguousarray(params["w_out"][l])
    in4 = [{"mT": np.asarray(r3[c]["mT_o"]), "wo": wo, "x": xs[c]} for c in cores]
    r4 = run_bass_kernel_spmd(_prog("l4", build_l4), in4, core_ids=cores).results
    return [np.asarray(r4[c]["out"]) for c in cores]


def kernel_unfused(x, norm_g, w_in, qk_gain, na_rel_bias, diff_lambda, diff_subln_g,
           w_branch_a, w_branch_b, w_branch_c, w_branch_d, w_out):
    params = dict(norm_g=np.asarray(norm_g), w_in=np.asarray(w_in), qk_gain=np.asarray(qk_gain),
                  na_rel_bias=np.asarray(na_rel_bias), diff_lambda=np.asarray(diff_lambda),
                  diff_subln_g=np.asarray(diff_subln_g), w_branch_a=np.asarray(w_branch_a),
                  w_branch_b=np.asarray(w_branch_b), w_branch_c=np.asarray(w_branch_c),
                  w_branch_d=np.asarray(w_branch_d), w_out=np.asarray(w_out))
    x = np.asarray(x, dtype=np.float32)
    cs = rope_table()
    xs = [np.ascontiguousarray(x[c // 4, (c % 4) * 1024:(c % 4 + 1) * 1024]) for c in range(8)]
    for l in range(2):
        xs = run_layer(xs, params, l, cs)
    out = np.empty((2, SEQ, D_MODEL), np.float32)
    for c in range(8):
        out[c // 4, (c % 4) * 1024:(c % 4 + 1) * 1024] = xs[c]
    return out


def build_fused(nl=2, stop=None, dbg_groups=None, mixers="ABCD"):
    nc = bass.Bass("TRN2", target_bir_lowering=False)
    dt = nc.dram_tensor
    I32 = mybir.dt.int32
    I = lambda n, s, d=F32: dt(n, s, d, kind="ExternalInput").ap()
    x = I("x", [SLAB, D_MODEL])
    ng = I("ng", [nl, D_MODEL])
    w_in = I("w_in", [nl, D_MODEL, (N_IN if stop not in ("l1", "cc", "l2") else NQKVZ) if dbg_groups is None else 512 * dbg_groups])
    qkg = I("qkg", [nl, 1024])
    cs = I("cs", [SLAB, 128])
    relb = I("relb", [nl, 3720])
    dlam = I("dlam", [nl, 512])
    subg = I("subg", [nl, 256])
    if stop not in ("l1", "cc", "l2"):
        wba = I("wba", [nl, 1024, D_MODEL])
        wbb = I("wbb", [nl, 1024, D_MODEL])
        wbc = I("wbc", [nl, 1024, D_MODEL])
        wbd = I("wbd", [nl, 512, D_MODEL])
    if stop not in ("l1", "cc", "l2", "l3"):
        wo = I("wo", [nl, D_MODEL, D_MODEL])
    abias = I("abias", [nl, 8, 128, 5, 7, 128])
    cstrip = I("cstrip", [4, 128, 4992])
    dbias = I("dbias", [128, 12, 2, 128])
    dval = I("dval", [128, 64])
    idxK = I("idxK", [128, 90], I32)
    idxVA = I("idxVA", [128, 192], I32)
    idxVD = I("idxVD", [128, 768], I32)
    out = dt("out", [SLAB, D_MODEL], F32, kind="ExternalOutput").ap()
    T = lambda n, s, d: dt(n, s, d).ap()
    xnT_d = T("xnT_d", [32, 128, SLAB], BF16)
    qT_d = T("qT_d", [NQH, 128, SLAB], BF16)
    kT_loc = T("kT_loc", [NKH * 128, SLAB], BF16)
    v_loc = T("v_loc", [SLAB, NVU * 128], BF16)
    G_k = T("G_k", [4 * NKH * 128, SLAB], BF16)
    G_v = T("G_v", [SEQ, NVU * 128], BF16)
    sz_d = T("sz_d", [SLAB, NZU * 128], F32)
    yT_d = T("yT_d", [24, 128, SLAB], BF16)
    od_d = T("od_d", [SLAB, 3, 4, OD_W], F32)
    mT_d = T("mT_d", [32, 128, SLAB], BF16)
    x1_d = T("x1_d", [SLAB, D_MODEL], F32)
    P = Prog(nc)
    P.reg_values = [15359, 4095, SEQ * NVU - 1]
    groups = [[0, 1, 2, 3], [4, 5, 6, 7]]
    for l in range(nl):
        r_G = Res("G%d" % l)
        r_ccK = Res("ccK%d" % l)
        r_ccV = Res("ccV%d" % l)
        xin = x if l == 0 else x1_d
        xout = x1_d if l < nl - 1 else out
        emit_l1(P, xin, ng[l:l + 1, :], w_in[l], qkg[l:l + 1, :], cs, xnT_d, qT_d,
                kT_loc.rearrange("(h p) t -> h p t", p=128), v_loc, sz_d, l, dbg_groups=dbg_groups)
        if stop == "l1":
            break
        r_parts = []
        for c_ in range(10):
            rp = Res("gk%d_%d" % (l, c_))
            r_parts.append(rp)
            P.dma(POOL, lambda e, c_=c_: e.collective_compute("AllGather", ALU.bypass, replica_groups=groups,
                                                              ins=[kT_loc[c_ * 384:(c_ + 1) * 384, :].opt()],
                                                              outs=[G_k[c_ * 1536:(c_ + 1) * 1536, :].opt()]), r_ccK, writes=[rp], inc=1)
        for tb_ in range(8):
            rp = Res("gv%d_%d" % (l, tb_))
            r_parts.append(rp)
            P.dma(POOL, lambda e, tb_=tb_: e.collective_compute("AllGather", ALU.bypass, replica_groups=groups,
                                                                ins=[v_loc[tb_ * 128:(tb_ + 1) * 128, :].opt()],
                                                                outs=[G_v[tb_ * 512:(tb_ + 1) * 512, :].opt()]), r_ccV, writes=[rp], inc=1)
        P.op(POOL, lambda e: e.memset(P._bar_t[:, 6:7], 0.0), reads=r_parts, writes=[r_G])
        if stop == "cc":
            break
        fz = dict(G_k4=G_k.rearrange("(c r h p) t -> c r h p t", c=10, r=4, h=3, p=128), G_kr=G_k, G_v=G_v, r_G=r_G,
                  idxK=idxK, idxVA=idxVA, idxVD=idxVD)
        emit_l2(P, l, qT_d, None, None, None, None, None, None, None, None, sz_d, abias[l], cstrip, dbias, dval,
                qkg[l:l + 1, :], relb[l:l + 1, :], dlam[l:l + 1, :], subg[l:l + 1, :], yT_d, od_d, fz=fz, mixers=mixers)
        if stop == "l2":
            break
        emit_l3(P, l, xnT_d, yT_d, od_d, sz_d[:, 3072:3584], w_in[l][:, NQKVZ:], [wba[l], wbb[l], wbc[l], wbd[l]], mT_d)
        if stop == "l3":
            break
        emit_l4(P, l, mT_d, wo[l], xin, xout)
    P.emit()
    return nc


def build_idx(qt):
    OOB = 0
    p = np.arange(128)
    vrow = lambda tok: (((tok % 1024) // 128) * 4 + tok // 1024) * 128 + tok % 128
    idxK = np.full((128, 90), OOB, np.int32)
    for kidx in range(30):
        for d_ in range(3):
            r = qt + d_ - 1
            if 0 <= r <= 3:
                idxK[:, kidx * 3 + d_] = (((kidx // 3) * 4 + r) * 3 + kidx % 3) * 128 + p
    idxVA = np.full((128, 192), OOB, np.int32)
    for kb in range(24):
        tok = (qt - 1) * 1024 + kb * 128 + p
        for h_ in range(8):
            idxVA[:, kb * 8 + h_] = np.where((tok >= 0) & (tok < SEQ), vrow(tok) * NVU + h_, OOB)
    idxVD = np.full((128, 768), OOB, np.int32)
    col = 0
    for (g, r, ub, qs, q0, blocks) in d_tiles():
        dil = D_PAT[g][1]
        cl3 = 3072 // dil
        for (k0, nk) in blocks:
            k = k0 + p
            tok = (qt - 1) * 1024 + dil * (k % cl3) + (k // cl3)
            ok = (tok >= 0) & (tok < SEQ) & (p < nk)
            for hd in range(12):
                idxVD[:, hd * 64 + col] = np.where(ok, vrow(tok) * NVU + 18 + hd, OOB)
            col += 1
    return idxK, idxVA, idxVD


def kernel(x, norm_g, w_in, qk_gain, na_rel_bias, diff_lambda, diff_subln_g,
           w_branch_a, w_branch_b, w_branch_c, w_branch_d, w_out):
    f32 = lambda a: np.ascontiguousarray(np.asarray(a), dtype=np.float32)
    x = f32(x)
    shared = {
        "ng": f32(norm_g), "w_in": f32(w_in), "qkg": f32(qk_gain).reshape(2, 1024), "relb": f32(na_rel_bias).reshape(2, 3720),
        "dlam": f32(diff_lambda).reshape(2, 512), "subg": f32(diff_subln_g).reshape(2, 256),
        "wba": f32(w_branch_a), "wbb": f32(w_branch_b), "wbc": f32(w_branch_c), "wbd": f32(w_branch_d), "wo": f32(w_out),
        "dbias": build_dbias(),
    }
    cs = rope_table()
    rel = f32(na_rel_bias)
    per_qt = []
    for qt in range(4):
        idxK, idxVA, idxVD = build_idx(qt)
        per_qt.append({"cs": np.ascontiguousarray(cs[qt * 1024:(qt + 1) * 1024]),
                       "abias": np.stack([build_abias(rel[l], qt) for l in range(2)]),
                       "cstrip": build_cstrip(qt), "dval": build_dval(qt), "idxK": idxK, "idxVA": idxVA, "idxVD": idxVD})
    ins = []
    for c in range(8):
        d = {"x": np.ascontiguousarray(x[c // 4, (c % 4) * 1024:(c % 4 + 1) * 1024])}
        d.update(shared)
        d.update(per_qt[c % 4])
        ins.append(d)
    res = run_bass_kernel_spmd(_prog("fused", build_fused), ins, core_ids=list(range(8))).results
    out = np.empty((2, SEQ, D_MODEL), np.float32)
    for c in range(8):
        out[c // 4, (c % 4) * 1024:(c % 4 + 1) * 1024] = np.asarray(res[c]["out"])
    return out


def build_l2f_test(mixers):
    nc = bass.Bass("TRN2", target_bir_lowering=False)
    dt = nc.dram_tensor
    I32 = mybir.dt.int32
    I = lambda n, s, d=F32: dt(n, s, d, kind="ExternalInput").ap()
    qT_d = I("qT", [NQH, 128, SLAB], BF16)
    G_k = I("G_k", [4 * NKH * 128, SLAB], BF16)
    G_v = I("G_v", [SEQ, NVU * 128], BF16)
    sz_d = I("sz", [SLAB, NZU * 128])
    qkg = I("qkg", [1, 1024]); relb = I("relb", [1, 3720]); dlam = I("dlam", [1, 512]); subg = I("subg", [1, 256])
    abias = I("abias", [8, 128, 5, 7, 128]); cstrip = I("cstrip", [4, 128, 4992]); dbias = I("dbias", [128, 12, 2, 128]); dval = I("dval", [128, 64])
    idxK = I("idxK", [128, 90], I32); idxVA = I("idxVA", [128, 192], I32); idxVD = I("idxVD", [128, 768], I32)
    yT_d = dt("yT_o", [24, 128, SLAB], BF16, kind="ExternalOutput").ap()
    od_d = dt("od_o", [SLAB, 3, 4, OD_W], F32, kind="ExternalOutput").ap()
    P = Prog(nc)
    P.reg_values = [15359, 4095, SEQ * NVU - 1]
    fz = dict(G_k4=G_k.rearrange("(c r h p) t -> c r h p t", c=10, r=4, h=3, p=128), G_kr=G_k, G_v=G_v, r_G=Res("G"),
              idxK=idxK, idxVA=idxVA, idxVD=idxVD)
    emit_l2(P, 0, qT_d, None, None, None, None, None, None, None, None, sz_d, abias, cstrip, dbias, dval, qkg, relb, dlam, subg,
            yT_d, od_d, fz=fz, mixers=mixers)
    P.emit()
    return nc


def build_l2f_test2(mixers, with_l34=False):
    nc = bass.Bass("TRN2", target_bir_lowering=False)
    dt = nc.dram_tensor
    I32 = mybir.dt.int32
    I = lambda n, s, d=F32: dt(n, s, d, kind="ExternalInput").ap()
    qT_d = I("qT", [NQH, 128, SLAB], BF16)
    kT_e = I("kT_e", [NKH * 128, SLAB], BF16)
    v_e = I("v_e", [SLAB, NVU * 128], BF16)
    sz_d = I("sz", [SLAB, NZU * 128])
    qkg = I("qkg", [1, 1024]); relb = I("relb", [1, 3720]); dlam = I("dlam", [1, 512]); subg = I("subg", [1, 256])
    abias = I("abias", [8, 128, 5, 7, 128]); cstrip = I("cstrip", [4, 128, 4992]); dbias = I("dbias", [128, 12, 2, 128]); dval = I("dval", [128, 64])
    idxK = I("idxK", [128, 90], I32); idxVA = I("idxVA", [128, 192], I32); idxVD = I("idxVD", [128, 768], I32)
    yT_d = dt("yT_o", [24, 128, SLAB], BF16, kind="ExternalOutput").ap()
    od_d = dt("od_o", [SLAB, 3, 4, OD_W], F32, kind="ExternalOutput").ap()
    kT_loc = dt("kT_loc", [NKH * 128, SLAB], BF16).ap()
    v_loc = dt("v_loc", [SLAB, NVU * 128], BF16).ap()
    G_k = dt("G_k", [4 * NKH * 128, SLAB], BF16).ap()
    G_v = dt("G_v", [SEQ, NVU * 128], BF16).ap()
    P = Prog(nc)
    P.reg_values = [15359, 4095, SEQ * NVU - 1]
    groups = [[0, 1, 2, 3], [4, 5, 6, 7]]
    r_loc = Res("locs")
    r_cp = Res("cp")
    P.push_scope()
    P.dma(SP, lambda e: [e.dma_start(out=kT_loc[i * 384:(i + 1) * 384, :], in_=kT_e[i * 384:(i + 1) * 384, :]) for i in range(10)] +
          [e.dma_start(out=v_loc[i * 128:(i + 1) * 128, :], in_=v_e[i * 128:(i + 1) * 128, :]) for i in range(8)], r_cp, writes=[r_loc], nd=18)
    P.pop_scope()
    r_G = Res("G"); r_ccK = Res("ccK"); r_ccV = Res("ccV")
    for c_ in range(10):
        P.dma(POOL, lambda e, c_=c_: e.collective_compute("AllGather", ALU.bypass, replica_groups=groups, ins=[kT_loc[c_ * 384:(c_ + 1) * 384, :].opt()],
                                                          outs=[G_k[c_ * 1536:(c_ + 1) * 1536, :].opt()]), r_ccK, writes=[r_G], inc=1)
    for tb_ in range(8):
        P.dma(POOL, lambda e, tb_=tb_: e.collective_compute("AllGather", ALU.bypass, replica_groups=groups, ins=[v_loc[tb_ * 128:(tb_ + 1) * 128, :].opt()],
                                                            outs=[G_v[tb_ * 512:(tb_ + 1) * 512, :].opt()]), r_ccV, writes=[r_G], inc=1)
    fz = dict(G_k4=G_k.rearrange("(c r h p) t -> c r h p t", c=10, r=4, h=3, p=128), G_kr=G_k, G_v=G_v, r_G=r_G,
              idxK=idxK, idxVA=idxVA, idxVD=idxVD)
    emit_l2(P, 0, qT_d, None, None, None, None, None, None, None, None, sz_d, abias, cstrip, dbias, dval, qkg, relb, dlam, subg,
            yT_d, od_d, fz=fz, mixers=mixers)
    P.emit()
    return nc
```

```python
import math
import numpy as np
import ml_dtypes
import concourse.bass as bass
import concourse.mybir as mybir
from concourse.bass_utils import run_bass_kernel_spmd
from contextlib import ExitStack

F32 = mybir.dt.float32
BF16 = mybir.dt.bfloat16
AF = mybir.ActivationFunctionType
ALU = mybir.AluOpType
AX = mybir.AxisListType
NPBF = ml_dtypes.bfloat16

PE, ACT, DVE, POOL, SP = "tensor", "scalar", "vector", "gpsimd", "sync"
ENGINES = (PE, ACT, DVE, POOL, SP)

D_MODEL = 4096
SEQ = 4096
NQKVZ = 15872
N_IN = 32256
SLAB = 1024
NTB = 8
RMS_EPS = 1e-6
NEG = -1e30


class Res:
    __slots__ = ("name", "last_w", "readers", "dma_sem", "dma_cnt", "slot", "phase")

    def __init__(self, name):
        self.name = name
        self.last_w = None
        self.readers = []
        self.dma_sem = None
        self.dma_cnt = 0
        self.slot = -1
        self.phase = -1


class Op:
    __slots__ = ("eng", "fn", "waits", "is_dma", "needs_inc", "sem_res", "tok", "nd", "inc", "seq")

    def __init__(self, eng, fn):
        self.eng = eng
        self.fn = fn
        self.waits = []
        self.is_dma = False
        self.needs_inc = False
        self.sem_res = None
        self.tok = None
        self.nd = 1
        self.inc = 16
        self.seq = 0


class Prog:
    def __init__(self, nc):
        self.nc = nc
        self.q = {e: [] for e in ENGINES}
        self.stack = ExitStack()
        self.dma_res = []
        self.scopes = []
        self.phase = 0
        self.nslot = 0
        self.max_slot = 0
        self.reg_values = []
        self.regs = {}
        self._bar_t = self.stack.enter_context(nc.sbuf_tensor("bar_t", [128, 8], F32))
        self._bar_b = self.stack.enter_context(nc.sbuf_tensor("bar_b", [128, 8], BF16))
        self._bar_ps = self.stack.enter_context(nc.psum_tensor("bar_ps", [128, 8], F32))

    def sbuf(self, name, shape, dt):
        st = self.scopes[-1] if self.scopes else self.stack
        return st.enter_context(self.nc.sbuf_tensor(name, list(shape), dt))

    def psum(self, name, shape, dt=F32):
        st = self.scopes[-1] if self.scopes else self.stack
        return st.enter_context(self.nc.psum_tensor(name, list(shape), dt))

    def push_scope(self):
        self.scopes.append(ExitStack())

    def pop_scope(self):
        self.barrier()
        self.scopes.pop().close()
        if not self.scopes:
            self.phase += 1
            self.nslot = 0

    def _dep(self, op, prod):
        if prod is None or prod is op:
            return
        if prod.eng == PE and op.eng == PE and not prod.is_dma:
            return
        op.waits.append(prod)
        if not prod.is_dma:
            prod.needs_inc = True

    def op(self, eng, fn, reads=(), writes=()):
        o = Op(eng, fn)
        self.nseq = getattr(self, "nseq", 0) + 1
        o.seq = self.nseq
        reads = [r for r in reads if r is not None]
        writes = [r for r in writes if r is not None]
        for r in reads:
            self._dep(o, r.last_w)
        for r in writes:
            self._dep(o, r.last_w)
            for rd in r.readers:
                self._dep(o, rd)
        for r in reads:
            r.readers.append(o)
        for r in writes:
            r.last_w = o
            r.readers = []
        self.q[eng].append(o)
        return o

    def dma(self, eng, fn, sem_res, reads=(), writes=(), nd=1, inc=16):
        o = self.op(eng, fn, reads, writes)
        o.is_dma = True
        o.sem_res = sem_res
        o.nd = nd
        o.inc = inc
        if sem_res.dma_sem is None:
            sem_res.dma_sem = "pending"
            sem_res.slot = self.nslot
            sem_res.phase = self.phase
            self.nslot += 1
            self.max_slot = max(self.max_slot, self.nslot)
            self.dma_res.append(sem_res)
        return o

    def barrier(self):
        t = self._bar_t
        tb_ = self._bar_b
        ps = self._bar_ps
        arr = {}
        arr[DVE] = self.op(DVE, lambda e: e.memset(t[:, 0:1], 0.0))
        arr[ACT] = self.op(ACT, lambda e: e.activation(out=t[:, 1:2], in_=t[:, 4:5], func=AF.Copy))
        arr[POOL] = self.op(POOL, lambda e: e.memset(t[:, 2:3], 0.0))
        arr[PE] = self.op(PE, lambda e: e.matmul(ps[0:1, 0:1], tb_[:, 0:1], tb_[:, 0:1], start=True, stop=True))
        for o in arr.values():
            o.needs_inc = True
        dmas = []
        for e in ENGINES:
            last = {}
            for o in self.q[e]:
                if o.is_dma:
                    last[id(o.sem_res)] = o
            dmas.extend(last.values())
        for e in ENGINES:
            o = Op(e, lambda eng: eng.nop())
            o.waits = list(arr.values()) + dmas
            self.q[e].append(o)

    def emit(self):
        nc = self.nc
        st = self.stack
        esem = {e: st.enter_context(nc.semaphore("es_" + e)) for e in (PE, ACT, DVE, POOL)}
        slot_sems = [st.enter_context(nc.semaphore("ds%d" % i)) for i in range(self.max_slot)]
        local = {}
        all_dma = sorted((o for e in ENGINES for o in self.q[e] if o.is_dma), key=lambda o: o.seq)
        for o in all_dma:
            o.sem_res.dma_cnt += o.nd * o.inc
            local[id(o)] = o.sem_res.dma_cnt
        for e in ENGINES:
            c = 0
            for o in self.q[e]:
                if (not o.is_dma) and o.needs_inc:
                    c += 1
                    o.tok = (e, c)
        nph = self.phase + 1
        tot = [[0] * self.max_slot for _ in range(nph + 1)]
        for r in self.dma_res:
            tot[r.phase][r.slot] += r.dma_cnt
        base = [[0] * self.max_slot for _ in range(nph + 1)]
        for p in range(1, nph + 1):
            for sl in range(self.max_slot):
                base[p][sl] = base[p - 1][sl] + tot[p - 1][sl]
        for r in self.dma_res:
            r.dma_sem = slot_sems[r.slot]
        for e in ENGINES:
            for o in self.q[e]:
                if o.is_dma:
                    r = o.sem_res
                    o.tok = (r.slot, base[r.phase][r.slot] + local[id(o)])
        final = [(slot_sems[sl], base[nph][sl]) for sl in range(self.max_slot)]
        self.stats = dict(slots=self.max_slot, max_dma_val=max(base[nph]) if self.max_slot else 0,
                          eng={e: sum(1 for o in self.q[e] if (not o.is_dma) and o.needs_inc) for e in ENGINES})
        global LAST_STATS
        LAST_STATS = self.stats

        def run(e):
            def body(eng):
                seen = {}
                if e == POOL:
                    for v_ in self.reg_values:
                        reg = eng.alloc_register("c%d" % v_)
                        eng.reg_mov(reg, v_)
                        self.regs[v_] = reg
                for o in self.q[e]:
                    for p in o.waits:
                        key, val = p.tok
                        if seen.get(key, 0) >= val:
                            continue
                        seen[key] = val
                        sem = esem[key] if isinstance(key, str) else slot_sems[key]
                        eng.wait_ge(sem, val)
                    ins = o.fn(eng)
                    if o.is_dma:
                        if isinstance(ins, (list, tuple)):
                            assert len(ins) == o.nd
                            for i_ in ins:
                                i_.then_inc(o.sem_res.dma_sem, o.inc)
                        else:
                            assert o.nd == 1
                            ins.then_inc(o.sem_res.dma_sem, o.inc)
                    elif o.needs_inc:
                        ins.then_inc(esem[e], 1)
                if e == SP:
                    for sem, val in final:
                        if val > 0:
                            eng.wait_ge(sem, val)
            return body

        with nc.Block() as block:
            for e in ENGINES:
                getattr(block, e)(run(e))
        st.close()


def make_identity(P, name="ident"):
    idf = P.sbuf(name + "_f", [128, 128], F32)
    ident = P.sbuf(name, [128, 128], BF16)
    r_idf = Res(name + "_f")
    r_id = Res(name)
    P.op(POOL, lambda e: e.memset(idf[:], 0.0), writes=[r_idf])
    P.op(POOL, lambda e: e.affine_select(idf[:], idf[:], [[-1, 128]], ALU.not_equal, 1.0, base=0,
                                         channel_multiplier=1), reads=[r_idf], writes=[r_idf])
    P.op(DVE, lambda e: e.tensor_copy(ident[:], idf[:]), reads=[r_idf], writes=[r_id])
    return ident, r_id


def unit_table():
    U = []
    U += [("q", 0, i) for i in range(8)] + [("k", 0, i) for i in range(8)]
    U += [("v", 0, i) for i in range(8)] + [("z", 0, i) for i in range(8)]
    U += [("q", 1, 8 + i) for i in range(8)] + [("k", 1, 8 + i) for i in range(2)]
    U += [("v", 1, 8 + i) for i in range(2)] + [("z", 1, 8 + i) for i in range(8)]
    U += [("q", 2, 16 + i) for i in range(8)] + [("k", 2, 10 + i) for i in range(8)]
    U += [("v", 2, 10 + i) for i in range(8)] + [("z", 2, 16 + i) for i in range(8)]
    U += [("q", 3, 24 + i) for i in range(12)] + [("k", 3, 18 + i) for i in range(12)]
    U += [("v", 3, 18 + i) for i in range(12)] + [("z", 3, 24 + i) for i in range(4)]
    assert len(U) == 124
    return U


NQH, NKH, NVU, NZU = 36, 30, 30, 28


def build_l1():
    nc = bass.Bass("TRN2", target_bir_lowering=False)
    x = nc.dram_tensor("x", [SLAB, D_MODEL], F32, kind="ExternalInput").ap()
    ng = nc.dram_tensor("ng", [1, D_MODEL], F32, kind="ExternalInput").ap()
    w = nc.dram_tensor("w", [D_MODEL, NQKVZ], F32, kind="ExternalInput").ap()
    qkg = nc.dram_tensor("qkg", [1, 1024], F32, kind="ExternalInput").ap()
    cs = nc.dram_tensor("cs", [SLAB, 128], F32, kind="ExternalInput").ap()
    xnT_o = nc.dram_tensor("xnT_o", [32, 128, SLAB], BF16, kind="ExternalOutput").ap()
    qT_o = nc.dram_tensor("qT_o", [NQH, 128, SLAB], BF16, kind="ExternalOutput").ap()
    kT_o = nc.dram_tensor("kT_o", [NKH, 128, SLAB], BF16, kind="ExternalOutput").ap()
    v_o = nc.dram_tensor("v_o", [SLAB, NVU * 128], BF16, kind="ExternalOutput").ap()
    z_o = nc.dram_tensor("z_o", [SLAB, NZU * 128], F32, kind="ExternalOutput").ap()
    P = Prog(nc)
    emit_l1(P, x, ng, w, qkg, cs, xnT_o, qT_o, kT_o, v_o, z_o, 0)
    P.emit()
    return nc


def emit_norm_transpose(P, x, ng, xnT, r_xnT, ident, r_id, tag):
    P.push_scope()
    gt = P.sbuf("gt" + tag, [128, D_MODEL], F32)
    r_gt = Res("gt")
    xb = [P.sbuf("xb%d%s" % (i, tag), [128, D_MODEL], F32) for i in range(2)]
    r_xb = [Res("xb0"), Res("xb1")]
    xn = [P.sbuf("xn%d%s" % (i, tag), [128, D_MODEL], BF16) for i in range(2)]
    r_xn = [Res("xn0"), Res("xn1")]
    st = P.sbuf("st" + tag, [128, 8], F32)
    r_st = Res("st")
    eps = P.sbuf("eps" + tag, [128, 1], F32)
    r_eps = Res("eps")
    pt = [P.psum("ptA%d%s" % (i, tag), [128, 4, 128], BF16) for i in range(2)]
    r_pt = [Res("ptA0"), Res("ptA1")]
    P.op(DVE, lambda e: e.memset(eps[:], RMS_EPS), writes=[r_eps])
    P.dma(SP, lambda e: e.dma_start(out=gt[:].unsqueeze(1), in_=ng.partition_broadcast(128)), r_gt, writes=[r_gt])
    cnt = 0
    for tb in range(NTB):
        b = tb % 2
        P.dma(SP, lambda e, tb=tb, b=b: e.dma_start(out=xb[b][:], in_=x[tb * 128:(tb + 1) * 128, :]),
              r_xb[b], writes=[r_xb[b]])
        P.op(ACT, lambda e, b=b: e.activation(out=xn[b][:], in_=xb[b][:], func=AF.Square, accum_out=st[:, 0:1]),
             reads=[r_xb[b]], writes=[r_xn[b], r_st])
        P.op(ACT, lambda e: e.activation(out=st[:, 1:2], in_=st[:, 0:1], func=AF.Ln, scale=1.0 / D_MODEL, bias=eps[:]),
             reads=[r_st, r_eps], writes=[r_st])
        P.op(ACT, lambda e: e.activation(out=st[:, 2:3], in_=st[:, 1:2], func=AF.Exp, scale=-0.5),
             reads=[r_st], writes=[r_st])
        P.op(DVE, lambda e, b=b: e.scalar_tensor_tensor(out=xn[b][:], in0=xb[b][:], scalar=st[:, 2:3], in1=gt[:],
                                                        op0=ALU.mult, op1=ALU.mult),
             reads=[r_xb[b], r_st, r_gt], writes=[r_xn[b]])
        for g in range(8):
            pb = cnt % 2
            cnt += 1

            def tr(e, g=g, b=b, pb=pb):
                last = None
                for j in range(4):
                    k = g * 4 + j
                    last = e.transpose(pt[pb][:, j, :], xn[b][:, k * 128:(k + 1) * 128], ident[:])
                return last
            P.op(PE, tr, reads=[r_xn[b], r_id], writes=[r_pt[pb]])
            if pb == 0:
                P.op(DVE, lambda e, g=g, tb=tb, pb=pb: e.tensor_copy(xnT[:, g * 4:g * 4 + 4, tb * 128:(tb + 1) * 128], pt[pb][:]),
                     reads=[r_pt[pb]], writes=[r_xnT])
            else:
                P.op(ACT, lambda e, g=g, tb=tb, pb=pb: e.copy(xnT[:, g * 4:g * 4 + 4, tb * 128:(tb + 1) * 128], pt[pb][:]),
                     reads=[r_pt[pb]], writes=[r_xnT])
    P.pop_scope()


def emit_l1(P, x, ng, w, qkg, cs, xnT_o, qT_o, kT_o, v_o, z_o, layer, dbg_groups=None):
    tag = "_%d" % layer
    U = unit_table()
    scale = 128 ** -0.5
    P.push_scope()
    ident, r_id = make_identity(P, "ident" + tag)
    xnT = P.sbuf("xnT" + tag, [128, 32, SLAB], BF16)
    r_xnT = Res("xnT")
    emit_norm_transpose(P, x, ng, xnT, r_xnT, ident, r_id, tag)
    P.dma(SP, lambda e: [e.dma_start(out=xnT_o[8 * i:8 * i + 8].rearrange("k p t -> p k t"), in_=xnT[:, 8 * i:8 * i + 8, :])
                         for i in range(4)], r_xnT, reads=[r_xnT], nd=4)

    P.push_scope()
    wb = [P.sbuf("wb%d%s" % (i, tag), [128, 32, 512], BF16) for i in range(2)]
    r_wb = [Res("wb0"), Res("wb1")]
    gain = P.sbuf("gain" + tag, [128, 1024], F32)
    r_gain = Res("gain")
    cst = P.sbuf("cst" + tag, [128, NTB, 128], F32)
    r_cst = Res("cst")
    eps = P.sbuf("epsB" + tag, [128, 1], F32)
    r_eps = Res("epsB")
    ps = [P.psum("psB%d%s" % (i, tag), [128, 512], F32) for i in range(2)]
    r_ps = [Res("psB0"), Res("psB1")]
    pt = [P.psum("ptB%d%s" % (i, tag), [128, 4, 128], BF16) for i in range(2)]
    r_pt = [Res("ptB0"), Res("ptB1")]
    sq = [P.sbuf("sq%d%s" % (i, tag), [128, 512], F32) for i in range(2)]
    r_sq = [Res("sq0"), Res("sq1")]
    qn = [P.sbuf("qn%d%s" % (i, tag), [128, 512], F32) for i in range(2)]
    r_qn = [Res("qn0"), Res("qn1")]
    rt = [P.sbuf("rt%d%s" % (i, tag), [128, 4, 64], F32) for i in range(4)]
    r_rt = Res("rt")
    ssq = [P.sbuf("ssq%d%s" % (i, tag), [128, 16], F32) for i in range(2)]
    r_ssq = [Res("ssq0"), Res("ssq1")]
    qb16 = [P.sbuf("qb16%d%s" % (i, tag), [128, 512], BF16) for i in range(2)]
    r_qb16 = [Res("qb160"), Res("qb161")]
    stg = [P.sbuf("stg%d%s" % (i, tag), [128, 4, SLAB], BF16) for i in range(2)]
    r_stg = [Res("stg0"), Res("stg1")]
    vst = [P.sbuf("vst%d%s" % (i, tag), [128, 512], BF16) for i in range(2)]
    r_vst = [Res("vst0"), Res("vst1")]
    zst = [P.sbuf("zst%d%s" % (i, tag), [128, 512], F32) for i in range(2)]
    r_zst = [Res("zst0"), Res("zst1")]

    P.op(DVE, lambda e: e.memset(eps[:], RMS_EPS), writes=[r_eps])
    P.dma(SP, lambda e: e.dma_start(out=gain[:].unsqueeze(1), in_=qkg.partition_broadcast(128)), r_gain, writes=[r_gain])
    P.dma(SP, lambda e: e.dma_start(out=cst[:], in_=cs.rearrange("(t p) c -> p t c", p=128)), r_cst, writes=[r_cst])
    for m in range(4):
        P.op(ACT, lambda e, m=m: e.mul(gain[:, m * 256:m * 256 + 128], gain[:, m * 256:m * 256 + 128], scale),
             reads=[r_gain], writes=[r_gain])

    wv = w.rearrange("(k p) n -> p k n", p=128)
    NG = NQKVZ // 512 if dbg_groups is None else dbg_groups

    def load_w(cg):
        b = cg % 2
        P.dma(POOL, lambda e, cg=cg, b=b: [e.dma_start(out=wb[b][:, 8 * i:8 * i + 8, :],
                                                       in_=wv[:, 8 * i:8 * i + 8, cg * 512:(cg + 1) * 512]) for i in range(4)],
              r_wb[b], writes=[r_wb[b]], nd=4)

    load_w(0)
    tcount = 0
    pending = None
    trc = 0
    for cg in range(NG):
        if cg + 1 < NG:
            load_w(cg + 1)
        b = cg % 2
        units = U[cg * 4:cg * 4 + 4]
        segs = []
        for i, (kind, mixer, idx) in enumerate(units):
            if segs and segs[-1][2] == kind and segs[-1][3] == mixer:
                segs[-1][1] += 1
            else:
                segs.append([i, 1, kind, mixer, idx])
        sb = cg % 2
        has_qk = any(s[2] in ("q", "k") for s in segs)
        for tb in range(NTB):
            pb = tcount % 2
            tcount += 1

            def mm(e, tb=tb, b=b, pb=pb):
                last = None
                for k in range(32):
                    last = e.matmul(ps[pb][:], xnT[:, k, tb * 128:(tb + 1) * 128], wb[b][:, k, :],
                                    start=(k == 0), stop=(k == 31))
                return last
            P.op(PE, mm, reads=[r_xnT, r_wb[b]], writes=[r_ps[pb]])
            if pending is not None:
                pending()
                pending = None
            stage2 = []
            for (o0, n, kind, mixer, idx) in segs:
                sl = slice(o0 * 128, (o0 + n) * 128)
                if kind == "v":
                    P.op(DVE, lambda e, pb=pb, sl=sl: e.tensor_copy(vst[pb][:, sl], ps[pb][:, sl]),
                         reads=[r_ps[pb]], writes=[r_vst[pb]])
                    P.dma(SP, lambda e, pb=pb, sl=sl, tb=tb, idx=idx, n=n: e.dma_start(
                        out=v_o[tb * 128:(tb + 1) * 128, idx * 128:(idx + n) * 128], in_=vst[pb][:, sl]),
                        r_vst[pb], reads=[r_vst[pb]])
                elif kind == "z":
                    P.op(ACT, lambda e, pb=pb, sl=sl: e.activation(out=zst[pb][:, sl], in_=ps[pb][:, sl], func=AF.Silu),
                         reads=[r_ps[pb]], writes=[r_zst[pb]])
                    P.dma(SP, lambda e, pb=pb, sl=sl, tb=tb, idx=idx, n=n: e.dma_start(
                        out=z_o[tb * 128:(tb + 1) * 128, idx * 128:(idx + n) * 128], in_=zst[pb][:, sl]),
                        r_zst[pb], reads=[r_zst[pb]])
                else:
                    gi = mixer * 256 + (0 if kind == "q" else 128)
                    P.op(ACT, lambda e, pb=pb, sl=sl: e.activation(out=sq[pb][:, sl], in_=ps[pb][:, sl], func=AF.Square),
                         reads=[r_ps[pb]], writes=[r_sq[pb]])
                    P.op(DVE, lambda e, pb=pb, sl=sl, o0=o0, n=n: e.reduce_sum(
                        out=ssq[pb][:, o0:o0 + n], in_=sq[pb][:, sl].rearrange("p (n d) -> p n d", d=128), axis=AX.X),
                        reads=[r_sq[pb]], writes=[r_ssq[pb]])
                    P.op(ACT, lambda e, pb=pb, o0=o0, n=n: e.activation(out=ssq[pb][:, 4 + o0:4 + o0 + n], in_=ssq[pb][:, o0:o0 + n],
                                                                      func=AF.Ln, scale=1.0 / 128, bias=eps[:]),
                         reads=[r_ssq[pb], r_eps], writes=[r_ssq[pb]])
                    P.op(ACT, lambda e, pb=pb, o0=o0, n=n: e.activation(out=ssq[pb][:, 8 + o0:8 + o0 + n], in_=ssq[pb][:, 4 + o0:4 + o0 + n],
                                                                      func=AF.Exp, scale=-0.5),
                         reads=[r_ssq[pb]], writes=[r_ssq[pb]])
                    P.op(DVE, lambda e, pb=pb, sl=sl, o0=o0, n=n: e.tensor_tensor(
                        out=qn[pb][:, sl].rearrange("p (n d) -> p n d", d=128),
                        in0=ps[pb][:, sl].rearrange("p (n d) -> p n d", d=128),
                        in1=ssq[pb][:, 8 + o0:8 + o0 + n].unsqueeze(2).to_broadcast([128, n, 128]), op=ALU.mult),
                        reads=[r_ps[pb], r_ssq[pb]], writes=[r_qn[pb]])
                    rope = (mixer == 1)
                    gv = lambda n=n, gi=gi: gain[:, gi:gi + 128].unsqueeze(1).to_broadcast([128, n, 128])
                    if not rope:
                        P.op(DVE, lambda e, pb=pb, sl=sl, n=n, gv=gv: e.tensor_tensor(
                            out=qb16[pb][:, sl].rearrange("p (n d) -> p n d", d=128),
                            in0=qn[pb][:, sl].rearrange("p (n d) -> p n d", d=128), in1=gv(), op=ALU.mult),
                            reads=[r_qn[pb], r_gain], writes=[r_qb16[pb]])
                    else:
                        P.op(DVE, lambda e, pb=pb, sl=sl, n=n, gv=gv: e.tensor_tensor(
                            out=qn[pb][:, sl].rearrange("p (n d) -> p n d", d=128),
                            in0=qn[pb][:, sl].rearrange("p (n d) -> p n d", d=128), in1=gv(), op=ALU.mult),
                            reads=[r_qn[pb], r_gain], writes=[r_qn[pb]])
                        v4 = lambda pb=pb, sl=sl: qn[pb][:, sl].rearrange("p (n i two) -> p n i two", two=2, i=64)
                        o4 = lambda pb=pb, sl=sl: qb16[pb][:, sl].rearrange("p (n i two) -> p n i two", two=2, i=64)
                        cosv = lambda tb=tb, n=n: cst[:, tb, 0:64].unsqueeze(1).to_broadcast([128, n, 64])
                        sinv = lambda tb=tb, n=n: cst[:, tb, 64:128].unsqueeze(1).to_broadcast([128, n, 64])

                        def ropef(e, v4=v4, o4=o4, cosv=cosv, sinv=sinv, n=n):
                            x1 = v4()[:, :, :, 0]
                            x2 = v4()[:, :, :, 1]
                            e.tensor_tensor(out=rt[0][:, 0:n, :], in0=x1, in1=cosv(), op=ALU.mult)
                            e.tensor_tensor(out=rt[1][:, 0:n, :], in0=x2, in1=sinv(), op=ALU.mult)
                            e.tensor_tensor(out=rt[2][:, 0:n, :], in0=x1, in1=sinv(), op=ALU.mult)
                            return e.tensor_tensor(out=rt[3][:, 0:n, :], in0=x2, in1=cosv(), op=ALU.mult)
                        P.op(DVE, ropef, reads=[r_qn[pb], r_cst], writes=[r_rt])

                        def ropeg(e, o4=o4, n=n):
                            e.tensor_tensor(out=o4()[:, :, :, 0], in0=rt[0][:, 0:n, :], in1=rt[1][:, 0:n, :], op=ALU.subtract)
                            return e.tensor_tensor(out=o4()[:, :, :, 1], in0=rt[2][:, 0:n, :], in1=rt[3][:, 0:n, :], op=ALU.add)
                        P.op(DVE, ropeg, reads=[r_rt], writes=[r_qb16[pb], r_rt])
                    stage2.append((o0, n))
            if stage2:
                def s2(pb=pb, tb=tb, sb=sb, stage2=stage2):
                    nonlocal trc
                    for (o0, n) in stage2:
                        tp = trc % 2
                        trc += 1

                        def tr(e, pb=pb, o0=o0, n=n, tp=tp):
                            last = None
                            for j in range(n):
                                last = e.transpose(pt[tp][:, j, :], qb16[pb][:, (o0 + j) * 128:(o0 + j + 1) * 128], ident[:])
                            return last
                        P.op(PE, tr, reads=[r_qb16[pb], r_id], writes=[r_pt[tp]])
                        if tp == 0:
                            P.op(DVE, lambda e, tp=tp, o0=o0, n=n, tb=tb, sb=sb: e.tensor_copy(
                                stg[sb][:, o0:o0 + n, tb * 128:(tb + 1) * 128], pt[tp][:, 0:n, :]),
                                reads=[r_pt[tp]], writes=[r_stg[sb]])
                        else:
                            P.op(ACT, lambda e, tp=tp, o0=o0, n=n, tb=tb, sb=sb: e.copy(
                                stg[sb][:, o0:o0 + n, tb * 128:(tb + 1) * 128], pt[tp][:, 0:n, :]),
                                reads=[r_pt[tp]], writes=[r_stg[sb]])
                pending = s2
            if tb == NTB - 1 and has_qk:
                if pending is not None:
                    pending()
                    pending = None

                def store(e, sb=sb, segs=segs):
                    outs = []
                    for (o0, n, kind, mixer, idx) in segs:
                        if kind == "q":
                            outs.append(e.dma_start(out=qT_o[idx:idx + n].rearrange("h p t -> p h t"), in_=stg[sb][:, o0:o0 + n, :]))
                        elif kind == "k":
                            outs.append(e.dma_start(out=kT_o[idx:idx + n].rearrange("h p t -> p h t"), in_=stg[sb][:, o0:o0 + n, :]))
                    return outs
                nqk = sum(1 for s in segs if s[2] in ("q", "k"))
                P.dma(SP, store, r_stg[sb], reads=[r_stg[sb]], nd=nqk)
    if pending is not None:
        pending()
    P.pop_scope()
    P.pop_scope()


D_PAT = ((128, 1), (512, 4), (2048, 16))
DBG = {}
OD_W = 132


def d_tiles():
    items = []
    for g, (win, dil) in enumerate(D_PAT):
        ncls = dil
        cl3 = 3072 // dil
        cl1 = 1024 // dil
        qs = min(128, cl1)
        for r in range(ncls):
            for ub in range(cl1 // qs):
                q0 = r * cl1 + ub * qs
                kstart = r * cl3 + cl1 + ub * qs - 64
                nkeys = qs + 128
                blocks = []
                o = 0
                while o < nkeys:
                    nk = min(128, nkeys - o)
                    blocks.append((kstart + o, nk))
                    o += nk
                items.append((g, r, ub, qs, q0, blocks))
    return items


def build_l2(layer=0):
    nc = bass.Bass("TRN2", target_bir_lowering=False)
    dt = nc.dram_tensor
    I = lambda n, s, d=BF16: dt(n, s, d, kind="ExternalInput").ap()
    qT = I("qT", [NQH, 128, SLAB])
    kTa = I("kTa", [8, 128, 3072]); vEa = I("vEa", [8, 128, 24, 129])
    kTb = I("kTb", [2, 128, 4096]); vEb = I("vEb", [2, 128, 32, 129])
    kTc = I("kTc", [8, 128, 4096]); vEc = I("vEc", [4, 128, 32, 257])
    kTd = I("kTd", [12, 128, 3072]); vEd = I("vEd", [12, 3072, 129])
    sz = I("sz", [SLAB, NZU * 128], F32)
    abias = I("abias", [8, 128, 5, 7, 128], F32)
    cstrip = I("cstrip", [4, 128, 4992], F32)
    dbias = I("dbias", [128, 12, 2, 128], F32)
    dval = I("dval", [128, 64], F32)
    qkg = I("qkg", [1, 1024], F32)
    relb = I("relb", [1, 3720], F32)
    dlam = I("dlam", [1, 512], F32)
    subg = I("subg", [1, 256], F32)
    yT_o = dt("yT_o", [24, 128, SLAB], BF16, kind="ExternalOutput").ap()
    od_o = dt("od_o", [3, SLAB, 4, OD_W], F32, kind="ExternalOutput").ap()
    P = Prog(nc)
    emit_l2(P, layer, qT, kTa, vEa, kTb, vEb, kTc, vEc, kTd, vEd, sz, abias, cstrip, dbias, dval, qkg, relb, dlam, subg,
            yT_o, od_o)
    P.emit()
    return nc


def emit_l2(P, layer, qT, kTa, vEa, kTb, vEb, kTc, vEc, kTd, vEd, sz, abias, cstrip, dbias, dval, qkg, relb, dlam, subg,
            yT_o, od_o, fz=None, mixers="ABCD"):
    tag = "_a%d" % layer
    lam_init = 0.8 - 0.6 * math.exp(-0.3 * layer)
    P.push_scope()
    ident, r_id = make_identity(P, "identB" + tag)
    cons = P.sbuf("cons" + tag, [128, 64], F32)
    r_cons = Res("cons")
    gain = P.sbuf("gainc" + tag, [128, 1024], F32)
    r_gain = Res("gainc")
    bia = [P.sbuf("bia%d%s" % (i, tag), [128, 4992], F32) for i in range(2)]
    r_bia = [Res("bia0"), Res("bia1")]
    rb = bia[0][:, 0:3720]
    r_rb = r_bia[0]
    lamt = P.sbuf("lamt" + tag, [128, 512], F32)
    r_lamt = Res("lamt")
    sgt = P.sbuf("sgt" + tag, [128, 256], F32)
    r_sgt = Res("sgt")
    dvt = P.sbuf("dvt" + tag, [128, 64], F32)
    r_dvt = Res("dvt")
    dbt = P.sbuf("dbt" + tag, [128, 12, 2, 128], F32)
    r_dbt = Res("dbt")
    P.dma(SP, lambda e: e.dma_start(out=gain[:].unsqueeze(1), in_=qkg.partition_broadcast(128)), r_gain, writes=[r_gain])
    P.dma(SP, lambda e: e.dma_start(out=rb.unsqueeze(1), in_=relb.partition_broadcast(128)), r_rb, writes=[r_rb])
    P.dma(SP, lambda e: e.dma_start(out=lamt[:].unsqueeze(1), in_=dlam.partition_broadcast(128)), r_lamt, writes=[r_lamt])
    P.dma(SP, lambda e: e.dma_start(out=sgt[:].unsqueeze(1), in_=subg.partition_broadcast(128)), r_sgt, writes=[r_sgt])
    P.dma(SP, lambda e: e.dma_start(out=dvt[:], in_=dval), r_dvt, writes=[r_dvt])
    P.dma(SP, lambda e: e.dma_start(out=dbt[:], in_=dbias), r_dbt, writes=[r_dbt])
    P.op(DVE, lambda e: e.reduce_max(out=cons[:, 0:8], in_=gain[:].rearrange("p (n d) -> p n d", d=128), axis=AX.X,
                                     apply_absolute_value=True), reads=[r_gain], writes=[r_cons])
    for m in range(4):
        P.op(DVE, lambda e, m=m: e.scalar_tensor_tensor(out=cons[:, 8 + m:9 + m], in0=cons[:, 2 * m:2 * m + 1], scalar=-(128 ** 0.5),
                                                        in1=cons[:, 2 * m + 1:2 * m + 2], op0=ALU.mult, op1=ALU.mult),
             reads=[r_cons], writes=[r_cons])
    P.op(DVE, lambda e: e.reduce_max(out=cons[:, 12:13], in_=rb, axis=AX.X), reads=[r_rb], writes=[r_cons])
    P.op(DVE, lambda e: e.tensor_scalar(cons[:, 12:13], cons[:, 12:13], 0.0, None, ALU.max), reads=[r_cons], writes=[r_cons])
    P.op(DVE, lambda e: e.tensor_sub(cons[:, 8:9], cons[:, 8:9], cons[:, 12:13]), reads=[r_cons], writes=[r_cons])
    P.op(DVE, lambda e: e.memset(cons[:, 13:14], RMS_EPS), writes=[r_cons])
    P.op(DVE, lambda e: e.tensor_tensor(out=lamt[:, 0:128], in0=lamt[:, 0:128], in1=lamt[:, 128:256], op=ALU.mult),
         reads=[r_lamt], writes=[r_lamt])
    P.op(DVE, lambda e: e.tensor_tensor(out=lamt[:, 256:384], in0=lamt[:, 256:384], in1=lamt[:, 384:512], op=ALU.mult),
         reads=[r_lamt], writes=[r_lamt])
    P.op(DVE, lambda e: e.reduce_sum(out=cons[:, 14:15], in_=lamt[:, 0:128], axis=AX.X), reads=[r_lamt], writes=[r_cons])
    P.op(DVE, lambda e: e.reduce_sum(out=cons[:, 15:16], in_=lamt[:, 256:384], axis=AX.X), reads=[r_lamt], writes=[r_cons])
    P.op(ACT, lambda e: e.activation(out=cons[:, 14:16], in_=cons[:, 14:16], func=AF.Exp), reads=[r_cons], writes=[r_cons])
    P.op(DVE, lambda e: e.tensor_sub(cons[:, 16:17], cons[:, 15:16], cons[:, 14:15]), reads=[r_cons], writes=[r_cons])
    P.op(DVE, lambda e: e.tensor_scalar(cons[:, 16:17], cons[:, 16:17], -lam_init, None, ALU.add), reads=[r_cons], writes=[r_cons])
    P.op(DVE, lambda e: e.tensor_scalar(dvt[:], dvt[:], cons[:, 11:12], None, ALU.add), reads=[r_cons, r_dvt], writes=[r_dvt])
    P.op(ACT, lambda e: e.mul(sgt[:], sgt[:], 1.0 - lam_init), reads=[r_sgt], writes=[r_sgt])

    sT = [P.psum("sT%d%s" % (i, tag), [128, 512], F32) for i in range(3)]
    r_sT = [Res("sT0"), Res("sT1"), Res("sT2")]
    Ot = [P.psum("O%d%s" % (i, tag), [128, 512], F32) for i in range(2)]
    r_O = [Res("O0"), Res("O1")]
    ptT = P.psum("ptT" + tag, [128, 2, 128], BF16)
    r_ptT = Res("ptT")
    tmp = [P.sbuf("tmp%d%s" % (i, tag), [128, 512], F32) for i in range(3)]
    r_tmp = [Res("tmp0"), Res("tmp1"), Res("tmp2")]
    pT = [P.sbuf("pT%d%s" % (i, tag), [128, 512], BF16) for i in range(3)]
    r_pT = [Res("pT0"), Res("pT1"), Res("pT2")]
    kt = [P.sbuf("kt%d%s" % (i, tag), [128, 4096], BF16) for i in range(2)]
    r_kt = [Res("kt0"), Res("kt1")]
    vt = [P.sbuf("vt%d%s" % (i, tag), [128, 32 * 257], BF16) for i in range(2)]
    r_vt = [Res("vt0"), Res("vt1")]
    qt_ = [P.sbuf("qt%d%s" % (i, tag), [128, 2, SLAB], BF16) for i in range(2)]
    r_qt = [Res("qt0"), Res("qt1")]
    szt = [P.sbuf("szt%d%s" % (i, tag), [128, NTB, 256], F32) for i in range(2)]
    r_szt = [Res("szt0"), Res("szt1")]
    ysb = P.sbuf("ysb" + tag, [128, 256], BF16)
    r_ysb = Res("ysb")
    yst = [P.sbuf("yst%d%s" % (i, tag), [128, 2, SLAB], BF16) for i in range(2)]
    r_yst = [Res("yst0"), Res("yst1")]
    fin = P.sbuf("fin" + tag, [128, 16], F32)
    r_fin = Res("fin")
    f1 = P.sbuf("f1" + tag, [128, 256], F32)
    r_f1 = Res("f1")
    f2 = P.sbuf("f2" + tag, [128, 256], F32)
    r_f2 = Res("f2")
    gz = P.sbuf("gz" + tag, [128, 256], F32)
    r_gz = Res("gz")
    odt = [P.sbuf("odt%d%s" % (i, tag), [128, OD_W], F32) for i in range(2)]
    r_odt = [Res("odt0"), Res("odt1")]
    dv = [P.sbuf("dv%d%s" % (i, tag), [128, 2, 129], BF16) for i in range(2)]
    r_dv = [Res("dv0"), Res("dv1")]

    I32 = mybir.dt.int32
    if fz is not None:
        ixK = P.sbuf("ixK" + tag, [128, 90], I32)
        ixA = P.sbuf("ixA" + tag, [128, 192], I32)
        ixD = P.sbuf("ixD" + tag, [128, 768], I32)
        r_ix = Res("ix")
        P.dma(SP, lambda e: [e.dma_start(out=ixK[:], in_=fz["idxK"]), e.dma_start(out=ixA[:], in_=fz["idxVA"]),
                             e.dma_start(out=ixD[:], in_=fz["idxVD"])], r_ix, writes=[r_ix], nd=3)
        vA = P.sbuf("vA" + tag, [128, 14, 8, 129], BF16)
        r_vA = Res("vA")
        ktcm = P.sbuf("ktcm" + tag, [128, 3072], BF16)
        r_ktcm = Res("ktcm")
        qcm = P.sbuf("qcm" + tag, [128, SLAB], BF16)
        r_qcm = Res("qcm")
        for i in range(2):
            P.op(POOL, lambda e, i=i: e.memset(kt[i][:], 0.0), writes=[r_kt[i]])
            P.op(POOL, lambda e, i=i: e.memset(dv[i][:], 1.0), writes=[r_dv[i]])
            P.op(POOL, lambda e, i=i: e.memset(vt[i][:], 1.0), writes=[r_vt[i]])
        P.op(POOL, lambda e: e.memset(vA[:], 1.0), writes=[r_vA])
        G_k4, G_kr, G_v, r_G = fz["G_k4"], fz["G_kr"], fz["G_v"], fz["r_G"]

        def k_window(hb, kidx):
            P.dma(POOL, lambda e: [e.reg_mov(P.regs[15359], 15359)] and [e.indirect_dma_start(out=kt[hb][:, d_ * 1024:(d_ + 1) * 1024], out_offset=None, in_=G_kr,
                                                        in_offset=bass.IndirectOffsetOnAxis(ap=ixK[:, kidx * 3 + d_:kidx * 3 + d_ + 1], axis=0),
                                                        bounds_check=P.regs[15359], oob_is_err=False) for d_ in range(3)],
                  r_kt[hb], reads=[r_ix, r_G], writes=[r_kt[hb]], nd=3)

        def k_global(dst, r_dst, kidx):
            P.dma(SP, lambda e: e.dma_start(out=dst[:, 0:4096].rearrange("d (r t) -> d r t", r=4),
                                            in_=G_k4[kidx // 3, :, kidx % 3].rearrange("r d t -> d r t")), r_dst, reads=[r_G], writes=[r_dst])

        def v_global(hb, c0, wv):
            P.dma(SP, lambda e: [e.dma_start(
                out=vt[hb][:, 0:32 * (wv + 1)].rearrange("p (r tb w) -> p r tb w", r=4, tb=8, w=wv + 1)[:, r_, :, 0:wv],
                in_=G_v.rearrange("(tb r p) c -> tb r p c", tb=8, r=4, p=128)[:, r_, :, c0:c0 + wv].rearrange("tb p c -> p tb c"))
                for r_ in range(4)], r_vt[hb], reads=[r_G], writes=[r_vt[hb]], nd=4)

    st = {"s": 0, "o": 0}
    deferred = []

    def flush():
        while deferred:
            deferred.pop(0)()

    def attend(q_ap, nq, blocks, W, negc_col, oi, bias_fn=None, col_fn=None, r_q=None, r_k=None, r_v=None, r_b=None):
        nb = len(blocks)
        groups = [(gi, blocks[gi:gi + 4]) for gi in range(0, nb, 4)]
        sbs = []
        for _ in groups:
            sbs.append(st["s"] % 3)
            st["s"] += 1

        def issue_qk(idx):
            gi, grp = groups[idx]
            sb = sbs[idx]

            def qk(e, grp=grp, sb=sb):
                last = None
                for j, (k_ap, v_ap, nk) in enumerate(grp):
                    last = e.matmul(sT[sb][0:nk, j * nq:(j + 1) * nq], k_ap, q_ap, start=True, stop=True)
                return last
            P.op(PE, qk, reads=[r_q, r_k], writes=[r_sT[sb]])
            flush()

        issue_qk(0)
        for idx, (gi, grp) in enumerate(groups):
            sb = sbs[idx]
            if idx + 1 < len(groups):
                issue_qk(idx + 1)
            uniform = all(nk == 128 for (_, _, nk) in grp) and col_fn is None
            src = sT[sb]
            r_src = r_sT[sb]
            if bias_fn is not None:
                def addb(e, grp=grp, sb=sb, gi=gi):
                    last = None
                    if all(nk == 128 for (_, _, nk) in grp) and bias_fn(gi, len(grp)) is not None:
                        return e.tensor_tensor(out=tmp[sb][:, 0:len(grp) * nq], in0=sT[sb][:, 0:len(grp) * nq],
                                               in1=bias_fn(gi, len(grp)), op=ALU.add)
                    for j, (k_ap, v_ap, nk) in enumerate(grp):
                        last = e.tensor_tensor(out=tmp[sb][0:nk, j * nq:(j + 1) * nq], in0=sT[sb][0:nk, j * nq:(j + 1) * nq],
                                               in1=bias_fn(gi + j, 1)[0:nk], op=ALU.add)
                    return last
                P.op(DVE, addb, reads=[r_sT[sb], r_b], writes=[r_tmp[sb]])
                src = tmp[sb]
                r_src = r_tmp[sb]

            def ex(e, grp=grp, sb=sb, gi=gi, src=src, uniform=uniform):
                if uniform:
                    return e.activation(out=pT[sb][:, 0:len(grp) * nq], in_=src[:, 0:len(grp) * nq], func=AF.Exp, bias=negc_col)
                last = None
                for j, (k_ap, v_ap, nk) in enumerate(grp):
                    col = negc_col if col_fn is None else col_fn(gi + j)
                    last = e.activation(out=pT[sb][0:nk, j * nq:(j + 1) * nq], in_=src[0:nk, j * nq:(j + 1) * nq], func=AF.Exp,
                                        bias=col[0:nk])
                return last
            P.op(ACT, ex, reads=[r_src, r_cons, r_dvt], writes=[r_pT[sb]])

            def pv(e, grp=grp, sb=sb, first=(idx == 0), last_g=(idx == len(groups) - 1)):
                last = None
                for j, (k_ap, v_ap, nk) in enumerate(grp):
                    last = e.matmul(Ot[oi][0:nq, 0:W], pT[sb][0:nk, j * nq:(j + 1) * nq], v_ap,
                                    start=(first and j == 0), stop=(last_g and j == len(grp) - 1))
                return last
            deferred.append(lambda pv=pv, sb=sb: P.op(PE, pv, reads=[r_pT[sb], r_v], writes=[r_O[oi]]))

    hc = {"n": 0}

    def head_loads(kT_src, kcols, v_src, vcols, q_srcs, sz_cols, bias_src=None, bias_cols=0):
        hb = hc["n"] % 2
        hc["n"] += 1
        if kT_src is not None:
            P.dma(SP, lambda e: e.dma_start(out=kt[hb][:, 0:kcols], in_=kT_src), r_kt[hb], writes=[r_kt[hb]])
        if v_src is not None:
            P.dma(SP, lambda e: e.dma_start(out=vt[hb][:, 0:vcols], in_=v_src), r_vt[hb], writes=[r_vt[hb]])
        P.dma(SP, lambda e: [e.dma_start(out=qt_[hb][:, i, :], in_=qs_) for i, qs_ in enumerate(q_srcs)], r_qt[hb],
              writes=[r_qt[hb]], nd=len(q_srcs))
        if sz_cols is not None:
            c0, cn = sz_cols
            P.dma(SP, lambda e: e.dma_start(out=szt[hb][:, :, 0:cn], in_=sz[:, c0:c0 + cn].rearrange("(t p) c -> p t c", p=128)),
                  r_szt[hb], writes=[r_szt[hb]])
        if bias_src is not None:
            P.dma(SP, lambda e: e.dma_start(out=bia[hb][:, 0:bias_cols], in_=bias_src), r_bia[hb], writes=[r_bia[hb]])
        return hb

    def finalize_simple(oi, hb, tb, u_local, ystage, slot):
        r_ys = r_yst[yi["n"] % 2]
        deferred.append(lambda: finalize_simple_now(oi, hb, tb, u_local, ystage, slot, r_ys))

    def finalize_simple_now(oi, hb, tb, u_local, ystage, slot, r_ys):
        P.op(DVE, lambda e: e.reciprocal(fin[:, 0:1], Ot[oi][:, 128:129]), reads=[r_O[oi]], writes=[r_fin])
        P.op(DVE, lambda e: e.scalar_tensor_tensor(out=ysb[:, 0:128], in0=Ot[oi][:, 0:128], scalar=fin[:, 0:1],
                                                   in1=szt[hb][:, tb, u_local * 128:(u_local + 1) * 128], op0=ALU.mult, op1=ALU.mult),
             reads=[r_O[oi], r_fin, r_szt[hb]], writes=[r_ysb])
        P.op(PE, lambda e: e.transpose(ptT[:, 0, :], ysb[:, 0:128], ident[:]), reads=[r_ysb, r_id], writes=[r_ptT])
        P.op(ACT, lambda e: e.copy(ystage[:, slot, tb * 128:(tb + 1) * 128], ptT[:, 0, :]), reads=[r_ptT], writes=[r_ys])

    yi = {"n": 0}

    slot_of = [0, 1, 2, 2, 2, 2, 3, 4]
    if fz is not None and "A" in mixers:
        for i in range(14):
            P.dma(POOL, lambda e, i=i: [e.reg_mov(P.regs[SEQ * NVU - 1], SEQ * NVU - 1)] and [
                e.indirect_dma_start(out=vA[:, i, h_, 0:128], out_offset=None, in_=G_v.rearrange("t (c d) -> (t c) d", d=128),
                                     in_offset=bass.IndirectOffsetOnAxis(ap=ixA[:, (5 + i) * 8 + h_:(5 + i) * 8 + h_ + 1], axis=0),
                                     bounds_check=P.regs[SEQ * NVU - 1], oob_is_err=False) for h_ in range(8)],
                  r_vA, reads=[r_ix, r_G], writes=[r_vA], nd=8)
    for h in (range(8) if "A" in mixers else []):
        if fz is None:
            hb = head_loads(kTa[h], 3072, vEa[h].rearrange("p k w -> p (k w)"), 24 * 129, [qT[h]], (h * 128, 128),
                            abias[h].rearrange("p s k q -> p (s k q)"), 5 * 7 * 128)
        else:
            hb = head_loads(None, 0, None, 0, [qT[h]], (h * 128, 128), abias[h].rearrange("p s k q -> p (s k q)"), 5 * 7 * 128)
            k_window(hb, h)
        yb = yi["n"] % 2
        for qb in range(NTB):
            oi = st["o"] % 2
            st["o"] += 1
            blocks = []
            for i in range(7):
                kb = 8 + qb - 3 + i
                if fz is None:
                    blocks.append((kt[hb][:, kb * 128:(kb + 1) * 128], vt[hb][:, kb * 129:(kb + 1) * 129], 128))
                else:
                    blocks.append((kt[hb][:, kb * 128:(kb + 1) * 128], vA[:, kb - 5, h, :], 128))
            s = slot_of[qb]
            bf = lambda gi, n, s=s, hb=hb: bia[hb][:, (s * 7 + gi) * 128:(s * 7 + gi + n) * 128]
            if DBG.get("A") == "loads":
                continue
            attend(qt_[hb][:, 0, qb * 128:(qb + 1) * 128], 128, blocks, 129, cons[:, 8:9], oi, bias_fn=bf,
                   r_q=r_qt[hb], r_k=r_kt[hb], r_v=(r_vt[hb] if fz is None else r_vA), r_b=r_bia[hb])
            if DBG.get("A") == "nofin":
                continue
            finalize_simple(oi, hb, qb, 0, yst[yb], 0)
        flush()
        P.dma(SP, lambda e, h=h, yb=yb: e.dma_start(out=yT_o[h], in_=yst[yb][:, 0, :]), r_yst[yb], reads=[r_yst[yb]])
        yi["n"] += 1

    for h in (range(8) if "B" in mixers else []):
        kv = h // 4
        if fz is None:
            hb = head_loads(kTb[kv], 4096, vEb[kv].rearrange("p k w -> p (k w)"), 32 * 129, [qT[8 + h]], ((8 + h) * 128, 128))
        else:
            hb = head_loads(None, 0, None, 0, [qT[8 + h]], ((8 + h) * 128, 128))
            k_global(kt[hb], r_kt[hb], 8 + kv)
            v_global(hb, 1024 + kv * 128, 128)
        yb = yi["n"] % 2
        for qb in range(NTB):
            oi = st["o"] % 2
            st["o"] += 1
            blocks = [(kt[hb][:, kb * 128:(kb + 1) * 128], vt[hb][:, kb * 129:(kb + 1) * 129], 128) for kb in range(32)]
            attend(qt_[hb][:, 0, qb * 128:(qb + 1) * 128], 128, blocks, 129, cons[:, 9:10], oi,
                   r_q=r_qt[hb], r_k=r_kt[hb], r_v=r_vt[hb])
            finalize_simple(oi, hb, qb, 0, yst[yb], 0)
        flush()
        P.dma(SP, lambda e, h=h, yb=yb: e.dma_start(out=yT_o[8 + h], in_=yst[yb][:, 0, :]), r_yst[yb], reads=[r_yst[yb]])
        yi["n"] += 1

    kt2 = P.sbuf("kt2" + tag, [128, 4096], BF16)
    r_kt2 = Res("kt2")
    if fz is not None:
        for i in range(2):
            P.op(POOL, lambda e, i=i: e.memset(vt[i][:], 1.0), writes=[r_vt[i]])
    for h in (range(4) if "C" in mixers else []):
        if fz is None:
            hb = head_loads(kTc[2 * h], 4096, vEc[h].rearrange("p k w -> p (k w)"), 32 * 257, [qT[16 + 2 * h], qT[16 + 2 * h + 1]],
                            ((16 + 2 * h) * 128, 256), cstrip[h], 4992)
            P.dma(SP, lambda e, h=h: e.dma_start(out=kt2[:], in_=kTc[2 * h + 1]), r_kt2, writes=[r_kt2])
        else:
            hb = head_loads(None, 0, None, 0, [qT[16 + 2 * h], qT[16 + 2 * h + 1]], ((16 + 2 * h) * 128, 256), cstrip[h], 4992)
            k_global(kt[hb], r_kt[hb], 10 + 2 * h)
            k_global(kt2, r_kt2, 10 + 2 * h + 1)
            v_global(hb, 1280 + h * 256, 256)
        yb = yi["n"] % 2
        for qb in range(NTB):
            for m in range(2):
                ksrc = kt[hb] if m == 0 else kt2
                blocks = [(ksrc[:, kb * 128:(kb + 1) * 128], vt[hb][:, kb * 257:(kb + 1) * 257], 128) for kb in range(32)]
                bf = lambda gi, n, qb=qb, hb=hb: (bia[hb][:, (qb - gi) * 128 + 3968:(qb - gi) * 128 + 3968 + 128] if n == 1 else None)
                attend(qt_[hb][:, m, qb * 128:(qb + 1) * 128], 128, blocks, 257, cons[:, 10:11], m, bias_fn=bf,
                       r_q=r_qt[hb], r_k=(r_kt[hb] if m == 0 else r_kt2), r_v=r_vt[hb], r_b=r_bia[hb])
            flush()
            P.op(DVE, lambda e: e.reciprocal(fin[:, 0:1], Ot[0][:, 256:257]), reads=[r_O[0]], writes=[r_fin])
            P.op(DVE, lambda e: e.reciprocal(fin[:, 1:2], Ot[1][:, 256:257]), reads=[r_O[1]], writes=[r_fin])
            P.op(DVE, lambda e: e.tensor_tensor(out=fin[:, 2:3], in0=fin[:, 1:2], in1=cons[:, 16:17], op=ALU.mult),
                 reads=[r_fin, r_cons], writes=[r_fin])
            P.op(DVE, lambda e: e.tensor_scalar(f1[:], Ot[0][:, 0:256], fin[:, 0:1], None, ALU.mult), reads=[r_O[0], r_fin], writes=[r_f1])
            P.op(DVE, lambda e: e.scalar_tensor_tensor(out=f2[:], in0=Ot[1][:, 0:256], scalar=fin[:, 2:3], in1=f1[:],
                                                       op0=ALU.mult, op1=ALU.add), reads=[r_O[1], r_fin, r_f1], writes=[r_f2])
            P.op(ACT, lambda e: e.activation(out=f1[:], in_=f2[:], func=AF.Square, accum_out=fin[:, 3:4]),
                 reads=[r_f2], writes=[r_f1, r_fin])
            P.op(ACT, lambda e: e.activation(out=fin[:, 4:5], in_=fin[:, 3:4], func=AF.Ln, scale=1.0 / 256, bias=cons[:, 13:14]),
                 reads=[r_fin, r_cons], writes=[r_fin])
            P.op(ACT, lambda e: e.activation(out=fin[:, 5:6], in_=fin[:, 4:5], func=AF.Exp, scale=-0.5), reads=[r_fin], writes=[r_fin])
            P.op(DVE, lambda e, hb=hb, qb=qb: e.tensor_tensor(out=gz[:], in0=szt[hb][:, qb, 0:256], in1=sgt[:], op=ALU.mult),
                 reads=[r_szt[hb], r_sgt], writes=[r_gz])
            P.op(DVE, lambda e: e.scalar_tensor_tensor(out=ysb[:, 0:256], in0=f2[:], scalar=fin[:, 5:6], in1=gz[:],
                                                       op0=ALU.mult, op1=ALU.mult), reads=[r_f2, r_fin, r_gz], writes=[r_ysb])

            def tr2(e):
                e.transpose(ptT[:, 0, :], ysb[:, 0:128], ident[:])
                return e.transpose(ptT[:, 1, :], ysb[:, 128:256], ident[:])
            P.op(PE, tr2, reads=[r_ysb, r_id], writes=[r_ptT])
            P.op(ACT, lambda e, qb=qb, yb=yb: e.copy(yst[yb][:, :, qb * 128:(qb + 1) * 128], ptT[:, :, :]), reads=[r_ptT], writes=[r_yst[yb]])
        P.dma(SP, lambda e, h=h, yb=yb: e.dma_start(out=yT_o[16 + 2 * h:16 + 2 * h + 2].rearrange("u p t -> p u t"), in_=yst[yb][:, :, :]),
              r_yst[yb], reads=[r_yst[yb]])
        yi["n"] += 1

    items = d_tiles()
    col_idx = 0
    cols_of = {}
    for it_i, (g, r, ub, qs, q0, blocks) in enumerate(items):
        cols_of[it_i] = list(range(col_idx, col_idx + len(blocks)))
        col_idx += len(blocks)
    assert col_idx <= 64
    oc = 0
    for g in (range(3) if "D" in mixers else []):
        for hh in range(4):
            hd = g * 4 + hh
            if fz is None:
                hb = head_loads(kTd[hd], 3072, None, 0, [qT[24 + hd]], None)
                ksrc, r_ksrc, qsrc, r_qsrc = kt[hb], r_kt[hb], qt_[hb][:, 0, :], r_qt[hb]
            else:
                hb = head_loads(None, 0, None, 0, [qT[24 + hd]], None)
                k_window(hb, 18 + hd)
                dil = D_PAT[g][1]
                if dil == 1:
                    ksrc, r_ksrc, qsrc, r_qsrc = kt[hb], r_kt[hb], qt_[hb][:, 0, :], r_qt[hb]
                else:
                    P.op(POOL, lambda e, hb=hb, dil=dil: e.tensor_copy(ktcm[:, 0:3072].rearrange("p (r u) -> p r u", r=dil),
                                                                       kt[hb][:, 0:3072].rearrange("p (u r) -> p r u", r=dil)),
                         reads=[r_kt[hb]], writes=[r_ktcm])
                    P.op(POOL, lambda e, hb=hb, dil=dil: e.tensor_copy(qcm[:, :].rearrange("p (r u) -> p r u", r=dil),
                                                                       qt_[hb][:, 0, :].rearrange("p (u r) -> p r u", r=dil)),
                         reads=[r_qt[hb]], writes=[r_qcm])
                    ksrc, r_ksrc, qsrc, r_qsrc = ktcm, r_ktcm, qcm[:, :], r_qcm
            for it_i, (g2, r, ub, qs, q0, blocks) in enumerate(items):
                if g2 != g:
                    continue
                oi = st["o"] % 2
                st["o"] += 1
                vbuf = dv[oc % 2]
                r_vbuf = r_dv[oc % 2]
                oc += 1
                if fz is None:
                    P.dma(SP, lambda e, hd=hd, blocks=blocks, vbuf=vbuf: [
                        e.dma_start(out=vbuf[0:nk, j, :], in_=vEd[hd][k0:k0 + nk, :]) for j, (k0, nk) in enumerate(blocks)],
                        r_vbuf, writes=[r_vbuf], nd=len(blocks))
                else:
                    P.dma(POOL, lambda e, hd=hd, blocks=blocks, vbuf=vbuf, it_i=it_i: [e.reg_mov(P.regs[SEQ * NVU - 1], SEQ * NVU - 1)] and [
                        e.indirect_dma_start(out=vbuf[0:nk, j, 0:128], out_offset=None, in_=G_v.rearrange("t (c d) -> (t c) d", d=128),
                                             in_offset=bass.IndirectOffsetOnAxis(ap=ixD[0:nk, hd * 64 + cols_of[it_i][j]:hd * 64 + cols_of[it_i][j] + 1], axis=0),
                                             bounds_check=P.regs[SEQ * NVU - 1], oob_is_err=False) for j, (k0, nk) in enumerate(blocks)],
                        r_vbuf, reads=[r_ix, r_G], writes=[r_vbuf], nd=len(blocks))
                blocks3 = [(ksrc[:, k0:k0 + nk], vbuf[0:nk, j, :], nk) for j, (k0, nk) in enumerate(blocks)]
                bf = lambda gi, n, hd=hd, qs=qs: (dbt[:, hd, gi, 0:qs] if n == 1 else None)
                cf = lambda bi, it_i=it_i: dvt[:, cols_of[it_i][bi]:cols_of[it_i][bi] + 1]
                attend(qsrc[:, q0:q0 + qs], qs, blocks3, 129, None, oi, bias_fn=bf, col_fn=cf,
                       r_q=r_qsrc, r_k=r_ksrc, r_v=r_vbuf, r_b=r_dbt)
                def d_fin(oi=oi, g=g, hh=hh, q0=q0, qs=qs, r=r, ub=ub):
                    ob = oi
                    P.op(DVE, lambda e, ob=ob, qs=qs: e.tensor_copy(odt[ob][0:qs, 0:129], Ot[ob][0:qs, 0:129]), reads=[r_O[ob]], writes=[r_odt[ob]])
                    if fz is None:
                        P.dma(SP, lambda e, ob=ob, g=g, hh=hh, q0=q0, qs=qs: e.dma_start(out=od_o[g, q0:q0 + qs, hh, 0:129], in_=odt[ob][0:qs, 0:129]),
                              r_odt[ob], reads=[r_odt[ob]])
                    else:
                        dil = D_PAT[g][1]
                        P.dma(SP, lambda e, ob=ob, g=g, hh=hh, r=r, ub=ub, qs=qs, dil=dil: e.dma_start(
                            out=od_o.rearrange("(u r) g h w -> r u g h w", r=dil)[r, ub * qs:(ub + 1) * qs, g, hh, 0:129], in_=odt[ob][0:qs, 0:129]),
                            r_odt[ob], reads=[r_odt[ob]])

                deferred.append(d_fin)
    flush()
    P.pop_scope()


BR_U0 = (0, 8, 16, 24)
BR_NK = (8, 8, 8, 4)


def build_l3():
    nc = bass.Bass("TRN2", target_bir_lowering=False)
    dt = nc.dram_tensor
    I = lambda n, s, d=BF16: dt(n, s, d, kind="ExternalInput").ap()
    xnT_i = I("xnT", [32, 128, SLAB])
    yT_i = I("yT", [24, 128, SLAB])
    od = I("od", [SLAB, 3, 4, OD_W], F32)
    szd = I("szd", [SLAB, 512], F32)
    wg = I("wg", [D_MODEL, 4 * D_MODEL], F32)
    wbr = I("wbr", [3584, D_MODEL], F32)
    mT_o = dt("mT_o", [32, 128, SLAB], BF16, kind="ExternalOutput").ap()
    P = Prog(nc)
    emit_l3(P, 0, xnT_i, yT_i, od, szd, wg, wbr, mT_o)
    P.emit()
    return nc


def emit_l3(P, layer, xnT_i, yT_i, od, szd, wg, wbr, mT_o):
    tag = "_m%d" % layer
    P.push_scope()
    ident, r_id = make_identity(P, "identM" + tag)
    xnT = P.sbuf("xnTm" + tag, [128, 32, SLAB], BF16)
    r_xnT = Res("xnTm")
    yT = P.sbuf("yTm" + tag, [128, 28, SLAB], BF16)
    r_yT = Res("yTm")
    P.dma(SP, lambda e: [e.dma_start(out=xnT[:, 8 * i:8 * i + 8, :], in_=xnT_i[8 * i:8 * i + 8].rearrange("k p t -> p k t"))
                         for i in range(4)], r_xnT, writes=[r_xnT], nd=4)
    r_yTl = Res("yTl")
    P.dma(SP, lambda e: [e.dma_start(out=yT[:, 8 * i:8 * i + 8, :], in_=yT_i[8 * i:8 * i + 8].rearrange("k p t -> p k t"))
                         for i in range(3)], r_yTl, writes=[r_yT], nd=3)
    odt = [P.sbuf("odm%d%s" % (i, tag), [128, 3, 4, OD_W], F32) for i in range(2)]
    r_odt = [Res("odm0"), Res("odm1")]
    szt = [P.sbuf("szm%d%s" % (i, tag), [128, 512], F32) for i in range(2)]
    r_szt = [Res("szm0"), Res("szm1")]
    acc = P.sbuf("accd" + tag, [128, 4, OD_W], F32)
    r_acc = Res("accd")
    rec = P.sbuf("recd" + tag, [128, 4], F32)
    r_rec = Res("recd")
    yd = P.sbuf("yd" + tag, [128, 512], BF16)
    r_yd = Res("yd")
    ptD = P.psum("ptD" + tag, [128, 4, 128], BF16)
    r_ptD = Res("ptD")
    for tb in range(NTB):
        b = tb % 2
        P.dma(SP, lambda e, tb=tb, b=b: e.dma_start(out=odt[b][:], in_=od[tb * 128:(tb + 1) * 128]), r_odt[b], writes=[r_odt[b]])
        P.dma(SP, lambda e, tb=tb, b=b: e.dma_start(out=szt[b][:], in_=szd[tb * 128:(tb + 1) * 128, :]), r_szt[b], writes=[r_szt[b]])
        P.op(DVE, lambda e, b=b: e.tensor_tensor(out=acc[:], in0=odt[b][:, 0], in1=odt[b][:, 1], op=ALU.add), reads=[r_odt[b]], writes=[r_acc])
        P.op(DVE, lambda e, b=b: e.tensor_tensor(out=acc[:], in0=acc[:], in1=odt[b][:, 2], op=ALU.add), reads=[r_odt[b], r_acc], writes=[r_acc])
        P.op(DVE, lambda e: e.reciprocal(rec[:], acc[:, :, 128]), reads=[r_acc], writes=[r_rec])
        for hh in range(4):
            P.op(DVE, lambda e, hh=hh, b=b: e.scalar_tensor_tensor(out=yd[:, hh * 128:(hh + 1) * 128], in0=acc[:, hh, 0:128], scalar=rec[:, hh:hh + 1],
                                                                   in1=szt[b][:, hh * 128:(hh + 1) * 128], op0=ALU.mult, op1=ALU.mult),
                 reads=[r_acc, r_rec, r_szt[b]], writes=[r_yd])

        def tr(e):
            last = None
            for hh in range(4):
                last = e.transpose(ptD[:, hh, :], yd[:, hh * 128:(hh + 1) * 128], ident[:])
            return last
        P.op(PE, tr, reads=[r_yd, r_id], writes=[r_ptD])
        P.op(ACT, lambda e, tb=tb: e.copy(yT[:, 24:28, tb * 128:(tb + 1) * 128], ptD[:]), reads=[r_ptD], writes=[r_yT])

    wgb = [P.sbuf("wgb%d%s" % (i, tag), [128, 32, 128], BF16) for i in range(3)]
    r_wgb = [Res("wgb%d" % i) for i in range(3)]
    wbb = [P.sbuf("wbb%d%s" % (i, tag), [128, 8, 128], BF16) for i in range(3)]
    r_wbb = [Res("wbb%d" % i) for i in range(3)]
    gp = [P.psum("gp%d%s" % (i, tag), [128, 512], F32) for i in range(2)]
    r_gp = [Res("gp0"), Res("gp1")]
    pp = [P.psum("pp%d%s" % (i, tag), [128, 512], F32) for i in range(2)]
    r_pp = [Res("pp0"), Res("pp1")]
    sg = [P.sbuf("sg%d%s" % (i, tag), [128, 512], F32) for i in range(2)]
    r_sg = [Res("sg0"), Res("sg1")]
    macc = [P.sbuf("macc%d%s" % (i, tag), [128, 512], F32) for i in range(2)]
    r_macc = [Res("macc0"), Res("macc1")]
    mst = [P.sbuf("mst%d%s" % (i, tag), [128, SLAB], BF16) for i in range(2)]
    r_mst = [Res("mst0"), Res("mst1")]
    wgv = wg.rearrange("(k p) n -> p k n", p=128)
    if isinstance(wbr, (list, tuple)):
        wbl = [w_.rearrange("(k p) n -> p k n", p=128) for w_ in wbr]
        wb_src = lambda br, cc: wbl[br][:, 0:BR_NK[br], cc * 128:(cc + 1) * 128]
    else:
        wbv = wbr.rearrange("(k p) n -> p k n", p=128)
        wb_src = lambda br, cc: wbv[:, BR_U0[br]:BR_U0[br] + BR_NK[br], cc * 128:(cc + 1) * 128]
    jobs = [(cc, br) for cc in range(32) for br in range(4)]

    def load(ji):
        cc, br = jobs[ji]
        b = ji % 3
        P.dma(POOL, lambda e, cc=cc, br=br, b=b: [e.dma_start(out=wgb[b][:, 16 * i:16 * i + 16, :],
                                                              in_=wgv[:, 16 * i:16 * i + 16, br * D_MODEL + cc * 128:br * D_MODEL + (cc + 1) * 128])
                                                  for i in range(2)], r_wgb[b], writes=[r_wgb[b]], nd=2)
        P.dma(POOL, lambda e, cc=cc, br=br, b=b: e.dma_start(out=wbb[b][:, 0:BR_NK[br], :],
                                                             in_=wb_src(br, cc)),
              r_wbb[b], writes=[r_wbb[b]])
    load(0)
    load(1)
    cnt = 0
    for ji, (cc, br) in enumerate(jobs):
        if ji + 2 < len(jobs):
            load(ji + 2)
        b = ji % 3
        ms = cc % 2
        for tg in range(2):
            pb = cnt % 2
            cnt += 1

            def gm(e, b=b, tg=tg, pb=pb):
                last = None
                for k in range(32):
                    last = e.matmul(gp[pb][:], wgb[b][:, k, :], xnT[:, k, tg * 512:(tg + 1) * 512], start=(k == 0), stop=(k == 31))
                return last
            P.op(PE, gm, reads=[r_wgb[b], r_xnT], writes=[r_gp[pb]])

            def pm(e, b=b, tg=tg, pb=pb, br=br):
                last = None
                nk = BR_NK[br]
                for k in range(nk):
                    last = e.matmul(pp[pb][:], wbb[b][:, k, :], yT[:, BR_U0[br] + k, tg * 512:(tg + 1) * 512], start=(k == 0), stop=(k == nk - 1))
                return last
            P.op(PE, pm, reads=[r_wbb[b], r_yT], writes=[r_pp[pb]])
            P.op(ACT, lambda e, pb=pb: e.activation(out=sg[pb][:], in_=gp[pb][:], func=AF.Sigmoid), reads=[r_gp[pb]], writes=[r_sg[pb]])
            if br == 0:
                P.op(DVE, lambda e, pb=pb, tg=tg: e.tensor_tensor(out=macc[tg][:], in0=sg[pb][:], in1=pp[pb][:], op=ALU.mult),
                     reads=[r_sg[pb], r_pp[pb]], writes=[r_macc[tg]])
            else:
                P.op(DVE, lambda e, pb=pb, tg=tg: e.tensor_tensor(out=sg[pb][:], in0=sg[pb][:], in1=pp[pb][:], op=ALU.mult),
                     reads=[r_sg[pb], r_pp[pb]], writes=[r_sg[pb]])
                if br < 3:
                    P.op(DVE, lambda e, pb=pb, tg=tg: e.tensor_tensor(out=macc[tg][:], in0=macc[tg][:], in1=sg[pb][:], op=ALU.add),
                         reads=[r_sg[pb], r_macc[tg]], writes=[r_macc[tg]])
                else:
                    P.op(DVE, lambda e, pb=pb, tg=tg, ms=ms: e.tensor_tensor(out=mst[ms][:, tg * 512:(tg + 1) * 512], in0=macc[tg][:], in1=sg[pb][:], op=ALU.add),
                         reads=[r_sg[pb], r_macc[tg]], writes=[r_mst[ms]])
        if br == 3:
            P.dma(SP, lambda e, cc=cc, ms=ms: e.dma_start(out=mT_o[cc], in_=mst[ms][:]), r_mst[ms], reads=[r_mst[ms]])
    P.pop_scope()


def build_l4():
    nc = bass.Bass("TRN2", target_bir_lowering=False)
    dt = nc.dram_tensor
    mT_i = dt("mT", [32, 128, SLAB], BF16, kind="ExternalInput").ap()
    wo = dt("wo", [D_MODEL, D_MODEL], F32, kind="ExternalInput").ap()
    x = dt("x", [SLAB, D_MODEL], F32, kind="ExternalInput").ap()
    out = dt("out", [SLAB, D_MODEL], F32, kind="ExternalOutput").ap()
    P = Prog(nc)
    emit_l4(P, 0, mT_i, wo, x, out)
    P.emit()
    return nc


def emit_l4(P, layer, mT_i, wo, x, out):
    tag = "_o%d" % layer
    P.push_scope()
    mT = P.sbuf("mT" + tag, [128, 32, SLAB], BF16)
    r_mT = Res("mT")
    P.dma(SP, lambda e: [e.dma_start(out=mT[:, 8 * i:8 * i + 8, :], in_=mT_i[8 * i:8 * i + 8].rearrange("k p t -> p k t"))
                         for i in range(4)], r_mT, writes=[r_mT], nd=4)
    wb = [P.sbuf("wob%d%s" % (i, tag), [128, 32, 512], BF16) for i in range(2)]
    r_wb = [Res("wob0"), Res("wob1")]
    xb = [P.sbuf("xob%d%s" % (i, tag), [128, 512], F32) for i in range(2)]
    r_xb = [Res("xob0"), Res("xob1")]
    ob = [P.sbuf("oob%d%s" % (i, tag), [128, 512], F32) for i in range(2)]
    r_ob = [Res("oob0"), Res("oob1")]
    ps = [P.psum("pso%d%s" % (i, tag), [128, 512], F32) for i in range(2)]
    r_ps = [Res("pso0"), Res("pso1")]
    wv = wo.rearrange("(k p) n -> p k n", p=128)

    def load_w(cg):
        b = cg % 2
        P.dma(POOL, lambda e, cg=cg, b=b: [e.dma_start(out=wb[b][:, 8 * i:8 * i + 8, :], in_=wv[:, 8 * i:8 * i + 8, cg * 512:(cg + 1) * 512])
                                           for i in range(4)], r_wb[b], writes=[r_wb[b]], nd=4)
    load_w(0)
    cnt = 0
    for cg in range(8):
        if cg + 1 < 8:
            load_w(cg + 1)
        b = cg % 2
        for tb in range(NTB):
            pb = cnt % 2
            cnt += 1
            P.dma(SP, lambda e, tb=tb, cg=cg, pb=pb: e.dma_start(out=xb[pb][:], in_=x[tb * 128:(tb + 1) * 128, cg * 512:(cg + 1) * 512]),
                  r_xb[pb], writes=[r_xb[pb]])

            def mm(e, tb=tb, b=b, pb=pb):
                last = None
                for k in range(32):
                    last = e.matmul(ps[pb][:], mT[:, k, tb * 128:(tb + 1) * 128], wb[b][:, k, :], start=(k == 0), stop=(k == 31))
                return last
            P.op(PE, mm, reads=[r_mT, r_wb[b]], writes=[r_ps[pb]])
            P.op(DVE, lambda e, pb=pb: e.tensor_tensor(out=ob[pb][:], in0=ps[pb][:], in1=xb[pb][:], op=ALU.add),
                 reads=[r_ps[pb], r_xb[pb]], writes=[r_ob[pb]])
            P.dma(SP, lambda e, tb=tb, cg=cg, pb=pb: e.dma_start(out=out[tb * 128:(tb + 1) * 128, cg * 512:(cg + 1) * 512], in_=ob[pb][:]),
                  r_ob[pb], reads=[r_ob[pb]])
    P.pop_scope()


def rope_table():
    t = np.arange(SEQ)
    row = (t // 64).astype(np.float32)
    col = (t % 64).astype(np.float32)
    inv = (np.float32(10000.0) ** (-np.arange(32, dtype=np.float32) / np.float32(32))).astype(np.float32)
    ang = np.concatenate([row[:, None] * inv, col[:, None] * inv], -1).astype(np.float32)
    return np.concatenate([np.cos(ang), np.sin(ang)], -1).astype(np.float32)


def build_abias(rel_bias, qt):
    out = np.full((8, 128, 5, 7, 128), NEG, np.float32)
    rep_qb = [0, 1, 2, 6, 7]
    p = np.arange(128)
    for s, qb in enumerate(rep_qb):
        Q = qt * 8 + qb
        tq = Q * 128 + np.arange(128)
        r = tq // 64
        c = tq % 64
        r0 = np.clip(r - 4, 0, 56)
        c0 = np.clip(c - 8, 0, 48)
        for i in range(7):
            tk = (Q - 3 + i) * 128 + p
            if tk[0] < 0 or tk[0] >= SEQ:
                continue
            kr = tk // 64
            kc = tk % 64
            valid = ((kr[:, None] >= r0[None, :]) & (kr[:, None] < r0[None, :] + 8)
                     & (kc[:, None] >= c0[None, :]) & (kc[:, None] < c0[None, :] + 16))
            ro = np.clip(kr[:, None] - r[None, :] + 7, 0, 14)
            co = np.clip(kc[:, None] - c[None, :] + 15, 0, 30)
            vals = rel_bias[:, ro, co]
            out[:, :, s, i, :] = np.where(valid[None], vals, np.float32(NEG))
    return out


def build_cstrip(qt):
    w = np.arange(4992, dtype=np.float32)[None, :]
    ki = np.arange(128, dtype=np.float32)[:, None]
    dist = np.abs(w - 3968 + qt * 1024 - ki)
    slopes = np.asarray([2.0 ** (-8.0 * (h + 1) / 4) for h in range(4)], np.float32)
    return (-slopes[:, None, None] * dist[None]).astype(np.float32)


def build_dbias():
    out = np.full((128, 12, 2, 128), NEG, np.float32)
    p = np.arange(128)[:, None]
    qi = np.arange(128)[None, :]
    for g, (win, dil) in enumerate(D_PAT):
        for hh in range(4):
            hd = g * 4 + hh
            slope = np.float32(2.0 ** (-8.0 * (hd + 1) / 12))
            for j in range(2):
                delta = j * 128 + p - 64 - qi
                pen = -(slope * np.abs(delta * dil).astype(np.float32))
                out[:, hd, j, :] = np.where(np.abs(delta) <= 64, pen, np.float32(NEG))
    return out


def build_dval(qt):
    out = np.zeros((128, 64), np.float32)
    col = 0
    for (g, r, ub, qs, q0, blocks) in d_tiles():
        dil = D_PAT[g][1]
        cl3 = 3072 // dil
        for (k0, nk) in blocks:
            k = k0 + np.arange(128)
            u3 = k % cl3
            rr = k // cl3
            tok = (qt - 1) * 1024 + dil * u3 + rr
            ok = (tok >= 0) & (tok < SEQ) & (np.arange(128) < nk)
            out[:, col] = np.where(ok, 0.0, NEG)
            col += 1
    return out


def class_major_cols(a, dil):
    T = a.shape[-1]
    return np.ascontiguousarray(a.reshape(a.shape[:-1] + (T // dil, dil)).swapaxes(-1, -2).reshape(a.shape))


def class_major_rows(a, dil):
    T = a.shape[0]
    return np.ascontiguousarray(a.reshape((T // dil, dil) + a.shape[1:]).swapaxes(0, 1).reshape(a.shape))


def window3(a, qt, axis):
    shp = list(a.shape)
    shp[axis] = 3072
    out = np.zeros(shp, a.dtype)
    lo = (qt - 1) * 1024
    s0 = max(lo, 0)
    s1 = min(lo + 3072, SEQ)
    src = [slice(None)] * a.ndim
    dst = [slice(None)] * a.ndim
    src[axis] = slice(s0, s1)
    dst[axis] = slice(s0 - lo, s1 - lo)
    out[tuple(dst)] = a[tuple(src)]
    return out


def with_ones(v):
    T, H, W = v.shape
    out = np.ones((H, T, W + 1), v.dtype)
    out[:, :, :W] = v.transpose(1, 0, 2)
    return out


def p_layout(a):
    H, T, W = a.shape
    return np.ascontiguousarray(a.reshape(H, T // 128, 128, W).transpose(0, 2, 1, 3))


def prep_l2(r1, params, l):
    ins = []
    dbias = build_dbias()
    for b in range(2):
        kT = np.concatenate([np.asarray(r1[b * 4 + q]["kT_o"]) for q in range(4)], axis=2)
        v = np.concatenate([np.asarray(r1[b * 4 + q]["v_o"]) for q in range(4)], axis=0)
        vEb = p_layout(with_ones(v[:, 1024:1280].reshape(SEQ, 2, 128)))
        vEc = p_layout(with_ones(v[:, 1280:2304].reshape(SEQ, 4, 256)))
        for qt in range(4):
            c = b * 4 + qt
            kTa = window3(kT[0:8], qt, 2)
            vEa = p_layout(with_ones(window3(v[:, 0:1024], qt, 0).reshape(3072, 8, 128)))
            kTd_w = window3(kT[18:30], qt, 2)
            vd_w = with_ones(window3(v[:, 2304:3840], qt, 0).reshape(3072, 12, 128))
            kTd = np.empty_like(kTd_w)
            vEd = np.empty_like(vd_w)
            qT = np.array(np.asarray(r1[c]["qT_o"]))
            for g, (win, dil) in enumerate(D_PAT):
                for hh in range(4):
                    hd = g * 4 + hh
                    kTd[hd] = class_major_cols(kTd_w[hd], dil)
                    vEd[hd] = class_major_rows(vd_w[hd], dil)
                    qT[24 + hd] = class_major_cols(qT[24 + hd], dil)
            ins.append({
                "qT": qT, "kTa": kTa, "vEa": vEa, "kTb": np.ascontiguousarray(kT[8:10]), "vEb": vEb,
                "kTc": np.ascontiguousarray(kT[10:18]), "vEc": vEc, "kTd": kTd, "vEd": vEd,
                "sz": np.asarray(r1[c]["z_o"]),
                "abias": build_abias(params["na_rel_bias"][l], qt), "cstrip": build_cstrip(qt), "dbias": dbias,
                "dval": build_dval(qt), "qkg": params["qk_gain"][l].reshape(1, 1024).copy(),
                "relb": params["na_rel_bias"][l].reshape(1, 3720).copy(),
                "dlam": params["diff_lambda"][l].reshape(1, 512).copy(),
                "subg": params["diff_subln_g"][l].reshape(1, 256).copy(),
            })
    return ins


def od_natural(od_o):
    out = np.empty((SLAB, 3, 4, OD_W), np.float32)
    for g, (win, dil) in enumerate(D_PAT):
        a = np.asarray(od_o[g])
        out[:, g] = a.reshape(dil, SLAB // dil, 4, OD_W).swapaxes(0, 1).reshape(SLAB, 4, OD_W)
    return out


_PROGS = {}


def _prog(name, fn):
    if name not in _PROGS:
        _PROGS[name] = fn()
    return _PROGS[name]


def run_layer(xs, params, l, cs):
    cores = list(range(8))
    w_in = params["w_in"][l]
    wq = np.ascontiguousarray(w_in[:, :NQKVZ])
    ng = params["norm_g"][l][None, :].copy()
    qkg = params["qk_gain"][l].reshape(1, 1024).copy()
    in1 = [{"x": xs[c], "ng": ng, "w": wq, "qkg": qkg, "cs": np.ascontiguousarray(cs[(c % 4) * 1024:(c % 4 + 1) * 1024])} for c in cores]
    r1 = run_bass_kernel_spmd(_prog("l1", build_l1), in1, core_ids=cores).results
    del wq, in1
    in2 = prep_l2(r1, params, l)
    r2 = run_bass_kernel_spmd(_prog("l2_%d" % l, lambda: build_l2(l)), in2, core_ids=cores).results
    del in2
    wg = np.ascontiguousarray(w_in[:, NQKVZ:])
    wbr = np.concatenate([params["w_branch_a"][l], params["w_branch_b"][l], params["w_branch_c"][l], params["w_branch_d"][l]], axis=0)
    in3 = [{"xnT": np.asarray(r1[c]["xnT_o"]), "yT": np.asarray(r2[c]["yT_o"]), "od": od_natural(r2[c]["od_o"]),
            "szd": np.ascontiguousarray(np.asarray(r1[c]["z_o"])[:, 3072:3584]), "wg": wg, "wbr": wbr} for c in cores]
    r3 = run_bass_kernel_spmd(_prog("l3", build_l3), in3, core_ids=cores).results
    del wg, in3, r1, r2
    wo = np.ascontiguousarray(params["w_out"][l])
    in4 = [{"mT": np.asarray(r3[c]["mT_o"]), "wo": wo, "x": xs[c]} for c in cores]
    r4 = run_bass_kernel_spmd(_prog("l4", build_l4), in4, core_ids=cores).results
    return [np.asarray(r4[c]["out"]) for c in cores]


def kernel_unfused(x, norm_g, w_in, qk_gain, na_rel_bias, diff_lambda, diff_subln_g,
           w_branch_a, w_branch_b, w_branch_c, w_branch_d, w_out):
    params = dict(norm_g=np.asarray(norm_g), w_in=np.asarray(w_in), qk_gain=np.asarray(qk_gain),
                  na_rel_bias=np.asarray(na_rel_bias), diff_lambda=np.asarray(diff_lambda),
                  diff_subln_g=np.asarray(diff_subln_g), w_branch_a=np.asarray(w_branch_a),
                  w_branch_b=np.asarray(w_branch_b), w_branch_c=np.asarray(w_branch_c),
                  w_branch_d=np.asarray(w_branch_d), w_out=np.asarray(w_out))
    x = np.asarray(x, dtype=np.float32)
    cs = rope_table()
    xs = [np.ascontiguousarray(x[c // 4, (c % 4) * 1024:(c % 4 + 1) * 1024]) for c in range(8)]
    for l in range(2):
        xs = run_layer(xs, params, l, cs)
    out = np.empty((2, SEQ, D_MODEL), np.float32)
    for c in range(8):
        out[c // 4, (c % 4) * 1024:(c % 4 + 1) * 1024] = xs[c]
    return out


def build_fused(nl=2, stop=None, dbg_groups=None, mixers="ABCD"):
    nc = bass.Bass("TRN2", target_bir_lowering=False)
    dt = nc.dram_tensor
    I32 = mybir.dt.int32
    I = lambda n, s, d=F32: dt(n, s, d, kind="ExternalInput").ap()
    x = I("x", [SLAB, D_MODEL])
    ng = I("ng", [nl, D_MODEL])
    w_in = I("w_in", [nl, D_MODEL, (N_IN if stop not in ("l1", "cc", "l2") else NQKVZ) if dbg_groups is None else 512 * dbg_groups])
    qkg = I("qkg", [nl, 1024])
    cs = I("cs", [SLAB, 128])
    relb = I("relb", [nl, 3720])
    dlam = I("dlam", [nl, 512])
    subg = I("subg", [nl, 256])
    if stop not in ("l1", "cc", "l2"):
        wba = I("wba", [nl, 1024, D_MODEL])
        wbb = I("wbb", [nl, 1024, D_MODEL])
        wbc = I("wbc", [nl, 1024, D_MODEL])
        wbd = I("wbd", [nl, 512, D_MODEL])
    if stop not in ("l1", "cc", "l2", "l3"):
        wo = I("wo", [nl, D_MODEL, D_MODEL])
    abias = I("abias", [nl, 8, 128, 5, 7, 128])
    cstrip = I("cstrip", [4, 128, 4992])
    dbias = I("dbias", [128, 12, 2, 128])
    dval = I("dval", [128, 64])
    idxK = I("idxK", [128, 90], I32)
    idxVA = I("idxVA", [128, 192], I32)
    idxVD = I("idxVD", [128, 768], I32)
    out = dt("out", [SLAB, D_MODEL], F32, kind="ExternalOutput").ap()
    T = lambda n, s, d: dt(n, s, d).ap()
    xnT_d = T("xnT_d", [32, 128, SLAB], BF16)
    qT_d = T("qT_d", [NQH, 128, SLAB], BF16)
    kT_loc = T("kT_loc", [NKH * 128, SLAB], BF16)
    v_loc = T("v_loc", [SLAB, NVU * 128], BF16)
    G_k = T("G_k", [4 * NKH * 128, SLAB], BF16)
    G_v = T("G_v", [SEQ, NVU * 128], BF16)
    sz_d = T("sz_d", [SLAB, NZU * 128], F32)
    yT_d = T("yT_d", [24, 128, SLAB], BF16)
    od_d = T("od_d", [SLAB, 3, 4, OD_W], F32)
    mT_d = T("mT_d", [32, 128, SLAB], BF16)
    x1_d = T("x1_d", [SLAB, D_MODEL], F32)
    P = Prog(nc)
    P.reg_values = [15359, 4095, SEQ * NVU - 1]
    groups = [[0, 1, 2, 3], [4, 5, 6, 7]]
    for l in range(nl):
        r_G = Res("G%d" % l)
        r_ccK = Res("ccK%d" % l)
        r_ccV = Res("ccV%d" % l)
        xin = x if l == 0 else x1_d
        xout = x1_d if l < nl - 1 else out
        emit_l1(P, xin, ng[l:l + 1, :], w_in[l], qkg[l:l + 1, :], cs, xnT_d, qT_d,
                kT_loc.rearrange("(h p) t -> h p t", p=128), v_loc, sz_d, l, dbg_groups=dbg_groups)
        if stop == "l1":
            break
        for c_ in range(10):
            P.dma(POOL, lambda e, c_=c_: e.collective_compute("AllGather", ALU.bypass, replica_groups=groups,
                                                              ins=[kT_loc[c_ * 384:(c_ + 1) * 384, :].opt()],
                                                              outs=[G_k[c_ * 1536:(c_ + 1) * 1536, :].opt()]), r_ccK, writes=[r_G], inc=1)
        for tb_ in range(8):
            P.dma(POOL, lambda e, tb_=tb_: e.collective_compute("AllGather", ALU.bypass, replica_groups=groups,
                                                                ins=[v_loc[tb_ * 128:(tb_ + 1) * 128, :].opt()],
                                                                outs=[G_v[tb_ * 512:(tb_ + 1) * 512, :].opt()]), r_ccV, writes=[r_G], inc=1)
        if stop == "cc":
            break
        fz = dict(G_k4=G_k.rearrange("(c r h p) t -> c r h p t", c=10, r=4, h=3, p=128), G_kr=G_k, G_v=G_v, r_G=r_G,
                  idxK=idxK, idxVA=idxVA, idxVD=idxVD)
        emit_l2(P, l, qT_d, None, None, None, None, None, None, None, None, sz_d, abias[l], cstrip, dbias, dval,
                qkg[l:l + 1, :], relb[l:l + 1, :], dlam[l:l + 1, :], subg[l:l + 1, :], yT_d, od_d, fz=fz, mixers=mixers)
        if stop == "l2":
            break
        emit_l3(P, l, xnT_d, yT_d, od_d, sz_d[:, 3072:3584], w_in[l][:, NQKVZ:], [wba[l], wbb[l], wbc[l], wbd[l]], mT_d)
        if stop == "l3":
            break
        emit_l4(P, l, mT_d, wo[l], xin, xout)
    P.emit()
    return nc


def build_idx(qt):
    OOB = 0
    p = np.arange(128)
    vrow = lambda tok: (((tok % 1024) // 128) * 4 + tok // 1024) * 128 + tok % 128
    idxK = np.full((128, 90), OOB, np.int32)
    for kidx in range(30):
        for d_ in range(3):
            r = qt + d_ - 1
            if 0 <= r <= 3:
                idxK[:, kidx * 3 + d_] = (((kidx // 3) * 4 + r) * 3 + kidx % 3) * 128 + p
    idxVA = np.full((128, 192), OOB, np.int32)
    for kb in range(24):
        tok = (qt - 1) * 1024 + kb * 128 + p
        for h_ in range(8):
            idxVA[:, kb * 8 + h_] = np.where((tok >= 0) & (tok < SEQ), vrow(tok) * NVU + h_, OOB)
    idxVD = np.full((128, 768), OOB, np.int32)
    col = 0
    for (g, r, ub, qs, q0, blocks) in d_tiles():
        dil = D_PAT[g][1]
        cl3 = 3072 // dil
        for (k0, nk) in blocks:
            k = k0 + p
            tok = (qt - 1) * 1024 + dil * (k % cl3) + (k // cl3)
            ok = (tok >= 0) & (tok < SEQ) & (p < nk)
            for hd in range(12):
                idxVD[:, hd * 64 + col] = np.where(ok, vrow(tok) * NVU + 18 + hd, OOB)
            col += 1
    return idxK, idxVA, idxVD


def kernel(x, norm_g, w_in, qk_gain, na_rel_bias, diff_lambda, diff_subln_g,
           w_branch_a, w_branch_b, w_branch_c, w_branch_d, w_out):
    f32 = lambda a: np.ascontiguousarray(np.asarray(a), dtype=np.float32)
    x = f32(x)
    shared = {
        "ng": f32(norm_g), "w_in": f32(w_in), "qkg": f32(qk_gain).reshape(2, 1024), "relb": f32(na_rel_bias).reshape(2, 3720),
        "dlam": f32(diff_lambda).reshape(2, 512), "subg": f32(diff_subln_g).reshape(2, 256),
        "wba": f32(w_branch_a), "wbb": f32(w_branch_b), "wbc": f32(w_branch_c), "wbd": f32(w_branch_d), "wo": f32(w_out),
        "dbias": build_dbias(),
    }
    cs = rope_table()
    rel = f32(na_rel_bias)
    per_qt = []
    for qt in range(4):
        idxK, idxVA, idxVD = build_idx(qt)
        per_qt.append({"cs": np.ascontiguousarray(cs[qt * 1024:(qt + 1) * 1024]),
                       "abias": np.stack([build_abias(rel[l], qt) for l in range(2)]),
                       "cstrip": build_cstrip(qt), "dval": build_dval(qt), "idxK": idxK, "idxVA": idxVA, "idxVD": idxVD})
    ins = []
    for c in range(8):
        d = {"x": np.ascontiguousarray(x[c // 4, (c % 4) * 1024:(c % 4 + 1) * 1024])}
        d.update(shared)
        d.update(per_qt[c % 4])
        ins.append(d)
    res = run_bass_kernel_spmd(_prog("fused", build_fused), ins, core_ids=list(range(8))).results
    out = np.empty((2, SEQ, D_MODEL), np.float32)
    for c in range(8):
        out[c // 4, (c % 4) * 1024:(c % 4 + 1) * 1024] = np.asarray(res[c]["out"])
    return out


def build_l2f_test(mixers):
    nc = bass.Bass("TRN2", target_bir_lowering=False)
    dt = nc.dram_tensor
    I32 = mybir.dt.int32
    I = lambda n, s, d=F32: dt(n, s, d, kind="ExternalInput").ap()
    qT_d = I("qT", [NQH, 128, SLAB], BF16)
    G_k = I("G_k", [4 * NKH * 128, SLAB], BF16)
    G_v = I("G_v", [SEQ, NVU * 128], BF16)
    sz_d = I("sz", [SLAB, NZU * 128])
    qkg = I("qkg", [1, 1024]); relb = I("relb", [1, 3720]); dlam = I("dlam", [1, 512]); subg = I("subg", [1, 256])
    abias = I("abias", [8, 128, 5, 7, 128]); cstrip = I("cstrip", [4, 128, 4992]); dbias = I("dbias", [128, 12, 2, 128]); dval = I("dval", [128, 64])
    idxK = I("idxK", [128, 90], I32); idxVA = I("idxVA", [128, 192], I32); idxVD = I("idxVD", [128, 768], I32)
    yT_d = dt("yT_o", [24, 128, SLAB], BF16, kind="ExternalOutput").ap()
    od_d = dt("od_o", [SLAB, 3, 4, OD_W], F32, kind="ExternalOutput").ap()
    P = Prog(nc)
    P.reg_values = [15359, 4095, SEQ * NVU - 1]
    fz = dict(G_k4=G_k.rearrange("(c r h p) t -> c r h p t", c=10, r=4, h=3, p=128), G_kr=G_k, G_v=G_v, r_G=Res("G"),
              idxK=idxK, idxVA=idxVA, idxVD=idxVD)
    emit_l2(P, 0, qT_d, None, None, None, None, None, None, None, None, sz_d, abias, cstrip, dbias, dval, qkg, relb, dlam, subg,
            yT_d, od_d, fz=fz, mixers=mixers)
    P.emit()
    return nc


def build_l2f_test2(mixers, with_l34=False):
    nc = bass.Bass("TRN2", target_bir_lowering=False)
    dt = nc.dram_tensor
    I32 = mybir.dt.int32
    I = lambda n, s, d=F32: dt(n, s, d, kind="ExternalInput").ap()
    qT_d = I("qT", [NQH, 128, SLAB], BF16)
    kT_e = I("kT_e", [NKH * 128, SLAB], BF16)
    v_e = I("v_e", [SLAB, NVU * 128], BF16)
    sz_d = I("sz", [SLAB, NZU * 128])
    qkg = I("qkg", [1, 1024]); relb = I("relb", [1, 3720]); dlam = I("dlam", [1, 512]); subg = I("subg", [1, 256])
    abias = I("abias", [8, 128, 5, 7, 128]); cstrip = I("cstrip", [4, 128, 4992]); dbias = I("dbias", [128, 12, 2, 128]); dval = I("dval", [128, 64])
    idxK = I("idxK", [128, 90], I32); idxVA = I("idxVA", [128, 192], I32); idxVD = I("idxVD", [128, 768], I32)
    yT_d = dt("yT_o", [24, 128, SLAB], BF16, kind="ExternalOutput").ap()
    od_d = dt("od_o", [SLAB, 3, 4, OD_W], F32, kind="ExternalOutput").ap()
    kT_loc = dt("kT_loc", [NKH * 128, SLAB], BF16).ap()
    v_loc = dt("v_loc", [SLAB, NVU * 128], BF16).ap()
    G_k = dt("G_k", [4 * NKH * 128, SLAB], BF16).ap()
    G_v = dt("G_v", [SEQ, NVU * 128], BF16).ap()
    P = Prog(nc)
    P.reg_values = [15359, 4095, SEQ * NVU - 1]
    groups = [[0, 1, 2, 3], [4, 5, 6, 7]]
    r_loc = Res("locs")
    r_cp = Res("cp")
    P.push_scope()
    P.dma(SP, lambda e: [e.dma_start(out=kT_loc[i * 384:(i + 1) * 384, :], in_=kT_e[i * 384:(i + 1) * 384, :]) for i in range(10)] +
          [e.dma_start(out=v_loc[i * 128:(i + 1) * 128, :], in_=v_e[i * 128:(i + 1) * 128, :]) for i in range(8)], r_cp, writes=[r_loc], nd=18)
    P.pop_scope()
    r_G = Res("G"); r_ccK = Res("ccK"); r_ccV = Res("ccV")
    for c_ in range(10):
        P.dma(POOL, lambda e, c_=c_: e.collective_compute("AllGather", ALU.bypass, replica_groups=groups, ins=[kT_loc[c_ * 384:(c_ + 1) * 384, :].opt()],
                                                          outs=[G_k[c_ * 1536:(c_ + 1) * 1536, :].opt()]), r_ccK, writes=[r_G], inc=1)
    for tb_ in range(8):
        P.dma(POOL, lambda e, tb_=tb_: e.collective_compute("AllGather", ALU.bypass, replica_groups=groups, ins=[v_loc[tb_ * 128:(tb_ + 1) * 128, :].opt()],
                                                            outs=[G_v[tb_ * 512:(tb_ + 1) * 512, :].opt()]), r_ccV, writes=[r_G], inc=1)
    fz = dict(G_k4=G_k.rearrange("(c r h p) t -> c r h p t", c=10, r=4, h=3, p=128), G_kr=G_k, G_v=G_v, r_G=r_G,
              idxK=idxK, idxVA=idxVA, idxVD=idxVD)
    emit_l2(P, 0, qT_d, None, None, None, None, None, None, None, None, sz_d, abias, cstrip, dbias, dval, qkg, relb, dlam, subg,
            yT_d, od_d, fz=fz, mixers=mixers)
    P.emit()
    return nc
```
